# Optimizing a Trainium2 kernel written in Bass

```python
import math
import jax, jax.numpy as jnp
from jax import lax
import numpy as np

D_MODEL = 1024
BATCH = 16
SEQ = 4096
DEPTH = 2

D_MIX = D_MODEL
W_CONV = D_MIX // 4
W_CFM = D_MIX // 4
W_SSM = D_MIX // 4
W_ATT = D_MIX - W_CONV - W_CFM - W_SSM
SHORT_K = 3
CFM_K = 31
SSM_CH = 16
SSM_GROUPS = W_SSM // SSM_CH
SSM_STATE = 64
HEAD_DIM = 64
N_HEADS = W_ATT // HEAD_DIM
DILATED_CFG = ((128, 1), (512, 4), (2048, 16))
ROPE_THETA = 10000.0
D_FF = -(-8 * D_MODEL // (3 * 256)) * 256
PROJ_SPLITS = (W_CONV, W_CONV, W_CONV, W_CFM, W_CFM, W_SSM, W_ATT, W_ATT, W_ATT)
N_IN = sum(PROJ_SPLITS)
GROUP_SPLITS = (W_CONV, W_CFM, W_SSM, W_ATT)
EPS = 1e-6

kernel_name = 'hybrid_parallel_group_decoder'


def _split_points(sizes):
    pts, acc = [], 0
    for s in sizes[:-1]:
        acc += s
        pts.append(acc)
    return pts


def rmsnorm(x, g):
    xf = x.astype(jnp.float32)
    y = xf * lax.rsqrt(jnp.mean(xf * xf, axis=-1, keepdims=True) + EPS)
    return (y * g.astype(jnp.float32)).astype(x.dtype)


def layernorm(x, g, b):
    xf = x.astype(jnp.float32)
    mu = jnp.mean(xf, axis=-1, keepdims=True)
    xc = xf - mu
    var = jnp.mean(xc * xc, axis=-1, keepdims=True)
    y = xc * lax.rsqrt(var + EPS) * g.astype(jnp.float32) + b.astype(jnp.float32)
    return y.astype(x.dtype)


def causal_depthwise_conv(x, w):
    k, c = w.shape
    return lax.conv_general_dilated(
        x, w[:, None, :].astype(x.dtype), window_strides=(1,),
        padding=((k - 1, 0),), dimension_numbers=('NWC', 'WIO', 'NWC'),
        feature_group_count=c)


def rope(x):
    seq, hd = x.shape[1], x.shape[-1]
    inv = ROPE_THETA ** (-jnp.arange(0, hd, 2, dtype=jnp.float32) / hd)
    ang = jnp.arange(seq, dtype=jnp.float32)[:, None] * inv[None, :]
    cos = jnp.cos(ang)[None, :, None, :]
    sin = jnp.sin(ang)[None, :, None, :]
    xf = x.astype(jnp.float32)
    x1, x2 = xf[..., :hd // 2], xf[..., hd // 2:]
    return jnp.concatenate([x1 * cos - x2 * sin, x2 * cos + x1 * sin], axis=-1).astype(x.dtype)


def short_conv_mixer(h, gate_b, gate_c, w):
    return gate_b * causal_depthwise_conv(gate_c * h, w)


def conformer_conv_mixer(a, g, dw_w, dw_b, ln_g, ln_b):
    z = a * jax.nn.sigmoid(g)
    z = causal_depthwise_conv(z, dw_w) + dw_b.astype(z.dtype)
    z = layernorm(z, ln_g, ln_b)
    return jax.nn.silu(z)


def s5_mixer(u, a_re, a_im, log_dt, b_re, b_im, c_re, c_im, d_skip, w_glu):
    bsz, seq, _ = u.shape
    f32 = jnp.float32
    uf = u.astype(f32).reshape(bsz, seq, SSM_GROUPS, SSM_CH)
    dt = jnp.exp(log_dt.astype(f32))[:, None]
    ar, ai = a_re.astype(f32), a_im.astype(f32)
    mag = jnp.exp(ar * dt)
    lr, li = mag * jnp.cos(ai * dt), mag * jnp.sin(ai * dt)
    den = ar * ar + ai * ai
    nr, ni = lr - 1.0, li
    fr = (nr * ar + ni * ai) / den
    fi = (ni * ar - nr * ai) / den
    br, bi = b_re.astype(f32), b_im.astype(f32)
    bbar_re = fr[..., None] * br - fi[..., None] * bi
    bbar_im = fr[..., None] * bi + fi[..., None] * br
    xr = jnp.einsum('blgh,gph->blgp', uf, bbar_re)
    xi = jnp.einsum('blgh,gph->blgp', uf, bbar_im)
    a_full_r = jnp.broadcast_to(lr, xr.shape)
    a_full_i = jnp.broadcast_to(li, xr.shape)

    def combine(e1, e2):
        a1r, a1i, b1r, b1i = e1
        a2r, a2i, b2r, b2i = e2
        return (a2r * a1r - a2i * a1i, a2r * a1i + a2i * a1r,
                a2r * b1r - a2i * b1i + b2r, a2r * b1i + a2i * b1r + b2i)

    _, _, sr, si = lax.associative_scan(combine, (a_full_r, a_full_i, xr, xi), axis=1)
    y = (jnp.einsum('blgp,ghp->blgh', sr, c_re.astype(f32))
         - jnp.einsum('blgp,ghp->blgh', si, c_im.astype(f32)))
    y = y.reshape(bsz, seq, W_SSM) + d_skip.astype(f32) * uf.reshape(bsz, seq, W_SSM)
    z = jax.nn.gelu(y)
    gl = z @ w_glu.astype(f32)
    out = gl[..., :W_SSM] * jax.nn.sigmoid(gl[..., W_SSM:])
    return out.astype(u.dtype)


def dilated_window_attention(q, k, v, window, dilation):
    bsz, seq, nh, hd = q.shape
    n_keys = window // dilation
    blk = n_keys
    n = seq // dilation
    nb = -(-n // blk)
    pad = nb * blk - n

    def regroup(t):
        t = t.reshape(bsz, n, dilation, nh, hd).transpose(0, 2, 3, 1, 4)
        t = jnp.pad(t, ((0, 0), (0, 0), (0, 0), (0, pad), (0, 0)))
        return t.reshape(bsz, dilation, nh, nb, blk, hd)

    qb, kb, vb = regroup(q), regroup(k), regroup(v)

    def prev(t):
        return jnp.pad(t, ((0, 0), (0, 0), (0, 0), (1, 0), (0, 0), (0, 0)))[:, :, :, :-1]

    kk = jnp.concatenate([prev(kb), kb], axis=4)
    vv = jnp.concatenate([prev(vb), vb], axis=4)
    s = jnp.einsum('bdhnqe,bdhnke->bdhnqk', qb, kk,
                   preferred_element_type=jnp.float32) * (hd ** -0.5)
    bi = jnp.arange(nb)[:, None, None]
    qi = jnp.arange(blk)[None, :, None]
    kj = jnp.arange(2 * blk)[None, None, :]
    rel = blk + qi - kj
    valid = (rel >= 0) & (rel <= n_keys) & ((bi > 0) | (kj >= blk))
    s = jnp.where(valid, s, -jnp.inf)
    m = jnp.max(s, axis=-1, keepdims=True)
    p = jnp.exp(s - m)
    den = jnp.sum(p, axis=-1, keepdims=True)
    o = jnp.einsum('bdhnqk,bdhnke->bdhnqe', p, vv.astype(jnp.float32)) / den
    lse = (m + jnp.log(den))[..., 0]
    o = o.reshape(bsz, dilation, nh, nb * blk, hd)[:, :, :, :n]
    o = o.transpose(0, 3, 1, 2, 4).reshape(bsz, seq, nh, hd)
    lse = lse.reshape(bsz, dilation, nh, nb * blk)[:, :, :, :n]
    lse = lse.transpose(0, 3, 1, 2).reshape(bsz, seq, nh)
    return o, lse


def dilated_attention_mixer(q, k, v):
    bsz, seq, _ = q.shape
    q = rope(q.reshape(bsz, seq, N_HEADS, HEAD_DIM))
    k = rope(k.reshape(bsz, seq, N_HEADS, HEAD_DIM))
    v = v.reshape(bsz, seq, N_HEADS, HEAD_DIM)
    outs, lses = [], []
    for window, dilation in DILATED_CFG:
        o, lse = dilated_window_attention(q, k, v, window, dilation)
        outs.append(o)
        lses.append(lse)
    wts = jax.nn.softmax(jnp.stack(lses, axis=0), axis=0)
    o = jnp.sum(wts[..., None] * jnp.stack(outs, axis=0), axis=0)
    return o.reshape(bsz, seq, W_ATT).astype(q.dtype)


def setup_inputs(seed: int = 0) -> dict:
    key = jax.random.key(seed)
    ks = iter(jax.random.split(key, 32))
    f32 = jnp.float32

    def nrm(shape, scale):
        return jax.random.normal(next(ks), shape, f32) * scale

    x = nrm((BATCH, SEQ, D_MODEL), 1.0)
    norm_mix_g = 1.0 + nrm((DEPTH, D_MODEL), 0.02)
    w_in = nrm((DEPTH, D_MODEL, N_IN), D_MODEL ** -0.5)
    conv3_w = nrm((DEPTH, SHORT_K, W_CONV), SHORT_K ** -0.5)
    cfm_dw_w = nrm((DEPTH, CFM_K, W_CFM), CFM_K ** -0.5)
    cfm_dw_b = nrm((DEPTH, W_CFM), 0.02)
    cfm_ln_g = 1.0 + nrm((DEPTH, W_CFM), 0.02)
    cfm_ln_b = nrm((DEPTH, W_CFM), 0.02)
    n_idx = jnp.arange(SSM_STATE, dtype=f32)
    s5_a_re = -0.5 + nrm((DEPTH, SSM_GROUPS, SSM_STATE), 0.01)
    s5_a_im = math.pi * n_idx + nrm((DEPTH, SSM_GROUPS, SSM_STATE), 0.01)
    s5_log_dt = jax.random.uniform(next(ks), (DEPTH, SSM_GROUPS), f32,
                                   math.log(1e-3), math.log(1e-1))
    s5_b_re = nrm((DEPTH, SSM_GROUPS, SSM_STATE, SSM_CH), (2 * SSM_CH) ** -0.5)
    s5_b_im = nrm((DEPTH, SSM_GROUPS, SSM_STATE, SSM_CH), (2 * SSM_CH) ** -0.5)
    s5_c_re = nrm((DEPTH, SSM_GROUPS, SSM_CH, SSM_STATE), SSM_STATE ** -0.5)
    s5_c_im = nrm((DEPTH, SSM_GROUPS, SSM_CH, SSM_STATE), SSM_STATE ** -0.5)
    s5_d = nrm((DEPTH, W_SSM), 0.1)
    s5_glu_w = nrm((DEPTH, W_SSM, 2 * W_SSM), W_SSM ** -0.5)
    grp_norm_g = 1.0 + nrm((DEPTH, D_MIX), 0.02)
    w_out = nrm((DEPTH, D_MIX, D_MODEL), D_MIX ** -0.5)
    norm_ffn_g = 1.0 + nrm((DEPTH, D_MODEL), 0.02)
    w_gate = nrm((DEPTH, D_MODEL, D_FF), D_MODEL ** -0.5)
    w_up = nrm((DEPTH, D_MODEL, D_FF), D_MODEL ** -0.5)
    w_down = nrm((DEPTH, D_FF, D_MODEL), D_FF ** -0.5)
    final_norm_g = 1.0 + nrm((D_MODEL,), 0.02)
    return {'x': x, 'norm_mix_g': norm_mix_g, 'w_in': w_in, 'conv3_w': conv3_w,
            'cfm_dw_w': cfm_dw_w, 'cfm_dw_b': cfm_dw_b, 'cfm_ln_g': cfm_ln_g,
            'cfm_ln_b': cfm_ln_b, 's5_a_re': s5_a_re, 's5_a_im': s5_a_im,
            's5_log_dt': s5_log_dt, 's5_b_re': s5_b_re, 's5_b_im': s5_b_im,
            's5_c_re': s5_c_re, 's5_c_im': s5_c_im, 's5_d': s5_d, 's5_glu_w': s5_glu_w,
            'grp_norm_g': grp_norm_g, 'w_out': w_out, 'norm_ffn_g': norm_ffn_g,
            'w_gate': w_gate, 'w_up': w_up, 'w_down': w_down, 'final_norm_g': final_norm_g}


def reference(x, norm_mix_g, w_in, conv3_w, cfm_dw_w, cfm_dw_b, cfm_ln_g, cfm_ln_b,
              s5_a_re, s5_a_im, s5_log_dt, s5_b_re, s5_b_im, s5_c_re, s5_c_im, s5_d,
              s5_glu_w, grp_norm_g, w_out, norm_ffn_g, w_gate, w_up, w_down,
              final_norm_g):
    proj_pts = _split_points(PROJ_SPLITS)
    grp_pts = _split_points(GROUP_SPLITS)
    for layer in range(DEPTH):
        h = rmsnorm(x, norm_mix_g[layer])
        p = h @ w_in[layer]
        (c_h, c_b, c_c, f_a, f_g, s_u, a_q, a_k, a_v) = jnp.split(p, proj_pts, axis=-1)
        y_conv = short_conv_mixer(c_h, c_b, c_c, conv3_w[layer])
        y_cfm = conformer_conv_mixer(f_a, f_g, cfm_dw_w[layer], cfm_dw_b[layer],
                                     cfm_ln_g[layer], cfm_ln_b[layer])
        y_ssm = s5_mixer(s_u, s5_a_re[layer], s5_a_im[layer], s5_log_dt[layer],
                         s5_b_re[layer], s5_b_im[layer], s5_c_re[layer], s5_c_im[layer],
                         s5_d[layer], s5_glu_w[layer])
        y_att = dilated_attention_mixer(a_q, a_k, a_v)
        g_conv, g_cfm, g_ssm, g_att = jnp.split(grp_norm_g[layer], grp_pts)
        mixed = jnp.concatenate([rmsnorm(y_conv, g_conv), rmsnorm(y_cfm, g_cfm),
                                 rmsnorm(y_ssm, g_ssm), rmsnorm(y_att, g_att)], axis=-1)
        x = x + (mixed @ w_out[layer]).astype(x.dtype)
        h = rmsnorm(x, norm_ffn_g[layer])
        ff = jax.nn.silu(h @ w_gate[layer]) * (h @ w_up[layer])
        x = x + (ff @ w_down[layer]).astype(x.dtype)
    return rmsnorm(x, final_norm_g)
```

```python
import math
import numpy as np
import concourse.bass as bass
import concourse.mybir as mybir
from concourse.bass_utils import run_bass_kernel_spmd
from contextlib import ExitStack

F32 = mybir.dt.float32
BF16 = mybir.dt.bfloat16
I32 = mybir.dt.int32
ALU = mybir.AluOpType
AF = mybir.ActivationFunctionType

N_CORES = 8
D = 1024
SEQ = 4096
NB = 512
DFF = 2816
NIN = 2816
EPS = 1e-6
TWO_PI = 2.0 * math.pi
N_DMA_SEMS = 24
USE_MF = True
PATTERNS = ((1, 32), (4, 8), (16, 2))


class Sched:
    ENGS = ("pe", "act", "dve", "pool", "sp")

    def __init__(self, nc, es):
        self.nc = nc
        self.ops = []
        self.last_writer = {}
        self.readers = {}
        self.n_dma = {"sp": 0, "pool": 0, "act": 0}
        self.cnt = {e: 0 for e in self.ENGS}
        self.esem = {e: es.enter_context(nc.semaphore("s_" + e)) for e in ("pe", "act", "dve", "pool")}
        self.dsem = {"sp": [es.enter_context(nc.semaphore("d%d" % j)) for j in range(N_DMA_SEMS)],
                     "pool": [es.enter_context(nc.semaphore("dp%d" % j)) for j in range(8)],
                     "act": [es.enter_context(nc.semaphore("da%d" % j)) for j in range(8)]}
        self.seen = {e: {} for e in self.ENGS}

    def add(self, eng, fn, reads=(), writes=(), dma=False):
        i = len(self.ops)
        deps = set()
        for r in reads:
            w = self.last_writer.get(r)
            if w is not None:
                deps.add(w)
        for w_ in writes:
            w = self.last_writer.get(w_)
            if w is not None:
                deps.add(w)
            for rd in self.readers.get(w_, ()):
                deps.add(rd)
        deps.discard(i)
        op = dict(eng=eng, fn=fn, deps=deps, dma=dma, has_dep=False)
        if dma:
            op["dma_idx"] = self.n_dma[eng]
            self.n_dma[eng] += 1
        self.ops.append(op)
        for r in reads:
            self.readers.setdefault(r, []).append(i)
        for w_ in writes:
            self.last_writer[w_] = i
            self.readers[w_] = []
        return i

    def emit(self):
        nc = self.nc
        ops = self.ops
        for op in ops:
            if op["eng"] == "pe" and not op["dma"]:
                op["deps"] = {d for d in op["deps"] if not (ops[d]["eng"] == "pe" and not ops[d]["dma"])}
            for d in op["deps"]:
                ops[d]["has_dep"] = True
        cnt = self.cnt
        for op in ops:
            if not op["dma"] and op["has_dep"]:
                cnt[op["eng"]] += 1
                op["ms"] = cnt[op["eng"]]
        esem, dsem = self.esem, self.dsem
        per_eng = {e: [op for op in ops if op["eng"] == e] for e in self.ENGS}
        final_dma = {}

        def dsem_of(p):
            pool_ = dsem[p["eng"]]
            K = len(pool_)
            j = p["dma_idx"]
            return (p["eng"], j % K), pool_[j % K], 16 * (j // K + 1), K

        for op in ops:
            if op["dma"]:
                key, sem, val, K = dsem_of(op)
                final_dma[key] = (sem, val)

        def run(engname, eng):
            seen = self.seen[engname]

            def wait(sem, key, val):
                if seen.get(key, 0) >= val:
                    return
                seen[key] = val
                eng.wait_ge(sem, val)

            for op in per_eng[engname]:
                need = {}
                for d in op["deps"]:
                    p = ops[d]
                    if p["dma"]:
                        key, sem, val, K = dsem_of(p)
                    else:
                        key, sem, val = p["eng"], esem[p["eng"]], p["ms"]
                    if key not in need or need[key][1] < val:
                        need[key] = (sem, val)
                for key in sorted(need, key=str):
                    wait(need[key][0], key, need[key][1])
                if op["dma"]:
                    key, sem, val, K = dsem_of(op)
                    if val > 16:
                        wait(sem, key, val - 16)
                    ins = op["fn"](eng)
                    ins.then_inc(sem, 16)
                else:
                    ins = op["fn"](eng)
                    if op["has_dep"]:
                        ins.then_inc(esem[engname], 1)
            if engname == "sp":
                for key in sorted(final_dma, key=str):
                    wait(final_dma[key][0], key, final_dma[key][1])

        with nc.Block() as block:
            @block.sync
            def _(e):
                run("sp", e)

            @block.tensor
            def _(e):
                run("pe", e)

            @block.scalar
            def _(e):
                run("act", e)

            @block.vector
            def _(e):
                run("dve", e)

            @block.gpsimd
            def _(e):
                run("pool", e)
        self.ops = []
        self.last_writer = {}
        self.readers = {}
        nc.all_engine_barrier()


def build_program(n_seq=2, layers=(0, 1), phases=("m1", "att", "s5", "m3a", "m3b"), debug=False):
    NT = n_seq * SEQ
    NBLK = NT // NB
    nc = bass.Bass("TRN2", target_bir_lowering=False)

    def din(name, shape):
        return nc.dram_tensor(name, shape, F32, kind="ExternalInput").ap()

    x_in = din("x", [NT, D])
    norm_mix_g = din("norm_mix_g", [2, D])
    w_in = din("w_in", [2, D, NIN])
    conv3_w = din("conv3_w", [2, 3, 256])
    cfm_dw_w = din("cfm_dw_w", [2, 31, 256])
    cfm_dw_b = din("cfm_dw_b", [2, 256])
    cfm_ln_g = din("cfm_ln_g", [2, 256])
    cfm_ln_b = din("cfm_ln_b", [2, 256])
    s5_a_re = din("s5_a_re", [2, 16, 64])
    s5_a_im = din("s5_a_im", [2, 16, 64])
    s5_log_dt = din("s5_log_dt", [2, 16])
    s5_b_re = din("s5_b_re", [2, 16, 64, 16])
    s5_b_im = din("s5_b_im", [2, 16, 64, 16])
    s5_c_re = din("s5_c_re", [2, 16, 16, 64])
    s5_c_im = din("s5_c_im", [2, 16, 16, 64])
    s5_d = din("s5_d", [2, 256])
    s5_glu_w = din("s5_glu_w", [2, 256, 512])
    grp_norm_g = din("grp_norm_g", [2, D])
    w_out = din("w_out", [2, D, D])
    norm_ffn_g = din("norm_ffn_g", [2, D])
    w_gate = din("w_gate", [2, D, DFF])
    w_up = din("w_up", [2, D, DFF])
    w_down = din("w_down", [2, DFF, D])
    final_norm_g = din("final_norm_g", [D])
    out = nc.dram_tensor("out", [NT, D], F32, kind="ExternalOutput").ap()

    skind = "ExternalOutput" if debug else "Internal"
    xs1 = nc.dram_tensor("xs1", [NT, D], F32, kind=skind).ap()
    xmid = nc.dram_tensor("xmid", [NT, D], F32, kind=skind).ap()
    ymix = nc.dram_tensor("ymix", [8, 128, NT], F32, kind=skind).ap()
    ubd = nc.dram_tensor("ubd", [2, 128, NT], BF16, kind=skind).ap()
    qTd = nc.dram_tensor("qTd", [2, 128, NT], BF16, kind=skind).ap()
    kTd = nc.dram_tensor("kTd", [2, 128, NT], BF16, kind=skind).ap()
    vTd = nc.dram_tensor("vTd", [2, 128, NT], BF16, kind=skind).ap()
    h2Td = nc.dram_tensor("h2Td", [8, 128, NT], BF16, kind=skind).ap()

    with ExitStack() as es0:
        es0.enter_context(nc.allow_non_contiguous_dma(reason="small parameter layouts"))
        S = Sched(nc, es0)
        A = S.add

        uniq = [0]

        def mk(es):
            def sb(name, shape, dt):
                uniq[0] += 1
                return es.enter_context(nc.sbuf_tensor("%s_%d" % (name, uniq[0]), shape, dt))

            def ps(name, shape, dt):
                uniq[0] += 1
                return es.enter_context(nc.psum_tensor("%s_%d" % (name, uniq[0]), shape, dt))
            return sb, ps

        def wkeys(dst_key, kt_n, col):
            return [(dst_key, col // 2048, k0) for k0 in range(0, kt_n, 4)]

        def load_w_bf16(dst, dst_key, src_ap, kt_n, ncols):
            v = src_ap.rearrange("(kt p) n -> p kt n", p=128)
            step = 2048
            for c0 in range(0, ncols, step):
                c1 = min(ncols, c0 + step)
                for k0 in range(0, kt_n, 4):
                    k1 = min(kt_n, k0 + 4)
                    A("pool", lambda e, c0=c0, c1=c1, k0=k0, k1=k1: e.dma_start(out=dst[:, k0:k1, c0:c1], in_=v[:, k0:k1, c0:c1]),
                      writes=[(dst_key, c0 // step, k0)], dma=True)

        def make_ident(sb, pfx):
            idi = sb(pfx + "idi", [128, 128], I32)
            identb = sb(pfx + "identb", [128, 128], BF16)
            identf = sb(pfx + "identf", [128, 128], F32)
            A("pool", lambda e: e.iota(idi[:], pattern=[[1, 128]], base=0, channel_multiplier=-1), writes=["idi"])
            A("dve", lambda e: e.tensor_scalar(out=identb[:], in0=idi[:], scalar1=0.0, scalar2=None, op0=ALU.is_equal),
              reads=["idi"], writes=["identb"])
            A("dve", lambda e: e.tensor_scalar(out=identf[:], in0=idi[:], scalar1=0.0, scalar2=None, op0=ALU.is_equal),
              reads=["idi"], writes=["identf"])
            return idi, identb, identf

        def range_reduce(ang, tmpf, tmpi, key, tkey):
            A("dve", lambda e: e.tensor_scalar(out=tmpf, in0=ang, scalar1=1.0 / TWO_PI, scalar2=None, op0=ALU.mult),
              reads=[key], writes=[tkey])
            A("dve", lambda e: e.tensor_copy(out=tmpi, in_=tmpf), reads=[tkey], writes=[tkey + "i"])
            A("dve", lambda e: e.tensor_copy(out=tmpf, in_=tmpi), reads=[tkey + "i"], writes=[tkey])
            A("dve", lambda e: e.scalar_tensor_tensor(out=ang, in0=tmpf, scalar=-TWO_PI, in1=ang, op0=ALU.mult, op1=ALU.add),
              reads=[tkey, key], writes=[key])
            A("dve", lambda e: e.tensor_scalar(out=ang, in0=ang, scalar1=math.pi, scalar2=-math.pi, op0=ALU.min, op1=ALU.max),
              reads=[key], writes=[key])

        def phase_m1(l, pre_w=None):
            xsrc = x_in if l == 0 else xs1
            with ExitStack() as es:
                sb, ps = mk(es)
                idi, identb, identf = make_ident(sb, "m1")
                wsb = pre_w if pre_w is not None else sb("m1w", [128, 8, NIN], BF16)
                gb = sb("m1gb", [128, D], F32)
                A("act", lambda e: e.dma_start(out=gb[:], in_=norm_mix_g[l].partition_broadcast(128)), writes=["gb"], dma=True)
                w3 = sb("m1w3", [128, 2, 3], F32)
                wdw = sb("m1wdw", [128, 2, 31], F32)
                dwb = sb("m1dwb", [128, 2], F32)
                lng = sb("m1lng", [128, 2], F32)
                lnb = sb("m1lnb", [128, 2], F32)

                def load_params():
                    for k in range(31):
                        A("act", lambda e, k=k: e.dma_start(out=wdw[:, :, k], in_=cfm_dw_w[l, k].rearrange("(t p) -> p t", p=128)), writes=["wdw"], dma=True)
                    for k in range(3):
                        A("act", lambda e, k=k: e.dma_start(out=w3[:, :, k], in_=conv3_w[l, k].rearrange("(t p) -> p t", p=128)), writes=["w3"], dma=True)
                    for dst, src, key in ((dwb, cfm_dw_b, "dwb"), (lng, cfm_ln_g, "lng"), (lnb, cfm_ln_b, "lnb")):
                        A("act", lambda e, dst=dst, src=src: e.dma_start(out=dst[:], in_=src[l].rearrange("(t p) -> p t", p=128)),
                          writes=[key], dma=True)
                diag = sb("m1diag", [128, 2, 31, 128], BF16)
                ones256 = sb("m1ones", [128, 128], BF16)
                COS = sb("m1cos", [128, SEQ], F32)
                SIN = sb("m1sin", [128, SEQ], F32)
                tf = sb("m1tf", [128, NB], F32)
                ti = sb("m1ti", [128, NB], I32)
                pidx = sb("m1pidx", [128, 1], I32)
                pj = sb("m1pj", [128, 1], I32)
                pf = sb("m1pf", [128, 1], F32)
                inv = sb("m1inv", [128, 1], F32)
                sgn = sb("m1sgn", [128, 1], F32)
                xt = [sb("m1xt%d" % i, [128, D], F32) for i in range(2)]
                sqj = sb("m1sqj", [128, D], BF16)
                ss = sb("m1ss", [128, 4], F32)
                rstd = sb("m1rstd", [128, 4], F32)
                hb = [sb("m1h%d" % i, [128, D], BF16) for i in range(8)]
                hT = [sb("m1hT%d" % i, [128, 8, NB], BF16) for i in range(2)]
                ch_sb = sb("m1ch", [128, 2, NB], F32)
                zbuf = sb("m1z", [128, 2, NB + 2], F32)
                acc = sb("m1acc", [128, 2, NB], F32)
                ycv = sb("m1ycv", [128, 2, NB], F32)
                sg_sb = sb("m1sg", [128, 2, NB], F32)
                zcb = sb("m1zcb", [128, 2, NB + 30], BF16)
                cf = sb("m1cf", [128, 2, NB], F32)
                cfb = sb("m1cfb", [128, 2, 2, NB], BF16)
                mean_sb = sb("m1mean", [128, NB], F32)
                var_sb = sb("m1var", [128, NB], F32)
                rs_sb = sb("m1rs", [128, NB], F32)
                tmpc = sb("m1tmpc", [128, 2, NB], F32)
                ycf = sb("m1ycf", [128, 2, NB], F32)
                ub_sb = sb("m1ub", [128, 2, NB], BF16)
                v_sb = sb("m1v", [128, 2, NB], BF16)
                t1 = sb("m1t1", [128, 4, NB], F32)
                t2 = sb("m1t2", [128, 2, NB], F32)
                qk_sb = sb("m1qk", [128, 4, NB], BF16)
                pt = [ps("m1pt%d" % i, [128, 8, 128], BF16) for i in range(2)]
                pin = [ps("m1pin%d" % i, [128, NB], F32) for i in range(4)]
                pc = [ps("m1pc%d" % i, [128, NB], F32) for i in range(2)]

                pin_ctr = [0]

                def inproj(m, b):
                    i = pin_ctr[0] % 4
                    pin_ctr[0] += 1
                    for kt in range(8):
                        A("pe", lambda e, kt=kt, i=i: e.matmul(pin[i][:], lhsT=wsb[:, kt, m * 128:(m + 1) * 128], rhs=hT[b % 2][:, kt, :],
                                                                start=(kt == 0), stop=(kt == 7)),
                          reads=[("w", min(2, (m * 128) // 1024), kt_) for kt_ in range(8)] + [("hT", b % 2)], writes=[("pin", i)])
                    return i

                def stage_N(b):
                    tok0 = b * NB
                    for tt in range(4):
                        xb = xt[tt % 2]
                        xk = ("xt", tt % 2)
                        r0 = tok0 + tt * 128
                        A("sp", lambda e, xb=xb, r0=r0: e.dma_start(out=xb[:], in_=xsrc[r0:r0 + 128, :]), writes=[xk], dma=True)
                        A("act", lambda e, xb=xb, tt=tt: e.activation(out=sqj[:], in_=xb[:], func=AF.Square, accum_out=ss[:, tt:tt + 1]),
                          reads=[xk], writes=["sqj", "ss"])
                        A("act", lambda e, tt=tt: e.activation(out=rstd[:, tt:tt + 1], in_=ss[:, tt:tt + 1], func=AF.Ln, scale=1.0 / D, bias=EPS),
                          reads=["ss"], writes=["rstd"])
                        A("act", lambda e, tt=tt: e.activation(out=rstd[:, tt:tt + 1], in_=rstd[:, tt:tt + 1], func=AF.Exp, scale=-0.5), reads=["rstd"], writes=["rstd"])
                        A("dve", lambda e, xb=xb, tt=tt: e.scalar_tensor_tensor(out=hb[(b % 2) * 4 + tt][:], in0=xb[:], scalar=rstd[:, tt:tt + 1], in1=gb[:],
                                                                                  op0=ALU.mult, op1=ALU.mult),
                          reads=[xk, "rstd", "gb"], writes=[("hb", b % 2, tt)])

                def stage_T(b):
                    for tt in range(4):
                        for kt in range(8):
                            A("pe", lambda e, kt=kt, tt=tt: e.transpose(out=pt[tt % 2][:, kt, :], in_=hb[(b % 2) * 4 + tt][:, kt * 128:(kt + 1) * 128],
                                                                          identity=identb[:]),
                              reads=[("hb", b % 2, tt), "identb"], writes=[("pt", tt % 2)])
                        A("act", lambda e, tt=tt, b=b: e.copy(out=hT[b % 2][:, :, tt * 128:(tt + 1) * 128], in_=pt[tt % 2][:]),
                          reads=[("pt", tt % 2)], writes=[("hT", b % 2)])

                def stage_B1(b):
                    tok0 = b * NB
                    bs = b % (SEQ // NB)
                    if bs == 0:
                        A("pool", lambda e: e.memset(zbuf[:, :, 0:2], 0.0), writes=["zbuf"])
                        A("pool", lambda e: e.memset(zcb[:, :, 0:30], 0.0), writes=[("zcb", 0), ("zcb", 1)])
                    for t in range(2):
                        i = inproj(0 + t, b)
                        A("act", lambda e, i=i, t=t: e.copy(out=ch_sb[:, t, :], in_=pin[i][:]), reads=[("pin", i)], writes=["ch"])
                    for t in range(2):
                        i = inproj(4 + t, b)
                        A("dve", lambda e, i=i, t=t: e.tensor_tensor(out=zbuf[:, t, 2:NB + 2], in0=pin[i][:], in1=ch_sb[:, t, :], op=ALU.mult),
                          reads=[("pin", i), "ch"], writes=["zbuf"])
                        A("dve", lambda e, t=t: e.tensor_scalar(out=acc[:, t, :], in0=zbuf[:, t, 0:NB], scalar1=w3[:, t, 0:1], scalar2=None, op0=ALU.mult),
                          reads=["zbuf", "w3"], writes=["acc"])
                        for k in (1, 2):
                            A("dve", lambda e, t=t, k=k: e.scalar_tensor_tensor(out=acc[:, t, :], in0=zbuf[:, t, k:NB + k], scalar=w3[:, t, k:k + 1],
                                                                                 in1=acc[:, t, :], op0=ALU.mult, op1=ALU.add),
                              reads=["zbuf", "w3", "acc"], writes=["acc"])
                        A("pool", lambda e, t=t: e.tensor_copy(out=zbuf[:, t, 0:2], in_=zbuf[:, t, NB:NB + 2]), reads=["zbuf"], writes=["zbuf"])
                    for t in range(2):
                        i = inproj(2 + t, b)
                        A("dve", lambda e, i=i, t=t: e.tensor_tensor(out=ycv[:, t, :], in0=pin[i][:], in1=acc[:, t, :], op=ALU.mult),
                          reads=[("pin", i), "acc"], writes=["ycv"])
                    A("pool", lambda e, tok0=tok0: e.dma_start(out=ymix[0:2, :, tok0:tok0 + NB].rearrange("t p n -> p t n"), in_=ycv[:]),
                      reads=["ycv"], writes=["ymix"], dma=True)

                def cfm_a(b):
                    for t in range(2):
                        i = inproj(8 + t, b)
                        A("act", lambda e, i=i, t=t: e.activation(out=sg_sb[:, t, :], in_=pin[i][:], func=AF.Sigmoid),
                          reads=[("pin", i)], writes=[("sg", t)])
                    for t in range(2):
                        i = inproj(6 + t, b)
                        A("dve", lambda e, i=i, t=t: e.tensor_tensor(out=zcb[:, t, 30:NB + 30], in0=pin[i][:], in1=sg_sb[:, t, :], op=ALU.mult),
                          reads=[("pin", i), ("sg", t)], writes=[("zcb", t)])

                def cfm_conv(b):
                    for t in range(2):
                        for k in range(31):
                            A("pe", lambda e, t=t, k=k: e.matmul(pc[t][:], lhsT=diag[:, t, k, :], rhs=zcb[:, t, k:k + NB],
                                                                  start=(k == 0), stop=(k == 30)),
                              reads=["diag", ("zcb", t)], writes=[("pc", t)])
                        A("pool", lambda e, t=t: e.tensor_copy(out=zcb[:, t, 0:30], in_=zcb[:, t, NB:NB + 30]), reads=[("zcb", t)], writes=[("zcb", t)])
                        A("act", lambda e, t=t: e.activation(out=cf[:, t, :], in_=pc[t][:], func=AF.Identity, bias=dwb[:, t:t + 1]),
                          reads=[("pc", t), "dwb"], writes=[("cf", t)])
                        A("act", lambda e, t=t: e.copy(out=cfb[:, 0, t, :], in_=cf[:, t, :]), reads=[("cf", t)], writes=[("cfb", 0, t)])
                        A("act", lambda e, t=t: e.activation(out=cfb[:, 1, t, :], in_=cf[:, t, :], func=AF.Square), reads=[("cf", t)], writes=[("cfb", 1, t)])

                def cfm_stats(b):
                    tok0 = b * NB
                    for j in range(2):
                        for t in range(2):
                            A("pe", lambda e, j=j, t=t: e.matmul(pc[j][:], lhsT=ones256[:], rhs=cfb[:, j, t, :], start=(t == 0), stop=(t == 1)),
                              reads=["ones256", ("cfb", j, t)], writes=[("pc", j)])
                    A("act", lambda e: e.copy(out=mean_sb[:], in_=pc[0][:]), reads=[("pc", 0)], writes=["mean"])
                    A("dve", lambda e: e.tensor_tensor(out=var_sb[:], in0=mean_sb[:], in1=mean_sb[:], op=ALU.mult), reads=["mean"], writes=["var"])
                    A("dve", lambda e: e.tensor_tensor(out=var_sb[:], in0=pc[1][:], in1=var_sb[:], op=ALU.subtract),
                      reads=[("pc", 1), "var"], writes=["var"])
                    A("act", lambda e: e.activation(out=rs_sb[:], in_=var_sb[:], func=AF.Ln, bias=EPS), reads=["var"], writes=["rs"])
                    A("act", lambda e: e.activation(out=rs_sb[:], in_=rs_sb[:], func=AF.Exp, scale=-0.5), reads=["rs"], writes=["rs"])
                    for t in range(2):
                        A("dve", lambda e, t=t: e.tensor_tensor(out=tmpc[:, t, :], in0=cf[:, t, :], in1=mean_sb[:], op=ALU.subtract),
                          reads=[("cf", t), "mean"], writes=[("tmpc", t)])
                        A("dve", lambda e, t=t: e.tensor_tensor(out=tmpc[:, t, :], in0=tmpc[:, t, :], in1=rs_sb[:], op=ALU.mult), reads=[("tmpc", t), "rs"], writes=[("tmpc", t)])
                        A("act", lambda e, t=t: e.activation(out=ycf[:, t, :], in_=tmpc[:, t, :], func=AF.Silu, scale=lng[:, t:t + 1], bias=lnb[:, t:t + 1]),
                          reads=[("tmpc", t), "lng", "lnb"], writes=["ycf"])
                    A("pool", lambda e, tok0=tok0: e.dma_start(out=ymix[2:4, :, tok0:tok0 + NB].rearrange("t p n -> p t n"), in_=ycf[:]),
                      reads=["ycf"], writes=["ymix"], dma=True)

                def ssm_u(b):
                    tok0 = b * NB
                    for t in range(2):
                        i = inproj(10 + t, b)
                        A("act", lambda e, i=i, t=t: e.copy(out=ub_sb[:, t, :], in_=pin[i][:]), reads=[("pin", i)], writes=["ub"])
                    A("pool", lambda e, tok0=tok0: e.dma_start(out=ubd[:, :, tok0:tok0 + NB].rearrange("t p n -> p t n"), in_=ub_sb[:]),
                      reads=["ub"], writes=["ubd"], dma=True)

                def rope(b, t4s, dstd):
                    tok0 = b * NB
                    bs = b % (SEQ // NB)
                    pos = slice(bs * NB, (bs + 1) * NB)
                    for t4 in t4s:
                        i = inproj(12 + t4, b)
                        A("dve", lambda e, i=i, t4=t4: e.tensor_tensor(out=t1[:, t4, :], in0=pin[i][:], in1=COS[:, pos], op=ALU.mult),
                          reads=[("pin", i), "COS"], writes=[("t1", t4)])
                        i = inproj(18 + t4, b)
                        A("dve", lambda e, i=i, t4=t4: e.tensor_tensor(out=t2[:, t4 % 2, :], in0=pin[i][:], in1=SIN[:, pos], op=ALU.mult),
                          reads=[("pin", i), "SIN"], writes=[("t2", t4 % 2)])
                        A("pool", lambda e, t4=t4: e.tensor_tensor(out=qk_sb[:, t4, :], in0=t1[:, t4, :], in1=t2[:, t4 % 2, :], op=ALU.add),
                          reads=[("t1", t4), ("t2", t4 % 2)], writes=[("qk", t4 // 2)])
                    g2 = t4s[0] // 2
                    A("pool", lambda e: e.dma_start(out=dstd[:, :, tok0:tok0 + NB].rearrange("t p n -> p t n"), in_=qk_sb[:, 2 * g2:2 * g2 + 2, :]),
                      reads=[("qk", g2)], writes=["qkTd"], dma=True)

                def vproj(b):
                    tok0 = b * NB
                    for t in range(2):
                        i = inproj(16 + t, b)
                        A("act", lambda e, i=i, t=t: e.copy(out=v_sb[:, t, :], in_=pin[i][:]), reads=[("pin", i)], writes=["v"])
                    A("pool", lambda e, tok0=tok0: e.dma_start(out=vTd[:, :, tok0:tok0 + NB].rearrange("t p n -> p t n"), in_=v_sb[:]),
                      reads=["v"], writes=["vTd"], dma=True)

                def diag_build():
                    for t in range(2):
                        for k in range(31):
                            A("act", lambda e, t=t, k=k: e.activation(out=diag[:, t, k, :], in_=identf[:], func=AF.Copy, scale=wdw[:, t, k:k + 1]),
                              reads=["identf", "wdw"], writes=["diag"])
                    A("pool", lambda e: e.memset(ones256[:], 1.0 / 256), writes=["ones256"])

                def late_setup():
                    A("pool", lambda e: e.iota(pidx[:], pattern=[[0, 1]], base=0, channel_multiplier=1), writes=["pidx"])
                    A("dve", lambda e: e.tensor_scalar(out=pj[:], in0=pidx[:], scalar1=31, scalar2=None, op0=ALU.bitwise_and),
                      reads=["pidx"], writes=["pj"])
                    A("dve", lambda e: e.tensor_copy(out=pf[:], in_=pj[:]), reads=["pj"], writes=["pf"])
                    A("act", lambda e: e.activation(out=inv[:], in_=pf[:], func=AF.Exp, scale=-math.log(10000.0) / 32.0),
                      reads=["pf"], writes=["inv"])
                    A("dve", lambda e: e.tensor_scalar(out=pj[:], in0=pidx[:], scalar1=32, scalar2=None, op0=ALU.bitwise_and),
                      reads=["pidx", "pf"], writes=["pj"])
                    A("dve", lambda e: e.tensor_copy(out=pf[:], in_=pj[:]), reads=["pj", "inv"], writes=["pf"])
                    A("dve", lambda e: e.tensor_scalar(out=sgn[:], in0=pf[:], scalar1=1.0 / 16, scalar2=-1.0, op0=ALU.mult, op1=ALU.add),
                      reads=["pf"], writes=["sgn"])
                    for c in range(SEQ // NB):
                        cs = slice(c * NB, (c + 1) * NB)
                        A("pool", lambda e, c=c: e.iota(ti[:], pattern=[[1, NB]], base=c * NB, channel_multiplier=0),
                          reads=["tf"], writes=["tfi"])
                        A("dve", lambda e, cs=cs: e.tensor_copy(out=SIN[:, cs], in_=ti[:]), reads=["tfi"], writes=["SIN"])
                        A("dve", lambda e, cs=cs: e.tensor_scalar(out=SIN[:, cs], in0=SIN[:, cs], scalar1=inv[:, 0:1], scalar2=None, op0=ALU.mult),
                          reads=["SIN", "inv"], writes=["SIN"])
                        A("dve", lambda e, cs=cs: e.tensor_scalar(out=COS[:, cs], in0=SIN[:, cs], scalar1=math.pi / 2, scalar2=None, op0=ALU.add),
                          reads=["SIN"], writes=["COS"])
                        range_reduce(SIN[:, cs], tf[:], ti[:], "SIN", "tf")
                        range_reduce(COS[:, cs], tf[:], ti[:], "COS", "tf")
                        A("act", lambda e, cs=cs: e.activation(out=SIN[:, cs], in_=SIN[:, cs], func=AF.Sin), reads=["SIN"], writes=["SIN"])
                        A("act", lambda e, cs=cs: e.activation(out=COS[:, cs], in_=COS[:, cs], func=AF.Sin), reads=["COS"], writes=["COS"])
                        A("dve", lambda e, cs=cs: e.tensor_scalar(out=SIN[:, cs], in0=SIN[:, cs], scalar1=sgn[:, 0:1], scalar2=None, op0=ALU.mult),
                          reads=["SIN", "sgn"], writes=["SIN"])


                def load_weights():
                    stg = [(t1[:, 0:2, :].rearrange("p a b -> p (a b)"), [("t1", 0), ("t1", 1)]),
                           (t1[:, 2:4, :].rearrange("p a b -> p (a b)"), [("t1", 2), ("t1", 3)]),
                           (qk_sb[:].rearrange("p a b -> p (a b)").bitcast(F32), [("qk", 0), ("qk", 1)]),
                           (ycf[:].rearrange("p a b -> p (a b)"), ["ycf"])]
                    wv = w_in[l].rearrange("(kt p) n -> p kt n", p=128)
                    ns = 0
                    for cc, (c0, c1) in enumerate(((0, 1024), (1024, 2048), (2048, NIN))):
                        for kt in range(8):
                            sbuf_, skeys = stg[ns % 4]
                            ns += 1
                            A("sp", lambda e, sbuf_=sbuf_, kt=kt, c0=c0, c1=c1: e.dma_start(out=sbuf_[:, 0:c1 - c0], in_=wv[:, kt, c0:c1]), writes=skeys, dma=True)
                            A("act" if ns % 2 else "dve", lambda e, sbuf_=sbuf_, kt=kt, c0=c0, c1=c1, ns=ns: (e.copy if ns % 2 else e.tensor_copy)(out=wsb[:, kt, c0:c1], in_=sbuf_[:, 0:c1 - c0]),
                              reads=skeys, writes=[("w", cc, kt)])

                stage_N(0)
                load_params()
                stage_T(0)
                if pre_w is None:
                    load_weights()
                diag_build()
                if NBLK > 1:
                    stage_N(1)
                late_setup()
                for b in range(NBLK):
                    stage_B1(b)
                    cfm_a(b)
                    ssm_u(b)
                    cfm_conv(b)
                    rope(b, (0, 1), qTd)
                    cfm_stats(b)
                    rope(b, (2, 3), kTd)
                    if b + 1 < NBLK:
                        stage_T(b + 1)
                    vproj(b)
                    if b + 2 < NBLK:
                        stage_N(b + 2)
                S.emit()

        def phase_att(l):
            with ExitStack() as es:
                sb, ps = mk(es)
                idi, identb, identf = make_ident(sb, "at")
                mb = sb("atmb", [128, 256], BF16)
                A("dve", lambda e: e.tensor_scalar(out=mb[:, 0:128], in0=idi[:], scalar1=0.0, scalar2=None, op0=ALU.is_ge),
                  reads=["idi"], writes=["mb"])
                A("dve", lambda e: e.tensor_scalar(out=mb[:, 128:256], in0=idi[:], scalar1=0.0, scalar2=None, op0=ALU.is_le),
                  reads=["idi"], writes=["mb"])
                sel = sb("atsel", [65, 64], F32)
                A("pool", lambda e: e.memset(sel[:], 0.0), writes=["sel"])
                A("pool", lambda e: e.memset(sel[64:65, :], 1.0), writes=["sel"])
                qT = sb("atq", [128, 2, SEQ], BF16)
                kT = sb("atk", [128, 2, SEQ], BF16)
                vT = sb("atv", [128, 2, SEQ], BF16)
                vext2 = [sb("atvext%d" % i, [128, 32, 65], BF16) for i in range(2)]
                for i in range(2):
                    A("pool", lambda e, i=i: e.memset(vext2[i][:, :, 64:65], 1.0), writes=[("vext", i)])
                accb2 = [sb("atacc%d" % i, [65, SEQ], F32) for i in range(2)]
                yo = sb("atyo", [64, SEQ], F32)
                rb = sb("atrb", [64, NB], F32)
                P = [sb("atP%d" % i, [128, 256], BF16) for i in range(5)]
                pB = ps("atpB", [128, NB], F32)
                pv0 = pB[:].bitcast(BF16).rearrange("p (a b) -> p a b", b=64)
                pS = [ps("atpS%d" % i, [128, 512], F32) for i in range(3)]
                pO = [ps("atpO%d" % i, [128, 512], F32) for i in range(4)]
                kb_ctr = [0]
                for s in range(n_seq):
                    t0 = s * SEQ
                    for dst, src, key in ((qT, qTd, "qT"), (kT, kTd, "kT"), (vT, vTd, "vT")):
                        A("sp", lambda e, dst=dst, src=src, t0=t0: e.dma_start(out=dst[:], in_=src[:, :, t0:t0 + SEQ].rearrange("t p n -> p t n")),
                          writes=[key], dma=True)
                    for ht in range(2):
                        for pi, (d, nb) in enumerate(PATTERNS):
                            qv = qT[:, ht, :].rearrange("p (n r) -> p r n", r=d)
                            kv = kT[:, ht, :].rearrange("p (n r) -> p r n", r=d)
                            vv = vT[:, ht, :].rearrange("p (n r) -> p r n", r=d)
                            avs = [accb2[hh][:].rearrange("p (n r) -> p r n", r=d) for hh in range(2)]
                            for hh in range(2):
                                po = 64 * hh
                                for g8 in range(4):
                                    for j8 in range(8):
                                        bi = g8 * 8 + j8
                                        r, kb = bi // nb, bi % nb
                                        A("pe", lambda e, j8=j8, r=r, kb=kb, vv=vv, po=po: e.transpose(
                                            out=pv0[:, j8, :], in_=vv[po:po + 64, r, kb * 128:(kb + 1) * 128], identity=identb[po:po + 64, po:po + 64]),
                                          reads=["vT", "identb"], writes=[("pv", 0)])
                                    A("act", lambda e, g8=g8, hh=hh: e.copy(out=vext2[hh][:, g8 * 8:(g8 + 1) * 8, 0:64], in_=pv0[:, 0:8, :]),
                                      reads=[("pv", 0)], writes=[("vext", hh)])
                            steps = [(hh, r, kb) for r in range(d) for kb in range(nb) for hh in range(2)]
                            cbase = kb_ctr[0]
                            kb_ctr[0] += len(steps)

                            def emit_qk(si, steps=steps, cbase=cbase, qv=qv, kv=kv, nb=nb):
                                hh, r, kb = steps[si]
                                po = 64 * hh
                                c = cbase + si
                                nq = 256 if kb + 1 < nb else 128
                                pSc = pS[c % 3]
                                A("pe", lambda e: e.matmul(pSc[:, 0:nq], lhsT=kv[po:po + 64, r, kb * 128:(kb + 1) * 128],
                                                           rhs=qv[po:po + 64, r, kb * 128:kb * 128 + nq], start=True, stop=True),
                                  reads=["kT", "qT"], writes=[("pS", c % 3)])

                            def emit_rest(si, steps=steps, cbase=cbase, avs=avs, nb=nb, pi=pi):
                                hh, r, kb = steps[si]
                                bi = r * nb + kb
                                c = cbase + si
                                nq = 256 if kb + 1 < nb else 128
                                pSc, Pc = pS[c % 3], P[c % 5]
                                vx = vext2[hh]
                                A("act", lambda e: e.activation(out=Pc[:, 0:nq], in_=pSc[:, 0:nq], func=AF.Exp, scale=0.125),
                                  reads=[("pS", c % 3)], writes=[("P", c % 5)])
                                A("pool", lambda e: e.tensor_tensor(out=Pc[:, 0:nq], in0=Pc[:, 0:nq], in1=mb[:, 0:nq], op=ALU.mult),
                                  reads=[("P", c % 5), "mb"], writes=[("P", c % 5)])
                                for half in range(nq // 128):
                                    qb = kb + half
                                    slot = 2 * hh + qb % 2
                                    first = (half == 1) or (kb == 0)
                                    A("pe", lambda e, half=half, slot=slot, first=first: e.matmul(
                                        pO[slot][0:65, 0:128], lhsT=vx[:, bi, :], rhs=Pc[:, half * 128:(half + 1) * 128],
                                        start=first, stop=(half == 0), skip_group_check=True),
                                      reads=[("vext", hh), ("P", c % 5)], writes=[("pO", slot)])
                                    if half == 0:
                                        dst = avs[hh][0:65, r, qb * 128:(qb + 1) * 128]
                                        if pi == 0:
                                            A("dve", lambda e, dst=dst, slot=slot: e.tensor_copy(out=dst, in_=pO[slot][0:65, 0:128]),
                                              reads=[("pO", slot)], writes=[("acc", hh)])
                                        else:
                                            A("dve", lambda e, dst=dst, slot=slot: e.tensor_tensor(out=dst, in0=pO[slot][0:65, 0:128], in1=dst, op=ALU.add),
                                              reads=[("pO", slot), ("acc", hh)], writes=[("acc", hh)])

                            for si in range(min(2, len(steps))):
                                emit_qk(si)
                            for si in range(len(steps)):
                                if si + 2 < len(steps):
                                    emit_qk(si + 2)
                                emit_rest(si)
                        for hh in range(2):
                            accb = accb2[hh]
                            po = 64 * hh
                            for c in range(SEQ // NB):
                                cs = slice(c * NB, (c + 1) * NB)
                                A("pe", lambda e, cs=cs, accb=accb: e.matmul(pB[0:64, :], lhsT=sel[:], rhs=accb[0:65, cs], start=True, stop=True),
                                  reads=["sel", ("acc", hh)], writes=[("pv", 0)])
                                A("dve", lambda e: e.reciprocal(out=rb[:], in_=pB[0:64, :]), reads=[("pv", 0)], writes=["rb"])
                                A("dve", lambda e, cs=cs, accb=accb: e.tensor_tensor(out=yo[:, cs], in0=accb[0:64, cs], in1=rb[:], op=ALU.mult),
                                  reads=[("acc", hh), "rb"], writes=["yo"])
                            A("pool", lambda e, ht=ht, po=po, t0=t0: e.dma_start(out=ymix[6 + ht, po:po + 64, t0:t0 + SEQ], in_=yo[:]),
                              reads=["yo"], writes=["ymix"], dma=True)
                S.emit()

        def phase_s5(l):
            with ExitStack() as eso:
                sbo, pso_ = mk(eso)
                Wst = sbo("s5W", [128, 16, 2, 8, 128], BF16)
                Vst = sbo("s5V", [128, 16, 2, 8, 128], BF16)
                BD = sbo("s5BD", [128, 16, 2, 128], BF16)
                Msr = sbo("s5Msr", [128, 8, 8], F32)
                Msi = sbo("s5Msi", [128, 8, 8], F32)
                nMsi = sbo("s5nMsi", [128, 8, 8], F32)
                wglu = sbo("s5wglu", [128, 2, 512], BF16)
                with ExitStack() as es:
                    sb, ps = mk(es)
                    idi, identb, identf = make_ident(sb, "s5")
                    load_w_bf16(wglu, "wglu", s5_glu_w[l], 2, 512)
                    are = sb("s5are", [128, 8], F32)
                    aim = sb("s5aim", [128, 8], F32)
                    ldt = sb("s5ldt", [128, 8], F32)
                    Bre = sb("s5Bre", [128, 8, 16], F32)
                    Bim = sb("s5Bim", [128, 8, 16], F32)
                    Cld = [sb("s5Cld%d" % i, [128, 128], F32) for i in range(2)]
                    CT = [sb("s5CT%d" % i, [128, 8, 16], F32) for i in range(2)]
                    Dsk = sb("s5D", [128, 2], F32)
                    A("sp", lambda e: e.dma_start(out=are[:], in_=s5_a_re[l].rearrange("(gt gl) p -> (gl p) gt", gl=2)), writes=["are"], dma=True)
                    A("sp", lambda e: e.dma_start(out=aim[:], in_=s5_a_im[l].rearrange("(gt gl) p -> (gl p) gt", gl=2)), writes=["aim"], dma=True)
                    for gl in range(2):
                        A("sp", lambda e, gl=gl: e.dma_start(out=ldt[gl * 64:(gl + 1) * 64, :],
                                                            in_=s5_log_dt[l].rearrange("(gt gl) -> gl gt", gl=2)[gl].partition_broadcast(64)),
                          writes=["ldt"], dma=True)
                    A("sp", lambda e: e.dma_start(out=Bre[:], in_=s5_b_re[l].rearrange("(gt gl) p h -> (gl p) gt h", gl=2)), writes=["Bre"], dma=True)
                    A("sp", lambda e: e.dma_start(out=Bim[:], in_=s5_b_im[l].rearrange("(gt gl) p h -> (gl p) gt h", gl=2)), writes=["Bim"], dma=True)
                    for ri, src in enumerate((s5_c_re, s5_c_im)):
                        for gt in range(8):
                            A("sp", lambda e, ri=ri, src=src, gt=gt: e.dma_start(
                                out=Cld[ri][gt * 16:(gt + 1) * 16, :].rearrange("h (gl p) -> h gl p", gl=2),
                                in_=src[l, 2 * gt:2 * gt + 2].rearrange("gl h p -> h gl p")),
                              writes=[("Cld", ri)], dma=True)
                    A("sp", lambda e: e.dma_start(out=Dsk[:], in_=s5_d[l].rearrange("(t p) -> p t", p=128)), writes=["Dsk"], dma=True)
                    pct = ps("s5pct", [128, 4, 128], F32)
                    for ri in range(2):
                        A("pe", lambda e, ri=ri: e.transpose(out=pct[:, ri, :], in_=Cld[ri][:], identity=identf[:]),
                          reads=[("Cld", ri), "identf"], writes=["pct"])
                        A("act", lambda e, ri=ri: e.copy(out=CT[ri][:].rearrange("p a b -> p (a b)"), in_=pct[:, ri, :]), reads=["pct"], writes=[("CT", ri)])

                    def sm(name):
                        return sb("s5_" + name, [128, 8], F32)
                    dt_, adr, mag, th, th2, sn, cs_, lr, li, den, t8, rden, nr, fr, fi = [sm(n) for n in
                        ("dt", "adr", "mag", "th", "th2", "sn", "cs", "lr", "li", "den", "t8", "rden", "nr", "fr", "fi")]
                    tf8 = sm("tf8")
                    ti8 = sb("s5_ti8", [128, 8], I32)

                    def TT(out, a, b, op, eng="dve", r=(), w=()):
                        A(eng, lambda e: e.tensor_tensor(out=out, in0=a, in1=b, op=op), reads=r, writes=w)

                    A("act", lambda e: e.activation(out=dt_[:], in_=ldt[:], func=AF.Exp), reads=["ldt"], writes=["dt"])
                    TT(adr[:], are[:], dt_[:], ALU.mult, r=["are", "dt"], w=["adr"])
                    A("act", lambda e: e.activation(out=mag[:], in_=adr[:], func=AF.Exp), reads=["adr"], writes=["mag"])
                    TT(th[:], aim[:], dt_[:], ALU.mult, r=["aim", "dt"], w=["th"])
                    A("dve", lambda e: e.tensor_scalar(out=th2[:], in0=th[:], scalar1=math.pi / 2, scalar2=None, op0=ALU.add), reads=["th"], writes=["th2"])
                    range_reduce(th[:], tf8[:], ti8[:], "th", "tf8")
                    range_reduce(th2[:], tf8[:], ti8[:], "th2", "tf8")
                    A("act", lambda e: e.activation(out=sn[:], in_=th[:], func=AF.Sin), reads=["th"], writes=["sn"])
                    A("act", lambda e: e.activation(out=cs_[:], in_=th2[:], func=AF.Sin), reads=["th2"], writes=["cs"])
                    TT(lr[:], mag[:], cs_[:], ALU.mult, r=["mag", "cs"], w=["lr"])
                    TT(li[:], mag[:], sn[:], ALU.mult, r=["mag", "sn"], w=["li"])
                    TT(den[:], are[:], are[:], ALU.mult, r=["are"], w=["den"])
                    TT(t8[:], aim[:], aim[:], ALU.mult, r=["aim"], w=["t8"])
                    TT(den[:], den[:], t8[:], ALU.add, r=["den", "t8"], w=["den"])
                    A("dve", lambda e: e.reciprocal(out=rden[:], in_=den[:]), reads=["den"], writes=["rden"])
                    A("dve", lambda e: e.tensor_scalar(out=nr[:], in0=lr[:], scalar1=-1.0, scalar2=None, op0=ALU.add), reads=["lr"], writes=["nr"])
                    TT(fr[:], nr[:], are[:], ALU.mult, r=["nr", "are"], w=["fr"])
                    TT(t8[:], li[:], aim[:], ALU.mult, r=["li", "aim", "den"], w=["t8"])
                    TT(fr[:], fr[:], t8[:], ALU.add, r=["fr", "t8"], w=["fr"])
                    TT(fr[:], fr[:], rden[:], ALU.mult, r=["fr", "rden"], w=["fr"])
                    TT(fi[:], li[:], are[:], ALU.mult, r=["li", "are"], w=["fi"])
                    TT(t8[:], nr[:], aim[:], ALU.mult, r=["nr", "aim", "fr"], w=["t8"])
                    TT(fi[:], fi[:], t8[:], ALU.subtract, r=["fi", "t8"], w=["fi"])
                    TT(fi[:], fi[:], rden[:], ALU.mult, r=["fi", "rden"], w=["fi"])
                    Bbr = sb("s5Bbr", [128, 8, 16], F32)
                    Bbi = sb("s5Bbi", [128, 8, 16], F32)
                    tb = sb("s5tb", [128, 8, 16], F32)
                    frb = fr[:].unsqueeze(2).to_broadcast([128, 8, 16])
                    fib = fi[:].unsqueeze(2).to_broadcast([128, 8, 16])
                    TT(Bbr[:], Bre[:], frb, ALU.mult, r=["Bre", "fr"], w=["Bbr"])
                    TT(tb[:], Bim[:], fib, ALU.mult, r=["Bim", "fi"], w=["tb"])
                    TT(Bbr[:], Bbr[:], tb[:], ALU.subtract, r=["Bbr", "tb"], w=["Bbr"])
                    TT(Bbi[:], Bim[:], frb, ALU.mult, r=["Bim", "fr"], w=["Bbi"])
                    TT(tb[:], Bre[:], fib, ALU.mult, r=["Bre", "fi", "Bbr"], w=["tb"])
                    TT(Bbi[:], Bbi[:], tb[:], ALU.add, r=["Bbi", "tb"], w=["Bbi"])
                    Lr = sb("s5Lr", [128, 17, 8], F32)
                    Li = sb("s5Li", [128, 17, 8], F32)
                    tl = sb("s5tl", [128, 8, 8], F32)
                    A("pool", lambda e: e.memset(Lr[:, 0, :], 1.0), writes=["L"])
                    A("pool", lambda e: e.memset(Li[:, 0, :], 0.0), writes=["L"])
                    A("dve", lambda e: e.tensor_copy(out=Lr[:, 1, :], in_=lr[:]), reads=["lr", "L"], writes=["L"])
                    A("dve", lambda e: e.tensor_copy(out=Li[:, 1, :], in_=li[:]), reads=["li", "L"], writes=["L"])
                    n = 1
                    while n < 16:
                        src_r, src_i = Lr[:, 1:n + 1, :], Li[:, 1:n + 1, :]
                        mr = Lr[:, n:n + 1, :].to_broadcast([128, n, 8])
                        mi = Li[:, n:n + 1, :].to_broadcast([128, n, 8])
                        dr, di = Lr[:, n + 1:2 * n + 1, :], Li[:, n + 1:2 * n + 1, :]
                        tln = tl[:, 0:n, :]
                        TT(dr, src_r, mr, ALU.mult, r=["L"], w=["L"])
                        TT(tln, src_i, mi, ALU.mult, r=["L"], w=["tl"])
                        TT(dr, dr, tln, ALU.subtract, r=["L", "tl"], w=["L"])
                        TT(di, src_r, mi, ALU.mult, r=["L"], w=["L"])
                        TT(tln, src_i, mr, ALU.mult, r=["L"], w=["tl"])
                        TT(di, di, tln, ALU.add, r=["L", "tl"], w=["L"])
                        n *= 2
                    A("dve", lambda e: e.tensor_copy(out=Msr[:, 0, :], in_=Lr[:, 16, :]), reads=["L"], writes=["Ms"])
                    A("dve", lambda e: e.tensor_copy(out=Msi[:, 0, :], in_=Li[:, 16, :]), reads=["L"], writes=["Ms"])
                    for s_ in range(7):
                        TT(Msr[:, s_ + 1, :], Msr[:, s_, :], Msr[:, s_, :], ALU.mult, r=["Ms"], w=["Ms"])
                        TT(t8[:], Msi[:, s_, :], Msi[:, s_, :], ALU.mult, r=["Ms"], w=["t8"])
                        TT(Msr[:, s_ + 1, :], Msr[:, s_ + 1, :], t8[:], ALU.subtract, r=["Ms", "t8"], w=["Ms"])
                        TT(t8[:], Msr[:, s_, :], Msi[:, s_, :], ALU.mult, r=["Ms"], w=["t8"])
                        A("dve", lambda e, s_=s_: e.tensor_scalar(out=Msi[:, s_ + 1, :], in0=t8[:], scalar1=2.0, scalar2=None, op0=ALU.mult),
                          reads=["t8"], writes=["Ms"])
                    A("dve", lambda e: e.tensor_scalar(out=nMsi[:], in0=Msi[:], scalar1=-1.0, scalar2=None, op0=ALU.mult), reads=["Ms"], writes=["nMs"])
                    LB = [sb("s5LB%d" % i, [128, 16, 8, 16], F32) for i in range(2)]
                    CL = [sb("s5CL%d" % i, [128, 16, 8, 16], F32) for i in range(2)]
                    scr = sb("s5scr", [128, 2048], F32)
                    tq = scr[:].rearrange("p (a b c) -> p a b c", a=16, b=8, c=16)
                    shp = [128, 16, 8, 16]
                    Lr0 = Lr[:, 0:16, :].unsqueeze(3).to_broadcast(shp)
                    Li0 = Li[:, 0:16, :].unsqueeze(3).to_broadcast(shp)
                    Lr1 = Lr[:, 1:17, :].unsqueeze(3).to_broadcast(shp)
                    Li1 = Li[:, 1:17, :].unsqueeze(3).to_broadcast(shp)
                    Bbrb = Bbr[:].unsqueeze(1).to_broadcast(shp)
                    Bbib = Bbi[:].unsqueeze(1).to_broadcast(shp)
                    CTrb = CT[0][:].unsqueeze(1).to_broadcast(shp)
                    CTib = CT[1][:].unsqueeze(1).to_broadcast(shp)
                    TT(LB[0][:], Lr0, Bbrb, ALU.mult, r=["L", "Bbr"], w=["LB0"])
                    TT(tq, Li0, Bbib, ALU.mult, r=["L", "Bbi"], w=["tq"])
                    TT(LB[0][:], LB[0][:], tq, ALU.subtract, r=["LB0", "tq"], w=["LB0"])
                    TT(LB[1][:], Lr0, Bbib, ALU.mult, r=["L", "Bbi"], w=["LB1"])
                    TT(tq, Li0, Bbrb, ALU.mult, r=["L", "Bbr", "LB0"], w=["tq"])
                    TT(LB[1][:], LB[1][:], tq, ALU.add, r=["LB1", "tq"], w=["LB1"])
                    TT(CL[0][:], Lr1, CTrb, ALU.mult, r=["L", ("CT", 0), "LB1"], w=["CL0"])
                    TT(tq, Li1, CTib, ALU.mult, r=["L", ("CT", 1), "LB1"], w=["tq"])
                    TT(CL[0][:], CL[0][:], tq, ALU.subtract, r=["CL0", "tq"], w=["CL0"])
                    TT(CL[1][:], Li1, CTrb, ALU.mult, r=["L", ("CT", 0)], w=["CL1"])
                    TT(tq, Lr1, CTib, ALU.mult, r=["L", ("CT", 1), "CL0"], w=["tq"])
                    TT(CL[1][:], CL[1][:], tq, ALU.add, r=["CL1", "tq"], w=["CL1"])
                    A("dve", lambda e: e.tensor_scalar(out=CL[1][:], in0=CL[1][:], scalar1=-1.0, scalar2=None, op0=ALU.mult), reads=["CL1"], writes=["CL1"])
                    Mi = sb("s5Mi", [128, 8, 8], I32)
                    Mk = sb("s5Mk", [128, 8, 8], F32)
                    for ct in range(2):
                        for gl in range(2):
                            A("pool", lambda e, ct=ct, gl=gl: e.iota(Mi[gl * 64:(gl + 1) * 64, 4 * ct:4 * ct + 4, :], pattern=[[-2, 4], [1, 8]],
                                                                    base=-gl, channel_multiplier=0), writes=["Mi"])
                    A("dve", lambda e: e.tensor_scalar(out=Mk[:], in0=Mi[:], scalar1=0.0, scalar2=None, op0=ALU.is_equal), reads=["Mi"], writes=["Mk"])
                    shp4 = [128, 8, 8, 16]
                    Mkb = Mk[:].unsqueeze(3).to_broadcast(shp4)
                    Cexp = sb("s5Cexp", [128, 2, 8, 128], F32)
                    for ri in range(2):
                        TT(Cexp[:, ri].rearrange("p g (q h) -> p g q h", h=16), CT[ri][:].unsqueeze(2).to_broadcast(shp4), Mkb, ALU.mult,
                           r=[("CT", ri), "Mk"], w=["Cexp"])
                    A("dve", lambda e: e.tensor_scalar(out=Cexp[:, 1], in0=Cexp[:, 1], scalar1=-1.0, scalar2=None, op0=ALU.mult), reads=["Cexp"], writes=["Cexp"])
                    for i in range(16):
                        for ri in range(2):
                            TT(Vst[:, i, ri].rearrange("p g (q h) -> p g q h", h=16), CL[ri][:, i].unsqueeze(2).to_broadcast(shp4), Mkb, ALU.mult,
                               eng=("pool" if (2 * i + ri) % 3 else "dve"), r=["CL%d" % ri, "Mk"], w=[("Vst", i, ri)])
                    Aexp0 = sb("s5Aexp0", [128, 2, 8, 128], F32)
                    Aexp = [Aexp0[:], scr[:].rearrange("p (r g n) -> p r g n", r=2, g=8, n=128)]
                    pbd = [ps("s5pbd%d" % i, [128, 512], F32) for i in range(2)]
                    pw = [ps("s5pw%d" % i, [128, 4, 128], F32) for i in range(2)]
                    nw = 0
                    for k in range(16):
                        Ak = Aexp[k % 2]
                        akey = ("Aexp", k % 2)
                        for ri in range(2):
                            TT(Ak[:, ri].rearrange("p g (q h) -> p g q h", h=16), LB[ri][:, k].unsqueeze(2).to_broadcast(shp4), Mkb, ALU.mult,
                               r=["LB%d" % ri, "Mk"], w=[akey, "tq"])
                        for ct in range(2):
                            pb = pbd[(2 * k + ct) % 2]
                            pkey = ("pbd", (2 * k + ct) % 2)
                            n_ = 0
                            for q in range(4):
                                for ri in range(2):
                                    gt = 4 * ct + q
                                    A("pe", lambda e, pb=pb, Ak=Ak, ri=ri, gt=gt, n_=n_: e.matmul(pb[:, 0:128], lhsT=Ak[:, ri, gt, :], rhs=Cexp[:, ri, gt, :],
                                                                                             start=(n_ == 0), stop=(n_ == 7)),
                                      reads=[akey, "Cexp"], writes=[pkey])
                                    n_ += 1
                            if k == 0:
                                A("dve", lambda e, pb=pb, ct=ct: e.scalar_tensor_tensor(out=BD[:, 0, ct, :], in0=identf[:], scalar=Dsk[:, ct:ct + 1],
                                                                                         in1=pb[:, 0:128], op0=ALU.mult, op1=ALU.add),
                                  reads=[pkey, "identf", "Dsk"], writes=["BD"])
                            else:
                                A("act", lambda e, pb=pb, ct=ct, k=k: e.copy(out=BD[:, k, ct, :], in_=pb[:, 0:128]), reads=[pkey], writes=["BD"])
                        for ri in range(2):
                            for g4 in range(2):
                                pwc = pw[nw % 2]
                                wkey = ("pw", nw % 2)
                                nw += 1
                                for q in range(4):
                                    gt = 4 * g4 + q
                                    A("pe", lambda e, pwc=pwc, Ak=Ak, ri=ri, gt=gt, q=q: e.transpose(out=pwc[:, q, :], in_=Ak[:, ri, gt, :], identity=identf[:]),
                                      reads=[akey, "identf"], writes=[wkey])
                                A("act", lambda e, pwc=pwc, k=k, ri=ri, g4=g4: e.copy(out=Wst[:, k, ri, 4 * g4:4 * g4 + 4, :], in_=pwc[:]),
                                  reads=[wkey], writes=["Wst"])
                    S.emit()
                with ExitStack() as es:
                    sb, ps = mk(es)
                    ub = sb("s5ub", [128, 2, SEQ], BF16)
                    ubv = ub[:].rearrange("p t (j c) -> p t j c", j=16)
                    St = [sb("s5St%d" % i, [128, 8, 256], F32) for i in range(2)]
                    Alt = [sb("s5Alt%d" % i, [128, 256], F32) for i in range(4)]
                    Tmp = [sb("s5Tmp%d" % i, [128, 256], F32) for i in range(4)]
                    z = sb("s5z", [128, 2, SEQ], BF16)
                    zv = z[:].rearrange("p t (c j) -> p t j c", j=16)
                    Stb = [sb("s5Stb%d" % i, [128, 8, 256], BF16) for i in range(2)]
                    yss = sb("s5yss", [128, 2, NB], F32)
                    gtmp = [sb("s5gtmp%d" % i, [128, 256], F32) for i in range(1)] * 2
                    pst = [ps("s5pst%d" % i, [128, 512], F32) for i in range(2)]
                    pso = [ps("s5pso%d" % i, [128, 512], F32) for i in range(2)]
                    pg = [ps("s5pg%d" % i, [128, NB], F32) for i in range(4)]
                    n1 = 0
                    n2 = 0
                    for s in range(n_seq):
                        t0 = s * SEQ
                        A("sp", lambda e, t0=t0: e.dma_start(out=z[:], in_=ubd[:, :, t0:t0 + SEQ].rearrange("t p n -> p t n")), writes=["z"], dma=True)
                        A("act", lambda e: e.copy(out=ubv[:, 0], in_=z[:, 0, :].rearrange("p (c j) -> p j c", j=16)), reads=["z"], writes=["ub"])
                        A("dve", lambda e: e.tensor_copy(out=ubv[:, 1], in_=z[:, 1, :].rearrange("p (c j) -> p j c", j=16)), reads=["z"], writes=["ub"])
                        for gt in range(8):
                            ct = gt // 4
                            for ri in range(2):
                                pc_ = pst[n1 % 2]
                                pk = ("pst", n1 % 2)
                                n1 += 1
                                for j in range(16):
                                    A("pe", lambda e, pc_=pc_, j=j, ri=ri, gt=gt, ct=ct: e.matmul(pc_[:, 0:256], lhsT=Wst[:, 15 - j, ri, gt, :], rhs=ubv[:, ct, j, :],
                                                                                             start=(j == 0), stop=(j == 15)),
                                      reads=["ub"], writes=[pk])
                                A("act", lambda e, pc_=pc_, ri=ri, gt=gt: e.copy(out=St[ri][:, gt, :], in_=pc_[:, 0:256]), reads=[pk], writes=[("St", gt)])
                        def hs_step(gt, s_, ch):
                            d = 1 << s_
                            A0, A1, T0, T1 = Alt[2 * ch], Alt[2 * ch + 1], Tmp[2 * ch], Tmp[2 * ch + 1]
                            ak, t0k, t1k = ("Alt", ch), ("Tmp0", ch), ("Tmp1", ch)
                            if s_ % 2 == 0:
                                sr, si, dr, di = St[0][:, gt, :], St[1][:, gt, :], A0[:], A1[:]
                                skey, dkey = ("St", gt), ak
                            else:
                                sr, si, dr, di = A0[:], A1[:], St[0][:, gt, :], St[1][:, gt, :]
                                skey, dkey = ak, ("St", gt)
                            mr, mi, nmi = Msr[:, s_, gt:gt + 1], Msi[:, s_, gt:gt + 1], nMsi[:, s_, gt:gt + 1]

                            def STT(out, a, sc, b, r, w):
                                A("dve", lambda e: e.scalar_tensor_tensor(out=out, in0=a, scalar=sc, in1=b, op0=ALU.mult, op1=ALU.add), reads=r, writes=w)
                            STT(T0[:, d:256], sr[:, 0:256 - d], mr, sr[:, d:256], [skey], [t0k])
                            STT(T1[:, d:256], si[:, 0:256 - d], mr, si[:, d:256], [skey], [t1k])
                            STT(dr[:, d:256], si[:, 0:256 - d], nmi, T0[:, d:256], [skey, t0k], [dkey])
                            STT(di[:, d:256], sr[:, 0:256 - d], mi, T1[:, d:256], [skey, t1k], [dkey])
                            A("pool", lambda e: e.tensor_copy(out=dr[:, 0:d], in_=sr[:, 0:d]), reads=[skey], writes=[dkey])
                            A("pool", lambda e: e.tensor_copy(out=di[:, 0:d], in_=si[:, 0:d]), reads=[skey], writes=[dkey])

                        def hs_pair(gta, gtb):
                            for s_ in range(8):
                                hs_step(gta, s_, 0)
                                hs_step(gtb, s_, 1)
                            for gt in (gta, gtb):
                                for ri in range(2):
                                    A("act" if ri == 0 else "pool", lambda e, ri=ri, gt=gt: (e.copy if ri == 0 else e.tensor_copy)(out=Stb[ri][:, gt, :], in_=St[ri][:, gt, :]),
                                      reads=[("St", gt)], writes=[("Stb", gt)])

                        def y_tile(i, ct):
                            nonlocal n2
                            po_ = pso[n2 % 2]
                            ok = ("pso", n2 % 2)
                            n2 += 1
                            for j in range(i + 1):
                                A("pe", lambda e, po_=po_, i=i, j=j, ct=ct: e.matmul(po_[:, 0:256], lhsT=BD[:, i - j, ct, :], rhs=ubv[:, ct, j, :],
                                                                                start=(j == 0), stop=False, skip_group_check=True),
                                  reads=["ub"], writes=[ok])
                            n_ = 0
                            for q in range(4):
                                for ri in range(2):
                                    gt = 4 * ct + q
                                    A("pe", lambda e, po_=po_, i=i, ri=ri, gt=gt, n_=n_: e.matmul(po_[:, 1:256], lhsT=Vst[:, i, ri, gt, :], rhs=Stb[ri][:, gt, 0:255],
                                                                                             start=False, stop=(n_ == 7), skip_group_check=True),
                                      reads=[("Stb", gt)], writes=[ok])
                                    n_ += 1
                            gt_ = gtmp[n2 % 2]
                            gk = ("gtmp", 0)
                            A("act", lambda e, po_=po_, gt_=gt_: e.activation(out=gt_[:], in_=po_[:, 0:256], func=AF.Square), reads=[ok], writes=[gk])
                            A("dve", lambda e, gt_=gt_: e.tensor_scalar(out=gt_[:], in0=gt_[:], scalar1=0.044715, scalar2=1.0, op0=ALU.mult, op1=ALU.add),
                              reads=[gk], writes=[gk])
                            A("dve", lambda e, po_=po_, gt_=gt_: e.tensor_tensor(out=gt_[:], in0=po_[:, 0:256], in1=gt_[:], op=ALU.mult), reads=[gk, ok], writes=[gk])
                            A("act", lambda e, gt_=gt_: e.activation(out=gt_[:], in_=gt_[:], func=AF.Sigmoid, scale=1.5957691216057308), reads=[gk], writes=[gk])
                            A("dve", lambda e, po_=po_, gt_=gt_, i=i, ct=ct: e.tensor_tensor(out=zv[:, ct, i, :], in0=po_[:, 0:256], in1=gt_[:], op=ALU.mult),
                              reads=[gk, ok], writes=["z"])

                        hs_pair(0, 1)
                        hs_pair(2, 3)
                        for i in range(16):
                            y_tile(i, 0)
                            if i == 1:
                                hs_pair(4, 5)
                            if i == 8:
                                hs_pair(6, 7)
                        for i in range(16):
                            y_tile(i, 1)
                        for c in range(SEQ // NB):
                            cs = slice(c * NB, (c + 1) * NB)
                            for mt in range(4):
                                for kt in range(2):
                                    A("pe", lambda e, mt=mt, kt=kt, cs=cs: e.matmul(pg[mt][:], lhsT=wglu[:, kt, mt * 128:(mt + 1) * 128], rhs=z[:, kt, cs],
                                                                               start=(kt == 0), stop=(kt == 1)),
                                      reads=["z"], writes=[("pg", mt)])
                            for t in range(2):
                                A("act", lambda e, t=t: e.activation(out=yss[:, t, :], in_=pg[2 + t][:], func=AF.Sigmoid), reads=[("pg", 2 + t)], writes=["yss"])
                                A("dve", lambda e, t=t: e.tensor_tensor(out=yss[:, t, :], in0=pg[t][:], in1=yss[:, t, :], op=ALU.mult),
                                  reads=[("pg", t), "yss"], writes=["yss"])
                            A("pool", lambda e, t0=t0, c=c: e.dma_start(out=ymix[4:6, :, t0 + c * NB:t0 + (c + 1) * NB].rearrange("t p n -> p t n"), in_=yss[:]),
                              reads=["yss"], writes=["ymix"], dma=True)
                    S.emit()

        def phase_m3a(l):
            xsrc = x_in if l == 0 else xs1
            with ExitStack() as es:
                sb, ps = mk(es)
                idi, identb, identf = make_ident(sb, "m3")
                wo = sb("m3wo", [128, 8, D], BF16)
                load_w_bf16(wo, "wo", w_out[l], 8, D)
                gg = sb("m3gg", [128, 8], F32)
                A("sp", lambda e: e.dma_start(out=gg[:], in_=grp_norm_g[l].rearrange("(t p) -> p t", p=128)), writes=["gg"], dma=True)
                gb2 = sb("m3gb2", [128, D], F32)
                A("sp", lambda e: e.dma_start(out=gb2[:], in_=norm_ffn_g[l].partition_broadcast(128)), writes=["gb2"], dma=True)
                ones256 = sb("m3ones", [128, 128], BF16)
                A("pool", lambda e: e.memset(ones256[:], 1.0 / 256), writes=["ones256"])
                ym = [sb("m3ym%d" % i, [128, 8, NB], F32) for i in range(2)]
                sqb = sb("m3sqb", [128, 8, NB], BF16)
                rs = [sb("m3rs%d" % i, [128, NB], F32) for i in range(2)]
                mixT = [sb("m3mixT%d" % i, [128, 8, NB], BF16) for i in range(2)]
                xt = [sb("m3xt%d" % i, [128, D], F32) for i in range(8)]
                sqj = sb("m3sqj", [128, D], BF16)
                ss = sb("m3ss", [128, 4], F32)
                rstd = sb("m3rstd", [128, 4], F32)
                hb = [sb("m3h%d" % i, [128, D], BF16) for i in range(4)]
                h2T = [sb("m3h2T%d" % i, [128, 8, NB], BF16) for i in range(2)]
                pstat = [ps("m3pstat%d" % i, [128, NB], F32) for i in range(2)]
                po = [ps("m3po%d" % i, [128, NB], F32) for i in range(4)]
                pt = [ps("m3pt%d" % i, [128, 8, 128], BF16) for i in range(2)]

                def L1(b):
                    tok0 = b * NB
                    ymb = ym[b % 2]
                    yk = ("ym", b % 2)
                    A("sp", lambda e: e.dma_start(out=ymb[:], in_=ymix[:, :, tok0:tok0 + NB].rearrange("t p n -> p t n")),
                      writes=[yk], dma=True)
                    for tile in range(8):
                        A("act", lambda e, tile=tile: e.activation(out=sqb[:, tile, :], in_=ymb[:, tile, :], func=AF.Square),
                          reads=[yk], writes=[("sqb", tile)])

                def L2(b):
                    ymb = ym[b % 2]
                    yk = ("ym", b % 2)
                    mx = mixT[b % 2]
                    for grp in range(4):
                        for t in range(2):
                            tile = 2 * grp + t
                            A("pe", lambda e, grp=grp, tile=tile, t=t: e.matmul(pstat[grp % 2][:], lhsT=ones256[:], rhs=sqb[:, tile, :], start=(t == 0), stop=(t == 1)),
                              reads=[("sqb", tile), "ones256"], writes=[("pstat", grp % 2)])
                        rsb = rs[grp % 2]
                        rk = ("rs", grp % 2)
                        A("act", lambda e, grp=grp, rsb=rsb: e.activation(out=rsb[:], in_=pstat[grp % 2][:], func=AF.Ln, bias=EPS),
                          reads=[("pstat", grp % 2)], writes=[rk])
                        A("act", lambda e, rsb=rsb: e.activation(out=rsb[:], in_=rsb[:], func=AF.Exp, scale=-0.5), reads=[rk], writes=[rk])
                        for t in range(2):
                            tile = 2 * grp + t
                            A("dve", lambda e, tile=tile, rsb=rsb: e.scalar_tensor_tensor(out=mx[:, tile, :], in0=ymb[:, tile, :], scalar=gg[:, tile:tile + 1],
                                                                                      in1=rsb[:], op0=ALU.mult, op1=ALU.mult),
                              reads=[yk, rk, "gg"], writes=[("mixT", b % 2, tile)])

                def OPm(b, tt):
                    mx = mixT[b % 2]
                    xb = xt[(b % 2) * 4 + tt]
                    xk = ("xt", b % 2, tt)
                    r0 = b * NB + tt * 128
                    for nh in range(2):
                        pi_ = (2 * tt + nh) % 4
                        for kt in range(8):
                            A("pe", lambda e, pi_=pi_, kt=kt, nh=nh: e.matmul(po[pi_][:], lhsT=mx[:, kt, tt * 128:(tt + 1) * 128],
                                                                              rhs=wo[:, kt, nh * NB:(nh + 1) * NB], start=(kt == 0), stop=(kt == 7)),
                              reads=[("mixT", b % 2, kt)] + wkeys("wo", 8, 0), writes=[("po", pi_)])
                        A("dve", lambda e, pi_=pi_, nh=nh: e.tensor_tensor(out=xb[:, nh * NB:(nh + 1) * NB], in0=po[pi_][:], in1=xb[:, nh * NB:(nh + 1) * NB], op=ALU.add),
                          reads=[("po", pi_), xk], writes=[xk])
                    A("pool", lambda e: e.dma_start(out=xmid[r0:r0 + 128, :], in_=xb[:]), reads=[xk], writes=["xmid"], dma=True)
                    A("act", lambda e: e.activation(out=sqj[:], in_=xb[:], func=AF.Square, accum_out=ss[:, tt:tt + 1]),
                      reads=[xk], writes=["sqj", ("ss", tt)])
                    A("act", lambda e: e.activation(out=rstd[:, tt:tt + 1], in_=ss[:, tt:tt + 1], func=AF.Ln, scale=1.0 / D, bias=EPS),
                      reads=[("ss", tt)], writes=[("rstd", tt)])
                    A("act", lambda e: e.activation(out=rstd[:, tt:tt + 1], in_=rstd[:, tt:tt + 1], func=AF.Exp, scale=-0.5), reads=[("rstd", tt)], writes=[("rstd", tt)])
                    A("dve", lambda e: e.scalar_tensor_tensor(out=hb[tt][:], in0=xb[:], scalar=rstd[:, tt:tt + 1], in1=gb2[:],
                                                              op0=ALU.mult, op1=ALU.mult),
                      reads=[xk, ("rstd", tt), "gb2"], writes=[("hb", tt)])

                def LX(b):
                    for tt in range(4):
                        xb = xt[(b % 2) * 4 + tt]
                        r0 = b * NB + tt * 128
                        A("sp", lambda e, xb=xb, r0=r0: e.dma_start(out=xb[:], in_=xsrc[r0:r0 + 128, :]), writes=[("xt", b % 2, tt)], dma=True)

                def TR(b, tt):
                    for kt in range(8):
                        A("pe", lambda e, kt=kt: e.transpose(out=pt[tt % 2][:, kt, :], in_=hb[tt][:, kt * 128:(kt + 1) * 128], identity=identb[:]),
                          reads=[("hb", tt), "identb"], writes=[("pt", tt % 2)])
                    A("act", lambda e: e.copy(out=h2T[b % 2][:, :, tt * 128:(tt + 1) * 128], in_=pt[tt % 2][:]),
                      reads=[("pt", tt % 2)], writes=[("h2T", b % 2)])
                    if tt == 3:
                        tok0 = b * NB
                        A("pool", lambda e: e.dma_start(out=h2Td[:, :, tok0:tok0 + NB].rearrange("t p n -> p t n"), in_=h2T[b % 2][:]),
                          reads=[("h2T", b % 2)], writes=["h2Td"], dma=True)

                L1(0)
                LX(0)
                L2(0)
                for b in range(NBLK):
                    if b + 1 < NBLK:
                        L1(b + 1)
                        LX(b + 1)
                    OPm(b, 0)
                    OPm(b, 1)
                    TR(b, 0)
                    OPm(b, 2)
                    TR(b, 1)
                    if b + 1 < NBLK:
                        L2(b + 1)
                    OPm(b, 3)
                    TR(b, 2)
                    TR(b, 3)
                S.emit()

        def phase_m3b(l, last):
            with ExitStack() as es:
                sb, ps = mk(es)
                wg = sb("m4wg", [128, 8, DFF], BF16)
                wu = sb("m4wu", [128, 8, DFF], BF16)
                wd = sb("m4wd", [128, 22, D], BF16)
                load_w_bf16(wg, "wg", w_gate[l], 8, DFF)
                load_w_bf16(wu, "wu", w_up[l], 8, DFF)
                load_w_bf16(wd, "wd", w_down[l], 22, D)
                gbf = sb("m4gbf", [128, D], F32)
                if last:
                    A("sp", lambda e: e.dma_start(out=gbf[:], in_=final_norm_g.partition_broadcast(128)), writes=["gbf"], dma=True)
                h2T = [sb("m4h2T%d" % i, [128, 8, NB], BF16) for i in range(2)]
                ffT = sb("m4ffT", [128, 22, NB], BF16)
                sl = [sb("m4sl%d" % i, [128, NB], F32) for i in range(2)]
                xt = [sb("m4xt%d" % i, [128, D], F32) for i in range(4)]
                sqj = sb("m4sqj", [128, D], BF16)
                ss = sb("m4ss", [128, 4], F32)
                rstd = sb("m4rstd", [128, 4], F32)
                pgt = [ps("m4pg%d" % i, [128, NB], F32) for i in range(2)]
                put = [ps("m4pu%d" % i, [128, NB], F32) for i in range(2)]
                pd = [ps("m4pd%d" % i, [128, NB], F32) for i in range(4)]
                nx = 0
                xdst = out if last else xs1
                for b in range(NBLK):
                    tok0 = b * NB
                    hb = h2T[b % 2]
                    hk = ("h2T", b % 2)
                    A("sp", lambda e, hb=hb, tok0=tok0: e.dma_start(out=hb[:], in_=h2Td[:, :, tok0:tok0 + NB].rearrange("t p n -> p t n")), writes=[hk], dma=True)
                    for tt in range(4):
                        A("sp", lambda e, tt=tt, tok0=tok0: e.dma_start(out=xt[tt][:], in_=xmid[tok0 + tt * 128:tok0 + (tt + 1) * 128, :]), writes=[("xt", tt)], dma=True)
                    for ft in range(22):
                        gi = ft % 2
                        for kt in range(8):
                            A("pe", lambda e, gi=gi, kt=kt, ft=ft, hb=hb: e.matmul(pgt[gi][:], lhsT=wg[:, kt, ft * 128:(ft + 1) * 128], rhs=hb[:, kt, :],
                                                                              start=(kt == 0), stop=(kt == 7)),
                              reads=wkeys("wg", 8, ft * 128) + [hk], writes=[("pgt", gi)])
                        for kt in range(8):
                            A("pe", lambda e, gi=gi, kt=kt, ft=ft, hb=hb: e.matmul(put[gi][:], lhsT=wu[:, kt, ft * 128:(ft + 1) * 128], rhs=hb[:, kt, :],
                                                                              start=(kt == 0), stop=(kt == 7)),
                              reads=wkeys("wu", 8, ft * 128) + [hk], writes=[("put", gi)])
                        A("act", lambda e, gi=gi: e.activation(out=sl[gi][:], in_=pgt[gi][:], func=AF.Silu), reads=[("pgt", gi)], writes=[("sl", gi)])
                        A("dve", lambda e, gi=gi, ft=ft: e.tensor_tensor(out=ffT[:, ft, :], in0=put[gi][:], in1=sl[gi][:], op=ALU.mult),
                          reads=[("put", gi), ("sl", gi)], writes=[("ffT", ft)])
                    for tt in range(4):
                        xb = xt[tt]
                        xk = ("xt", tt)
                        r0 = tok0 + tt * 128
                        for nh in range(2):
                            pi_ = (2 * tt + nh) % 4
                            for ft in range(22):
                                A("pe", lambda e, pi_=pi_, ft=ft, tt=tt, nh=nh: e.matmul(pd[pi_][:], lhsT=ffT[:, ft, tt * 128:(tt + 1) * 128],
                                                                                       rhs=wd[:, ft, nh * NB:(nh + 1) * NB], start=(ft == 0), stop=(ft == 21)),
                                  reads=[("ffT", ft)] + wkeys("wd", 22, 0), writes=[("pd", pi_)])
                            A("dve", lambda e, pi_=pi_, xb=xb, nh=nh: e.tensor_tensor(out=xb[:, nh * NB:(nh + 1) * NB], in0=pd[pi_][:], in1=xb[:, nh * NB:(nh + 1) * NB], op=ALU.add),
                              reads=[("pd", pi_), xk], writes=[xk])
                        if last:
                            A("act", lambda e, xb=xb, tt=tt: e.activation(out=sqj[:], in_=xb[:], func=AF.Square, accum_out=ss[:, tt:tt + 1]),
                              reads=[xk], writes=["sqj", "ss"])
                            A("act", lambda e, tt=tt: e.activation(out=rstd[:, tt:tt + 1], in_=ss[:, tt:tt + 1], func=AF.Ln, scale=1.0 / D, bias=EPS),
                              reads=["ss"], writes=["rstd"])
                            A("act", lambda e, tt=tt: e.activation(out=rstd[:, tt:tt + 1], in_=rstd[:, tt:tt + 1], func=AF.Exp, scale=-0.5), reads=["rstd"], writes=["rstd"])
                            A("dve", lambda e, xb=xb, tt=tt: e.scalar_tensor_tensor(out=xb[:], in0=xb[:], scalar=rstd[:, tt:tt + 1], in1=gbf[:],
                                                                                      op0=ALU.mult, op1=ALU.mult),
                              reads=[xk, "rstd", "gbf"], writes=[xk])
                        A("pool", lambda e, xb=xb, r0=r0: e.dma_start(out=xdst[r0:r0 + 128, :], in_=xb[:]), reads=[xk], writes=["xdst"], dma=True)
                S.emit()

        HF = DFF // 2

        def phase_mf1(l):
            xsrc = x_in if l == 0 else xs1
            with ExitStack() as es:
                sb, ps = mk(es)
                idi, identb, identf = make_ident(sb, "f1")
                wo = sb("f1wo", [128, 8, D], BF16)
                load_w_bf16(wo, "wo", w_out[l], 8, D)
                gg = sb("f1gg", [128, 8], F32)
                A("act", lambda e: e.dma_start(out=gg[:], in_=grp_norm_g[l].rearrange("(t p) -> p t", p=128)), writes=["gg"], dma=True)
                gb2 = sb("f1gb2", [128, D], F32)
                A("act", lambda e: e.dma_start(out=gb2[:], in_=norm_ffn_g[l].partition_broadcast(128)), writes=["gb2"], dma=True)
                ones256 = sb("f1ones", [128, 128], BF16)
                A("pool", lambda e: e.memset(ones256[:], 1.0 / 256), writes=["ones256"])
                wg = sb("f1wg", [128, 8, HF], BF16)
                wu = sb("f1wu", [128, 8, HF], BF16)
                wd = sb("f1wd", [128, 11, D], BF16)
                load_w_bf16(wg, "wg", w_gate[l][:, 0:HF], 8, HF)
                load_w_bf16(wu, "wu", w_up[l][:, 0:HF], 8, HF)
                load_w_bf16(wd, "wd", w_down[l][0:HF, :], 11, D)
                ym = sb("f1ym", [128, 8, NB], F32)
                sqb = sb("f1sqb", [128, 8, NB], BF16)
                rs = [sb("f1rs%d" % i, [128, NB], F32) for i in range(2)]
                mixT = sb("f1mixT", [128, 8, NB], BF16)
                xt = [sb("f1xt%d" % i, [128, D], F32) for i in range(8)]
                sqj = sb("f1sqj", [128, D], BF16)
                ss = sb("f1ss", [128, 4], F32)
                rstd = sb("f1rstd", [128, 4], F32)
                hb = [sb("f1h%d" % i, [128, D], BF16) for i in range(4)]
                h2T = [sb("f1h2T%d" % i, [128, 8, NB], BF16) for i in range(2)]
                ffT = sb("f1ffT", [128, 11, NB], BF16)
                sl = sb("f1sl", [128, NB], F32)
                pstat = ps("f1pstat", [128, NB], F32)
                po = [ps("f1po%d" % i, [128, NB], F32) for i in range(2)]
                pt = ps("f1pt", [128, 8, 128], BF16)
                pgt2 = [ps("f1pg%d" % i, [128, NB], F32) for i in range(2)]
                put2 = [ps("f1pu%d" % i, [128, NB], F32) for i in range(2)]
                sl2 = [sl, sb("f1sl1", [128, NB], F32)]

                def L1(b):
                    tok0 = b * NB
                    A("sp", lambda e: e.dma_start(out=ym[:], in_=ymix[:, :, tok0:tok0 + NB].rearrange("t p n -> p t n")),
                      writes=["ym"], dma=True)
                    for tile in range(8):
                        A("act", lambda e, tile=tile: e.activation(out=sqb[:, tile, :], in_=ym[:, tile, :], func=AF.Square),
                          reads=["ym"], writes=[("sqb", tile)])

                def LX(b):
                    for tt in range(4):
                        xb = xt[(b % 2) * 4 + tt]
                        r0 = b * NB + tt * 128
                        A("sp", lambda e, xb=xb, r0=r0: e.dma_start(out=xb[:], in_=xsrc[r0:r0 + 128, :]), writes=[("xt", b % 2, tt)], dma=True)

                def L2(b):
                    for grp in range(4):
                        for t in range(2):
                            tile = 2 * grp + t
                            A("pe", lambda e, tile=tile, t=t: e.matmul(pstat[:], lhsT=ones256[:], rhs=sqb[:, tile, :], start=(t == 0), stop=(t == 1)),
                              reads=[("sqb", tile), "ones256"], writes=["pstat"])
                        rsb = rs[grp % 2]
                        rk = ("rs", grp % 2)
                        A("act", lambda e, rsb=rsb: e.activation(out=rsb[:], in_=pstat[:], func=AF.Ln, bias=EPS), reads=["pstat"], writes=[rk])
                        A("act", lambda e, rsb=rsb: e.activation(out=rsb[:], in_=rsb[:], func=AF.Exp, scale=-0.5), reads=[rk], writes=[rk])
                        for t in range(2):
                            tile = 2 * grp + t
                            A("dve", lambda e, tile=tile, rsb=rsb: e.scalar_tensor_tensor(out=mixT[:, tile, :], in0=ym[:, tile, :], scalar=gg[:, tile:tile + 1],
                                                                                      in1=rsb[:], op0=ALU.mult, op1=ALU.mult),
                              reads=["ym", rk, "gg"], writes=[("mixT", tile)])

                def OPm(b, tt):
                    xb = xt[(b % 2) * 4 + tt]
                    xk = ("xt", b % 2, tt)
                    for nh in range(2):
                        for kt in range(8):
                            A("pe", lambda e, kt=kt, nh=nh: e.matmul(po[nh][:], lhsT=mixT[:, kt, tt * 128:(tt + 1) * 128],
                                                                     rhs=wo[:, kt, nh * NB:(nh + 1) * NB], start=(kt == 0), stop=(kt == 7)),
                              reads=[("mixT", kt)] + wkeys("wo", 8, 0), writes=[("po", nh)])
                        A("dve", lambda e, nh=nh: e.tensor_tensor(out=xb[:, nh * NB:(nh + 1) * NB], in0=po[nh][:], in1=xb[:, nh * NB:(nh + 1) * NB], op=ALU.add),
                          reads=[("po", nh), xk], writes=[xk])
                    A("act", lambda e: e.activation(out=sqj[:], in_=xb[:], func=AF.Square, accum_out=ss[:, tt:tt + 1]),
                      reads=[xk], writes=["sqj", ("ss", tt)])
                    A("act", lambda e: e.activation(out=rstd[:, tt:tt + 1], in_=ss[:, tt:tt + 1], func=AF.Ln, scale=1.0 / D, bias=EPS),
                      reads=[("ss", tt)], writes=[("rstd", tt)])
                    A("act", lambda e: e.activation(out=rstd[:, tt:tt + 1], in_=rstd[:, tt:tt + 1], func=AF.Exp, scale=-0.5), reads=[("rstd", tt)], writes=[("rstd", tt)])
                    A("dve", lambda e: e.scalar_tensor_tensor(out=hb[tt][:], in0=xb[:], scalar=rstd[:, tt:tt + 1], in1=gb2[:],
                                                              op0=ALU.mult, op1=ALU.mult),
                      reads=[xk, ("rstd", tt), "gb2"], writes=[("hb", tt)])

                def TR(b, tt):
                    for kt in range(8):
                        A("pe", lambda e, kt=kt: e.transpose(out=pt[:, kt, :], in_=hb[tt][:, kt * 128:(kt + 1) * 128], identity=identb[:]),
                          reads=[("hb", tt), "identb"], writes=["pt"])
                    A("act", lambda e: e.copy(out=h2T[b % 2][:, :, tt * 128:(tt + 1) * 128], in_=pt[:]),
                      reads=["pt"], writes=[("h2T", b % 2)])
                    if tt == 3:
                        tok0 = b * NB
                        A("pool", lambda e: e.dma_start(out=h2Td[:, :, tok0:tok0 + NB].rearrange("t p n -> p t n"), in_=h2T[b % 2][:]),
                          reads=[("h2T", b % 2)], writes=["h2Td"], dma=True)

                def GU(b, ft):
                    hk = ("h2T", b % 2)
                    hsrc = h2T[b % 2]
                    gi = ft % 2
                    pgt, put, slb = pgt2[gi], put2[gi], sl2[gi]
                    for kt in range(8):
                        A("pe", lambda e, kt=kt: e.matmul(pgt[:], lhsT=wg[:, kt, ft * 128:(ft + 1) * 128], rhs=hsrc[:, kt, :], start=(kt == 0), stop=(kt == 7)),
                          reads=wkeys("wg", 8, ft * 128) + [hk], writes=[("pgt", gi)])
                    for kt in range(8):
                        A("pe", lambda e, kt=kt: e.matmul(put[:], lhsT=wu[:, kt, ft * 128:(ft + 1) * 128], rhs=hsrc[:, kt, :], start=(kt == 0), stop=(kt == 7)),
                          reads=wkeys("wu", 8, ft * 128) + [hk], writes=[("put", gi)])
                    A("act", lambda e: e.activation(out=slb[:], in_=pgt[:], func=AF.Silu), reads=[("pgt", gi)], writes=[("sl", gi)])
                    A("dve", lambda e: e.tensor_tensor(out=ffT[:, ft, :], in0=put[:], in1=slb[:], op=ALU.mult),
                      reads=[("put", gi), ("sl", gi)], writes=[("ffT", ft)])

                def DN(b, tt):
                    xb = xt[(b % 2) * 4 + tt]
                    xk = ("xt", b % 2, tt)
                    r0 = b * NB + tt * 128
                    for nh in range(2):
                        for ft in range(11):
                            A("pe", lambda e, ft=ft, nh=nh: e.matmul(po[nh][:], lhsT=ffT[:, ft, tt * 128:(tt + 1) * 128],
                                                                     rhs=wd[:, ft, nh * NB:(nh + 1) * NB], start=(ft == 0), stop=(ft == 10)),
                              reads=[("ffT", ft)] + wkeys("wd", 11, 0), writes=[("po", nh)])
                        A("dve", lambda e, nh=nh: e.tensor_tensor(out=xb[:, nh * NB:(nh + 1) * NB], in0=po[nh][:], in1=xb[:, nh * NB:(nh + 1) * NB], op=ALU.add),
                          reads=[("po", nh), xk], writes=[xk])
                    A("pool", lambda e: e.dma_start(out=xmid[r0:r0 + 128, :], in_=xb[:]), reads=[xk], writes=["xmid"], dma=True)

                def m3a_items(b):
                    return [lambda: L2(b), lambda: OPm(b, 0), lambda: OPm(b, 1), lambda: TR(b, 0), lambda: OPm(b, 2),
                            lambda: TR(b, 1), lambda: OPm(b, 3), lambda: TR(b, 2), lambda: TR(b, 3)]

                L1(0)
                LX(0)
                for it in m3a_items(0):
                    it()
                for b in range(NBLK):
                    nxt = []
                    if b + 1 < NBLK:
                        L1(b + 1)
                        LX(b + 1)
                        nxt = m3a_items(b + 1)
                    for ft in range(11):
                        GU(b, ft)
                        if ft < len(nxt):
                            nxt[ft]()
                    for tt in range(4):
                        DN(b, tt)
                S.emit()

        def phase_mf2(l, last, prefetch=None):
            with ExitStack() as es:
                sb, ps = mk(es)
                wg = sb("f2wg", [128, 8, HF], BF16)
                wu = sb("f2wu", [128, 8, HF], BF16)
                wd = sb("f2wd", [128, 11, D], BF16)
                load_w_bf16(wg, "wg", w_gate[l][:, HF:DFF], 8, HF)
                load_w_bf16(wu, "wu", w_up[l][:, HF:DFF], 8, HF)
                load_w_bf16(wd, "wd", w_down[l][HF:DFF, :], 11, D)
                gbf = sb("f2gbf", [128, D], F32)
                if last:
                    A("act", lambda e: e.dma_start(out=gbf[:], in_=final_norm_g.partition_broadcast(128)), writes=["gbf"], dma=True)
                h2T = [sb("f2h2T%d" % i, [128, 8, NB], BF16) for i in range(2)]
                ffT = sb("f2ffT", [128, 11, NB], BF16)
                sl = [sb("f2sl%d" % i, [128, NB], F32) for i in range(2)]
                xt = [sb("f2xt%d" % i, [128, D], F32) for i in range(8)]
                sqj = sb("f2sqj", [128, D], BF16)
                ss = sb("f2ss", [128, 4], F32)
                rstd = sb("f2rstd", [128, 4], F32)
                pgt = [ps("f2pg%d" % i, [128, NB], F32) for i in range(2)]
                put = [ps("f2pu%d" % i, [128, NB], F32) for i in range(2)]
                pd = [ps("f2pd%d" % i, [128, NB], F32) for i in range(4)]
                xdst = out if last else xs1

                def LD(b):
                    tok0 = b * NB
                    A("sp", lambda e: e.dma_start(out=h2T[b % 2][:], in_=h2Td[:, :, tok0:tok0 + NB].rearrange("t p n -> p t n")), writes=[("h2T", b % 2)], dma=True)
                    for tt in range(4):
                        A("sp", lambda e, tt=tt: e.dma_start(out=xt[(b % 2) * 4 + tt][:], in_=xmid[tok0 + tt * 128:tok0 + (tt + 1) * 128, :]),
                          writes=[("xt", b % 2, tt)], dma=True)

                pf_chunks = []
                if prefetch is not None:
                    wnext, wsrc = prefetch
                    stgp = [sb("f2stg%d" % i, [128, 1024], F32) for i in range(2)]
                    wvn = wsrc.rearrange("(kt p) n -> p kt n", p=128)
                    for cc, (c0, c1) in enumerate(((0, 1024), (1024, 2048), (2048, NIN))):
                        for kt in range(8):
                            pf_chunks.append((kt, c0, c1))

                def PF(n):
                    for _ in range(n):
                        if not pf_chunks:
                            return
                        kt, c0, c1 = pf_chunks.pop(0)
                        i = len(pf_chunks) % 2
                        A("sp", lambda e, i=i, kt=kt, c0=c0, c1=c1: e.dma_start(out=stgp[i][:, 0:c1 - c0], in_=wvn[:, kt, c0:c1]), writes=[("stgp", i)], dma=True)
                        A("pool", lambda e, i=i, kt=kt, c0=c0, c1=c1: e.tensor_copy(out=wnext[:, kt, c0:c1], in_=stgp[i][:, 0:c1 - c0]),
                          reads=[("stgp", i)], writes=["wnext"])

                LD(0)
                for b in range(NBLK):
                    if b + 1 < NBLK:
                        LD(b + 1)
                    PF(2)
                    tok0 = b * NB
                    hb = h2T[b % 2]
                    hk = ("h2T", b % 2)
                    for ft in range(11):
                        gi = ft % 2
                        for kt in range(8):
                            A("pe", lambda e, gi=gi, kt=kt, ft=ft, hb=hb: e.matmul(pgt[gi][:], lhsT=wg[:, kt, ft * 128:(ft + 1) * 128], rhs=hb[:, kt, :],
                                                                              start=(kt == 0), stop=(kt == 7)),
                              reads=wkeys("wg", 8, ft * 128) + [hk], writes=[("pgt", gi)])
                        for kt in range(8):
                            A("pe", lambda e, gi=gi, kt=kt, ft=ft, hb=hb: e.matmul(put[gi][:], lhsT=wu[:, kt, ft * 128:(ft + 1) * 128], rhs=hb[:, kt, :],
                                                                              start=(kt == 0), stop=(kt == 7)),
                              reads=wkeys("wu", 8, ft * 128) + [hk], writes=[("put", gi)])
                        A("act", lambda e, gi=gi: e.activation(out=sl[gi][:], in_=pgt[gi][:], func=AF.Silu), reads=[("pgt", gi)], writes=[("sl", gi)])
                        A("dve", lambda e, gi=gi, ft=ft: e.tensor_tensor(out=ffT[:, ft, :], in0=put[gi][:], in1=sl[gi][:], op=ALU.mult),
                          reads=[("put", gi), ("sl", gi)], writes=[("ffT", ft)])
                    for tt in range(4):
                        xb = xt[(b % 2) * 4 + tt]
                        xk = ("xt", b % 2, tt)
                        r0 = tok0 + tt * 128
                        for nh in range(2):
                            pi_ = (2 * tt + nh) % 4
                            for ft in range(11):
                                A("pe", lambda e, pi_=pi_, ft=ft, tt=tt, nh=nh: e.matmul(pd[pi_][:], lhsT=ffT[:, ft, tt * 128:(tt + 1) * 128],
                                                                                       rhs=wd[:, ft, nh * NB:(nh + 1) * NB], start=(ft == 0), stop=(ft == 10)),
                                  reads=[("ffT", ft)] + wkeys("wd", 11, 0), writes=[("pd", pi_)])
                            A("dve", lambda e, pi_=pi_, xb=xb, nh=nh: e.tensor_tensor(out=xb[:, nh * NB:(nh + 1) * NB], in0=pd[pi_][:], in1=xb[:, nh * NB:(nh + 1) * NB], op=ALU.add),
                              reads=[("pd", pi_), xk], writes=[xk])
                        if last:
                            A("act", lambda e, xb=xb, tt=tt: e.activation(out=sqj[:], in_=xb[:], func=AF.Square, accum_out=ss[:, tt:tt + 1]),
                              reads=[xk], writes=["sqj", ("ss", tt)])
                            A("act", lambda e, tt=tt: e.activation(out=rstd[:, tt:tt + 1], in_=ss[:, tt:tt + 1], func=AF.Ln, scale=1.0 / D, bias=EPS),
                              reads=[("ss", tt)], writes=[("rstd", tt)])
                            A("act", lambda e, tt=tt: e.activation(out=rstd[:, tt:tt + 1], in_=rstd[:, tt:tt + 1], func=AF.Exp, scale=-0.5), reads=[("rstd", tt)], writes=[("rstd", tt)])
                            A("dve", lambda e, xb=xb, tt=tt: e.scalar_tensor_tensor(out=xb[:], in0=xb[:], scalar=rstd[:, tt:tt + 1], in1=gbf[:],
                                                                                      op0=ALU.mult, op1=ALU.mult),
                              reads=[xk, ("rstd", tt), "gbf"], writes=[xk])
                        A("pool", lambda e, xb=xb, r0=r0: e.dma_start(out=xdst[r0:r0 + 128, :], in_=xb[:]), reads=[xk], writes=["xdst"], dma=True)
                S.emit()

        pre_es = None
        pre_w = None
        for l in layers:
            if "m1" in phases:
                phase_m1(l, pre_w=pre_w)
            if pre_es is not None:
                pre_es.close()
                pre_es, pre_w = None, None
            if "att" in phases:
                phase_att(l)
            if "s5" in phases:
                phase_s5(l)
            if "m3a" in phases:
                phase_mf1(l) if USE_MF else phase_m3a(l)
            if "m3b" in phases:
                if USE_MF:
                    pf = None
                    if l == 0 and 1 in layers and "m1" in phases:
                        pre_es = ExitStack()
                        uniq[0] += 1
                        pre_w = pre_es.enter_context(nc.sbuf_tensor("wpre_%d" % uniq[0], [128, 8, NIN], BF16))
                        pf = (pre_w, w_in[1])
                    phase_mf2(l, last=(l == 1), prefetch=pf)
                else:
                    phase_m3b(l, last=(l == 1))
    return nc


_PERM = np.concatenate([np.arange(32, 64), np.arange(0, 32)])


def _prep_inputs(inputs):
    w_in = np.asarray(inputs["w_in"], dtype=np.float32)
    q = w_in[:, :, 1536:1792].reshape(2, D, 4, 64)[:, :, :, _PERM].reshape(2, D, 256)
    k = w_in[:, :, 1792:2048].reshape(2, D, 4, 64)[:, :, :, _PERM].reshape(2, D, 256)
    w_in_p = np.ascontiguousarray(np.concatenate([w_in, q, k], axis=2))
    shared = {n: np.ascontiguousarray(np.asarray(v, dtype=np.float32)) for n, v in inputs.items() if n not in ("x", "w_in")}
    shared["w_in"] = w_in_p
    return shared


_NC_CACHE = {}


def kernel(**inputs):
    x = np.ascontiguousarray(np.asarray(inputs["x"], dtype=np.float32))
    shared = _prep_inputs(inputs)
    if "full" not in _NC_CACHE:
        _NC_CACHE["full"] = build_program()
    nc = _NC_CACHE["full"]
    xs = x.reshape(N_CORES, 2 * SEQ, D)
    in_maps = [dict(shared, x=np.ascontiguousarray(xs[i])) for i in range(N_CORES)]
    res = run_bass_kernel_spmd(nc, in_maps, core_ids=list(range(N_CORES)))
    outs = [np.asarray(res.results[i]["out"], dtype=np.float32).reshape(2, SEQ, D) for i in range(N_CORES)]
    return np.concatenate(outs, axis=0)
```

```python
import math
import numpy as np
import concourse.bass as bass
import concourse.mybir as mybir
from concourse.bass_utils import run_bass_kernel_spmd
from contextlib import ExitStack

F32 = mybir.dt.float32
BF16 = mybir.dt.bfloat16
I32 = mybir.dt.int32
ALU = mybir.AluOpType
AF = mybir.ActivationFunctionType

N_CORES = 8
D = 1024
SEQ = 4096
NB = 512
DFF = 2816
NIN = 2816
EPS = 1e-6
TWO_PI = 2.0 * math.pi
N_DMA_SEMS = 24
USE_MF = True
PATTERNS = ((1, 32), (4, 8), (16, 2))


class Sched:
    ENGS = ("pe", "act", "dve", "pool", "sp")

    def __init__(self, nc, es):
        self.nc = nc
        self.ops = []
        self.last_writer = {}
        self.readers = {}
        self.n_dma = {"sp": 0, "pool": 0, "act": 0}
        self.cnt = {e: 0 for e in self.ENGS}
        self.esem = {e: es.enter_context(nc.semaphore("s_" + e)) for e in ("pe", "act", "dve", "pool")}
        self.dsem = {"sp": [es.enter_context(nc.semaphore("d%d" % j)) for j in range(N_DMA_SEMS)],
                     "pool": [es.enter_context(nc.semaphore("dp%d" % j)) for j in range(8)],
                     "act": [es.enter_context(nc.semaphore("da%d" % j)) for j in range(8)]}
        self.seen = {e: {} for e in self.ENGS}

    def add(self, eng, fn, reads=(), writes=(), dma=False):
        i = len(self.ops)
        deps = set()
        for r in reads:
            w = self.last_writer.get(r)
            if w is not None:
                deps.add(w)
        for w_ in writes:
            w = self.last_writer.get(w_)
            if w is not None:
                deps.add(w)
            for rd in self.readers.get(w_, ()):
                deps.add(rd)
        deps.discard(i)
        op = dict(eng=eng, fn=fn, deps=deps, dma=dma, has_dep=False)
        if dma:
            op["dma_idx"] = self.n_dma[eng]
            self.n_dma[eng] += 1
        self.ops.append(op)
        for r in reads:
            self.readers.setdefault(r, []).append(i)
        for w_ in writes:
            self.last_writer[w_] = i
            self.readers[w_] = []
        return i

    def emit(self):
        nc = self.nc
        ops = self.ops
        for op in ops:
            if op["eng"] == "pe" and not op["dma"]:
                op["deps"] = {d for d in op["deps"] if not (ops[d]["eng"] == "pe" and not ops[d]["dma"])}
            for d in op["deps"]:
                ops[d]["has_dep"] = True
        cnt = self.cnt
        for op in ops:
            if not op["dma"] and op["has_dep"]:
                cnt[op["eng"]] += 1
                op["ms"] = cnt[op["eng"]]
        esem, dsem = self.esem, self.dsem
        per_eng = {e: [op for op in ops if op["eng"] == e] for e in self.ENGS}
        final_dma = {}

        def dsem_of(p):
            pool_ = dsem[p["eng"]]
            K = len(pool_)
            j = p["dma_idx"]
            return (p["eng"], j % K), pool_[j % K], 16 * (j // K + 1), K

        for op in ops:
            if op["dma"]:
                key, sem, val, K = dsem_of(op)
                final_dma[key] = (sem, val)

        def run(engname, eng):
            seen = self.seen[engname]

            def wait(sem, key, val):
                if seen.get(key, 0) >= val:
                    return
                seen[key] = val
                eng.wait_ge(sem, val)

            for op in per_eng[engname]:
                need = {}
                for d in op["deps"]:
                    p = ops[d]
                    if p["dma"]:
                        key, sem, val, K = dsem_of(p)
                    else:
                        key, sem, val = p["eng"], esem[p["eng"]], p["ms"]
                    if key not in need or need[key][1] < val:
                        need[key] = (sem, val)
                for key in sorted(need, key=str):
                    wait(need[key][0], key, need[key][1])
                if op["dma"]:
                    key, sem, val, K = dsem_of(op)
                    if val > 16:
                        wait(sem, key, val - 16)
                    ins = op["fn"](eng)
                    ins.then_inc(sem, 16)
                else:
                    ins = op["fn"](eng)
                    if op["has_dep"]:
                        ins.then_inc(esem[engname], 1)
            if engname == "sp":
                for key in sorted(final_dma, key=str):
                    wait(final_dma[key][0], key, final_dma[key][1])

        with nc.Block() as block:
            @block.sync
            def _(e):
                run("sp", e)

            @block.tensor
            def _(e):
                run("pe", e)

            @block.scalar
            def _(e):
                run("act", e)

            @block.vector
            def _(e):
                run("dve", e)

            @block.gpsimd
            def _(e):
                run("pool", e)
        self.ops = []
        self.last_writer = {}
        self.readers = {}
        nc.all_engine_barrier()


def build_program(n_seq=2, layers=(0, 1), phases=("m1", "att", "s5", "m3a", "m3b"), debug=False):
    NT = n_seq * SEQ
    NBLK = NT // NB
    nc = bass.Bass("TRN2", target_bir_lowering=False)

    def din(name, shape):
        return nc.dram_tensor(name, shape, F32, kind="ExternalInput").ap()

    x_in = din("x", [NT, D])
    norm_mix_g = din("norm_mix_g", [2, D])
    w_in = din("w_in", [2, D, NIN])
    conv3_w = din("conv3_w", [2, 3, 256])
    cfm_dw_w = din("cfm_dw_w", [2, 31, 256])
    cfm_dw_b = din("cfm_dw_b", [2, 256])
    cfm_ln_g = din("cfm_ln_g", [2, 256])
    cfm_ln_b = din("cfm_ln_b", [2, 256])
    s5_a_re = din("s5_a_re", [2, 16, 64])
    s5_a_im = din("s5_a_im", [2, 16, 64])
    s5_log_dt = din("s5_log_dt", [2, 16])
    s5_b_re = din("s5_b_re", [2, 16, 64, 16])
    s5_b_im = din("s5_b_im", [2, 16, 64, 16])
    s5_c_re = din("s5_c_re", [2, 16, 16, 64])
    s5_c_im = din("s5_c_im", [2, 16, 16, 64])
    s5_d = din("s5_d", [2, 256])
    s5_glu_w = din("s5_glu_w", [2, 256, 512])
    grp_norm_g = din("grp_norm_g", [2, D])
    w_out = din("w_out", [2, D, D])
    norm_ffn_g = din("norm_ffn_g", [2, D])
    w_gate = din("w_gate", [2, D, DFF])
    w_up = din("w_up", [2, D, DFF])
    w_down = din("w_down", [2, DFF, D])
    final_norm_g = din("final_norm_g", [D])
    out = nc.dram_tensor("out", [NT, D], F32, kind="ExternalOutput").ap()

    skind = "ExternalOutput" if debug else "Internal"
    xs1 = nc.dram_tensor("xs1", [NT, D], F32, kind=skind).ap()
    xmid = nc.dram_tensor("xmid", [NT, D], F32, kind=skind).ap()
    ymix = nc.dram_tensor("ymix", [8, 128, NT], F32, kind=skind).ap()
    ubd = nc.dram_tensor("ubd", [2, 128, NT], BF16, kind=skind).ap()
    qTd = nc.dram_tensor("qTd", [2, 128, NT], BF16, kind=skind).ap()
    kTd = nc.dram_tensor("kTd", [2, 128, NT], BF16, kind=skind).ap()
    vTd = nc.dram_tensor("vTd", [2, 128, NT], BF16, kind=skind).ap()
    h2Td = nc.dram_tensor("h2Td", [8, 128, NT], BF16, kind=skind).ap()

    with ExitStack() as es0:
        es0.enter_context(nc.allow_non_contiguous_dma(reason="small parameter layouts"))
        S = Sched(nc, es0)
        A = S.add

        uniq = [0]

        def mk(es):
            def sb(name, shape, dt):
                uniq[0] += 1
                return es.enter_context(nc.sbuf_tensor("%s_%d" % (name, uniq[0]), shape, dt))

            def ps(name, shape, dt):
                uniq[0] += 1
                return es.enter_context(nc.psum_tensor("%s_%d" % (name, uniq[0]), shape, dt))
            return sb, ps

        def wkeys(dst_key, kt_n, col):
            return [(dst_key, col // 2048, k0) for k0 in range(0, kt_n, 4)]

        def load_w_bf16(dst, dst_key, src_ap, kt_n, ncols):
            v = src_ap.rearrange("(kt p) n -> p kt n", p=128)
            step = 2048
            for c0 in range(0, ncols, step):
                c1 = min(ncols, c0 + step)
                for k0 in range(0, kt_n, 4):
                    k1 = min(kt_n, k0 + 4)
                    A("pool", lambda e, c0=c0, c1=c1, k0=k0, k1=k1: e.dma_start(out=dst[:, k0:k1, c0:c1], in_=v[:, k0:k1, c0:c1]),
                      writes=[(dst_key, c0 // step, k0)], dma=True)

        def make_ident(sb, pfx):
            idi = sb(pfx + "idi", [128, 128], I32)
            identb = sb(pfx + "identb", [128, 128], BF16)
            identf = sb(pfx + "identf", [128, 128], F32)
            A("pool", lambda e: e.iota(idi[:], pattern=[[1, 128]], base=0, channel_multiplier=-1), writes=["idi"])
            A("dve", lambda e: e.tensor_scalar(out=identb[:], in0=idi[:], scalar1=0.0, scalar2=None, op0=ALU.is_equal),
              reads=["idi"], writes=["identb"])
            A("dve", lambda e: e.tensor_scalar(out=identf[:], in0=idi[:], scalar1=0.0, scalar2=None, op0=ALU.is_equal),
              reads=["idi"], writes=["identf"])
            return idi, identb, identf

        def range_reduce(ang, tmpf, tmpi, key, tkey):
            A("dve", lambda e: e.tensor_scalar(out=tmpf, in0=ang, scalar1=1.0 / TWO_PI, scalar2=None, op0=ALU.mult),
              reads=[key], writes=[tkey])
            A("dve", lambda e: e.tensor_copy(out=tmpi, in_=tmpf), reads=[tkey], writes=[tkey + "i"])
            A("dve", lambda e: e.tensor_copy(out=tmpf, in_=tmpi), reads=[tkey + "i"], writes=[tkey])
            A("dve", lambda e: e.scalar_tensor_tensor(out=ang, in0=tmpf, scalar=-TWO_PI, in1=ang, op0=ALU.mult, op1=ALU.add),
              reads=[tkey, key], writes=[key])
            A("dve", lambda e: e.tensor_scalar(out=ang, in0=ang, scalar1=math.pi, scalar2=-math.pi, op0=ALU.min, op1=ALU.max),
              reads=[key], writes=[key])

        def phase_m1(l, pre_w=None):
            xsrc = x_in if l == 0 else xs1
            with ExitStack() as es:
                sb, ps = mk(es)
                idi, identb, identf = make_ident(sb, "m1")
                wsb = pre_w if pre_w is not None else sb("m1w", [128, 8, NIN], BF16)
                gb = sb("m1gb", [128, D], F32)
                A("act", lambda e: e.dma_start(out=gb[:], in_=norm_mix_g[l].partition_broadcast(128)), writes=["gb"], dma=True)
                w3 = sb("m1w3", [128, 2, 3], F32)
                wdw = sb("m1wdw", [128, 2, 31], F32)
                dwb = sb("m1dwb", [128, 2], F32)
                lng = sb("m1lng", [128, 2], F32)
                lnb = sb("m1lnb", [128, 2], F32)


                def load_params():
                    prow = tf[:, 0:256]
                    A("act", lambda e: e.dma_start(out=prow[0:31, :], in_=cfm_dw_w[l]), writes=["tf"], dma=True)
                    A("act", lambda e: e.dma_start(out=prow[31:34, :], in_=conv3_w[l]), writes=["tf"], dma=True)
                    for j, src in enumerate((cfm_dw_b, cfm_ln_g, cfm_ln_b)):
                        A("act", lambda e, j=j, src=src: e.dma_start(out=prow[34 + j:35 + j, :], in_=src[l:l + 1, :]), writes=["tf"], dma=True)
                    for t in range(2):
                        A("pe", lambda e, t=t: e.transpose(out=pin[3][:, t * 64:t * 64 + 37], in_=prow[0:37, t * 128:(t + 1) * 128], identity=identf[0:37, 0:37]),
                          reads=["tf", "identf"], writes=[("pin", 3)])
                    for t in range(2):
                        A("dve", lambda e, t=t: e.tensor_copy(out=wdw[:, t, :], in_=pin[3][:, t * 64:t * 64 + 31]), reads=[("pin", 3)], writes=["wdw"])
                        A("dve", lambda e, t=t: e.tensor_copy(out=w3[:, t, :], in_=pin[3][:, t * 64 + 31:t * 64 + 34]), reads=[("pin", 3)], writes=["w3"])
                        A("dve", lambda e, t=t: e.tensor_copy(out=dwb[:, t:t + 1], in_=pin[3][:, t * 64 + 34:t * 64 + 35]), reads=[("pin", 3)], writes=["dwb"])
                        A("dve", lambda e, t=t: e.tensor_copy(out=lng[:, t:t + 1], in_=pin[3][:, t * 64 + 35:t * 64 + 36]), reads=[("pin", 3)], writes=["lng"])
                        A("dve", lambda e, t=t: e.tensor_copy(out=lnb[:, t:t + 1], in_=pin[3][:, t * 64 + 36:t * 64 + 37]), reads=[("pin", 3)], writes=["lnb"])

                diag = sb("m1diag", [128, 2, 31, 128], BF16)
                ones256 = sb("m1ones", [128, 128], BF16)
                COS = sb("m1cos", [128, SEQ], F32)
                SIN = sb("m1sin", [128, SEQ], F32)
                tf = sb("m1tf", [128, NB], F32)
                ti = sb("m1ti", [128, NB], I32)
                pidx = sb("m1pidx", [128, 1], I32)
                pj = sb("m1pj", [128, 1], I32)
                pf = sb("m1pf", [128, 1], F32)
                inv = sb("m1inv", [128, 1], F32)
                sgn = sb("m1sgn", [128, 1], F32)
                xt = [sb("m1xt%d" % i, [128, D], F32) for i in range(2)]
                sqj = sb("m1sqj", [128, D], BF16)
                ss = sb("m1ss", [128, 4], F32)
                rstd = sb("m1rstd", [128, 4], F32)
                hb = [sb("m1h%d" % i, [128, D], BF16) for i in range(8)]
                hT = [sb("m1hT%d" % i, [128, 8, NB], BF16) for i in range(2)]
                ch_sb = sb("m1ch", [128, 2, NB], F32)
                zbuf = sb("m1z", [128, 2, NB + 2], F32)
                acc = sb("m1acc", [128, 2, NB], F32)
                ycv = sb("m1ycv", [128, 2, NB], F32)
                sg_sb = sb("m1sg", [128, 2, NB], F32)
                zcb = sb("m1zcb", [128, 2, NB + 30], BF16)
                cf = sb("m1cf", [128, 2, NB], F32)
                cfb = sb("m1cfb", [128, 2, 2, NB], BF16)
                mean_sb = sb("m1mean", [128, NB], F32)
                var_sb = sb("m1var", [128, NB], F32)
                rs_sb = sb("m1rs", [128, NB], F32)
                tmpc = sb("m1tmpc", [128, 2, NB], F32)
                ycf = sb("m1ycf", [128, 2, NB], F32)
                ub_sb = sb("m1ub", [128, 2, NB], BF16)
                v_sb = sb("m1v", [128, 2, NB], BF16)
                t1 = sb("m1t1", [128, 4, NB], F32)
                t2 = sb("m1t2", [128, 2, NB], F32)
                qk_sb = sb("m1qk", [128, 4, NB], BF16)
                pt = [ps("m1pt%d" % i, [128, 8, 128], BF16) for i in range(2)]
                pin = [ps("m1pin%d" % i, [128, NB], F32) for i in range(4)]
                pc = [ps("m1pc%d" % i, [128, NB], F32) for i in range(2)]

                pin_ctr = [0]

                def inproj(m, b):
                    i = pin_ctr[0] % 4
                    pin_ctr[0] += 1
                    for kt in range(8):
                        A("pe", lambda e, kt=kt, i=i: e.matmul(pin[i][:], lhsT=wsb[:, kt, m * 128:(m + 1) * 128], rhs=hT[b % 2][:, kt, :],
                                                                start=(kt == 0), stop=(kt == 7)),
                          reads=[("w", min(2, (m * 128) // 1024), kt_) for kt_ in range(8)] + [("hT", b % 2)], writes=[("pin", i)])
                    return i

                def stage_N(b):
                    tok0 = b * NB
                    for tt in range(4):
                        xb = xt[tt % 2]
                        xk = ("xt", tt % 2)
                        r0 = tok0 + tt * 128
                        A("sp", lambda e, xb=xb, r0=r0: e.dma_start(out=xb[:], in_=xsrc[r0:r0 + 128, :]), writes=[xk], dma=True)
                        A("act", lambda e, xb=xb, tt=tt: e.activation(out=sqj[:], in_=xb[:], func=AF.Square, accum_out=ss[:, tt:tt + 1]),
                          reads=[xk], writes=["sqj", "ss"])
                        A("act", lambda e, tt=tt: e.activation(out=rstd[:, tt:tt + 1], in_=ss[:, tt:tt + 1], func=AF.Ln, scale=1.0 / D, bias=EPS),
                          reads=["ss"], writes=["rstd"])
                        A("act", lambda e, tt=tt: e.activation(out=rstd[:, tt:tt + 1], in_=rstd[:, tt:tt + 1], func=AF.Exp, scale=-0.5), reads=["rstd"], writes=["rstd"])
                        A("dve", lambda e, xb=xb, tt=tt: e.scalar_tensor_tensor(out=hb[(b % 2) * 4 + tt][:], in0=xb[:], scalar=rstd[:, tt:tt + 1], in1=gb[:],
                                                                                  op0=ALU.mult, op1=ALU.mult),
                          reads=[xk, "rstd", "gb"], writes=[("hb", b % 2, tt)])

                def stage_T(b):
                    for tt in range(4):
                        for kt in range(8):
                            A("pe", lambda e, kt=kt, tt=tt: e.transpose(out=pt[tt % 2][:, kt, :], in_=hb[(b % 2) * 4 + tt][:, kt * 128:(kt + 1) * 128],
                                                                          identity=identb[:]),
                              reads=[("hb", b % 2, tt), "identb"], writes=[("pt", tt % 2)])
                        A("act", lambda e, tt=tt, b=b: e.copy(out=hT[b % 2][:, :, tt * 128:(tt + 1) * 128], in_=pt[tt % 2][:]),
                          reads=[("pt", tt % 2)], writes=[("hT", b % 2)])

                def stage_B1(b):
                    tok0 = b * NB
                    bs = b % (SEQ // NB)
                    if bs == 0:
                        A("pool", lambda e: e.memset(zbuf[:, :, 0:2], 0.0), writes=["zbuf"])
                        A("pool", lambda e: e.memset(zcb[:, :, 0:30], 0.0), writes=[("zcb", 0), ("zcb", 1)])
                    for t in range(2):
                        i = inproj(0 + t, b)
                        A("act", lambda e, i=i, t=t: e.copy(out=ch_sb[:, t, :], in_=pin[i][:]), reads=[("pin", i)], writes=["ch"])
                    for t in range(2):
                        i = inproj(4 + t, b)
                        A("dve", lambda e, i=i, t=t: e.tensor_tensor(out=zbuf[:, t, 2:NB + 2], in0=pin[i][:], in1=ch_sb[:, t, :], op=ALU.mult),
                          reads=[("pin", i), "ch"], writes=["zbuf"])
                        A("dve", lambda e, t=t: e.tensor_scalar(out=acc[:, t, :], in0=zbuf[:, t, 0:NB], scalar1=w3[:, t, 0:1], scalar2=None, op0=ALU.mult),
                          reads=["zbuf", "w3"], writes=["acc"])
                        for k in (1, 2):
                            A("dve", lambda e, t=t, k=k: e.scalar_tensor_tensor(out=acc[:, t, :], in0=zbuf[:, t, k:NB + k], scalar=w3[:, t, k:k + 1],
                                                                                 in1=acc[:, t, :], op0=ALU.mult, op1=ALU.add),
                              reads=["zbuf", "w3", "acc"], writes=["acc"])
                        A("pool", lambda e, t=t: e.tensor_copy(out=zbuf[:, t, 0:2], in_=zbuf[:, t, NB:NB + 2]), reads=["zbuf"], writes=["zbuf"])
                    for t in range(2):
                        i = inproj(2 + t, b)
                        A("dve", lambda e, i=i, t=t: e.tensor_tensor(out=ycv[:, t, :], in0=pin[i][:], in1=acc[:, t, :], op=ALU.mult),
                          reads=[("pin", i), "acc"], writes=["ycv"])
                    A("pool", lambda e, tok0=tok0: e.dma_start(out=ymix[0:2, :, tok0:tok0 + NB].rearrange("t p n -> p t n"), in_=ycv[:]),
                      reads=["ycv"], writes=["ymix"], dma=True)

                def cfm_a(b):
                    for t in range(2):
                        i = inproj(8 + t, b)
                        A("act", lambda e, i=i, t=t: e.activation(out=sg_sb[:, t, :], in_=pin[i][:], func=AF.Sigmoid),
                          reads=[("pin", i)], writes=[("sg", t)])
                    for t in range(2):
                        i = inproj(6 + t, b)
                        A("dve", lambda e, i=i, t=t: e.tensor_tensor(out=zcb[:, t, 30:NB + 30], in0=pin[i][:], in1=sg_sb[:, t, :], op=ALU.mult),
                          reads=[("pin", i), ("sg", t)], writes=[("zcb", t)])

                def cfm_conv(b):
                    for t in range(2):
                        for k in range(31):
                            A("pe", lambda e, t=t, k=k: e.matmul(pc[t][:], lhsT=diag[:, t, k, :], rhs=zcb[:, t, k:k + NB],
                                                                  start=(k == 0), stop=(k == 30)),
                              reads=["diag", ("zcb", t)], writes=[("pc", t)])
                        A("pool", lambda e, t=t: e.tensor_copy(out=zcb[:, t, 0:30], in_=zcb[:, t, NB:NB + 30]), reads=[("zcb", t)], writes=[("zcb", t)])
                        A("act", lambda e, t=t: e.activation(out=cf[:, t, :], in_=pc[t][:], func=AF.Identity, bias=dwb[:, t:t + 1]),
                          reads=[("pc", t), "dwb"], writes=[("cf", t)])
                        A("act", lambda e, t=t: e.copy(out=cfb[:, 0, t, :], in_=cf[:, t, :]), reads=[("cf", t)], writes=[("cfb", 0, t)])
                        A("act", lambda e, t=t: e.activation(out=cfb[:, 1, t, :], in_=cf[:, t, :], func=AF.Square), reads=[("cf", t)], writes=[("cfb", 1, t)])

                def cfm_stats(b):
                    tok0 = b * NB
                    for j in range(2):
                        for t in range(2):
                            A("pe", lambda e, j=j, t=t: e.matmul(pc[j][:], lhsT=ones256[:], rhs=cfb[:, j, t, :], start=(t == 0), stop=(t == 1)),
                              reads=["ones256", ("cfb", j, t)], writes=[("pc", j)])
                    A("act", lambda e: e.copy(out=mean_sb[:], in_=pc[0][:]), reads=[("pc", 0)], writes=["mean"])
                    A("dve", lambda e: e.tensor_tensor(out=var_sb[:], in0=mean_sb[:], in1=mean_sb[:], op=ALU.mult), reads=["mean"], writes=["var"])
                    A("dve", lambda e: e.tensor_tensor(out=var_sb[:], in0=pc[1][:], in1=var_sb[:], op=ALU.subtract),
                      reads=[("pc", 1), "var"], writes=["var"])
                    A("act", lambda e: e.activation(out=rs_sb[:], in_=var_sb[:], func=AF.Ln, bias=EPS), reads=["var"], writes=["rs"])
                    A("act", lambda e: e.activation(out=rs_sb[:], in_=rs_sb[:], func=AF.Exp, scale=-0.5), reads=["rs"], writes=["rs"])
                    for t in range(2):
                        A("dve", lambda e, t=t: e.tensor_tensor(out=tmpc[:, t, :], in0=cf[:, t, :], in1=mean_sb[:], op=ALU.subtract),
                          reads=[("cf", t), "mean"], writes=[("tmpc", t)])
                        A("dve", lambda e, t=t: e.tensor_tensor(out=tmpc[:, t, :], in0=tmpc[:, t, :], in1=rs_sb[:], op=ALU.mult), reads=[("tmpc", t), "rs"], writes=[("tmpc", t)])
                        A("act", lambda e, t=t: e.activation(out=ycf[:, t, :], in_=tmpc[:, t, :], func=AF.Silu, scale=lng[:, t:t + 1], bias=lnb[:, t:t + 1]),
                          reads=[("tmpc", t), "lng", "lnb"], writes=["ycf"])
                    A("pool", lambda e, tok0=tok0: e.dma_start(out=ymix[2:4, :, tok0:tok0 + NB].rearrange("t p n -> p t n"), in_=ycf[:]),
                      reads=["ycf"], writes=["ymix"], dma=True)

                def ssm_u(b):
                    tok0 = b * NB
                    for t in range(2):
                        i = inproj(10 + t, b)
                        A("act", lambda e, i=i, t=t: e.copy(out=ub_sb[:, t, :], in_=pin[i][:]), reads=[("pin", i)], writes=["ub"])
                    A("pool", lambda e, tok0=tok0: e.dma_start(out=ubd[:, :, tok0:tok0 + NB].rearrange("t p n -> p t n"), in_=ub_sb[:]),
                      reads=["ub"], writes=["ubd"], dma=True)

                def rope(b, t4s, dstd):
                    tok0 = b * NB
                    bs = b % (SEQ // NB)
                    pos = slice(bs * NB, (bs + 1) * NB)
                    for t4 in t4s:
                        i = inproj(12 + t4, b)
                        A("dve", lambda e, i=i, t4=t4: e.tensor_tensor(out=t1[:, t4, :], in0=pin[i][:], in1=COS[:, pos], op=ALU.mult),
                          reads=[("pin", i), "COS"], writes=[("t1", t4)])
                        i = inproj(18 + t4, b)
                        A("dve", lambda e, i=i, t4=t4: e.tensor_tensor(out=t2[:, t4 % 2, :], in0=pin[i][:], in1=SIN[:, pos], op=ALU.mult),
                          reads=[("pin", i), "SIN"], writes=[("t2", t4 % 2)])
                        A("pool", lambda e, t4=t4: e.tensor_tensor(out=qk_sb[:, t4, :], in0=t1[:, t4, :], in1=t2[:, t4 % 2, :], op=ALU.add),
                          reads=[("t1", t4), ("t2", t4 % 2)], writes=[("qk", t4 // 2)])
                    g2 = t4s[0] // 2
                    A("pool", lambda e: e.dma_start(out=dstd[:, :, tok0:tok0 + NB].rearrange("t p n -> p t n"), in_=qk_sb[:, 2 * g2:2 * g2 + 2, :]),
                      reads=[("qk", g2)], writes=["qkTd"], dma=True)

                def vproj(b):
                    tok0 = b * NB
                    for t in range(2):
                        i = inproj(16 + t, b)
                        A("act", lambda e, i=i, t=t: e.copy(out=v_sb[:, t, :], in_=pin[i][:]), reads=[("pin", i)], writes=["v"])
                    A("pool", lambda e, tok0=tok0: e.dma_start(out=vTd[:, :, tok0:tok0 + NB].rearrange("t p n -> p t n"), in_=v_sb[:]),
                      reads=["v"], writes=["vTd"], dma=True)

                def diag_build():
                    for t in range(2):
                        for k in range(31):
                            A("act", lambda e, t=t, k=k: e.activation(out=diag[:, t, k, :], in_=identf[:], func=AF.Copy, scale=wdw[:, t, k:k + 1]),
                              reads=["identf", "wdw"], writes=["diag"])
                    A("pool", lambda e: e.memset(ones256[:], 1.0 / 256), writes=["ones256"])

                def late_setup():
                    A("pool", lambda e: e.iota(pidx[:], pattern=[[0, 1]], base=0, channel_multiplier=1), writes=["pidx"])
                    A("dve", lambda e: e.tensor_scalar(out=pj[:], in0=pidx[:], scalar1=31, scalar2=None, op0=ALU.bitwise_and),
                      reads=["pidx"], writes=["pj"])
                    A("dve", lambda e: e.tensor_copy(out=pf[:], in_=pj[:]), reads=["pj"], writes=["pf"])
                    A("act", lambda e: e.activation(out=inv[:], in_=pf[:], func=AF.Exp, scale=-math.log(10000.0) / 32.0),
                      reads=["pf"], writes=["inv"])
                    A("dve", lambda e: e.tensor_scalar(out=pj[:], in0=pidx[:], scalar1=32, scalar2=None, op0=ALU.bitwise_and),
                      reads=["pidx", "pf"], writes=["pj"])
                    A("dve", lambda e: e.tensor_copy(out=pf[:], in_=pj[:]), reads=["pj", "inv"], writes=["pf"])
                    A("dve", lambda e: e.tensor_scalar(out=sgn[:], in0=pf[:], scalar1=1.0 / 16, scalar2=-1.0, op0=ALU.mult, op1=ALU.add),
                      reads=["pf"], writes=["sgn"])
                    for c in range(SEQ // NB):
                        cs = slice(c * NB, (c + 1) * NB)
                        A("pool", lambda e, c=c: e.iota(ti[:], pattern=[[1, NB]], base=c * NB, channel_multiplier=0),
                          reads=["tf"], writes=["tfi"])
                        A("dve", lambda e, cs=cs: e.tensor_copy(out=SIN[:, cs], in_=ti[:]), reads=["tfi"], writes=["SIN"])
                        A("dve", lambda e, cs=cs: e.tensor_scalar(out=SIN[:, cs], in0=SIN[:, cs], scalar1=inv[:, 0:1], scalar2=None, op0=ALU.mult),
                          reads=["SIN", "inv"], writes=["SIN"])
                        A("dve", lambda e, cs=cs: e.tensor_scalar(out=COS[:, cs], in0=SIN[:, cs], scalar1=math.pi / 2, scalar2=None, op0=ALU.add),
                          reads=["SIN"], writes=["COS"])
                        range_reduce(SIN[:, cs], tf[:], ti[:], "SIN", "tf")
                        range_reduce(COS[:, cs], tf[:], ti[:], "COS", "tf")
                        A("act", lambda e, cs=cs: e.activation(out=SIN[:, cs], in_=SIN[:, cs], func=AF.Sin), reads=["SIN"], writes=["SIN"])
                        A("act", lambda e, cs=cs: e.activation(out=COS[:, cs], in_=COS[:, cs], func=AF.Sin), reads=["COS"], writes=["COS"])
                        A("dve", lambda e, cs=cs: e.tensor_scalar(out=SIN[:, cs], in0=SIN[:, cs], scalar1=sgn[:, 0:1], scalar2=None, op0=ALU.mult),
                          reads=["SIN", "sgn"], writes=["SIN"])


                def load_weights():
                    stg = [(t1[:, 0:2, :].rearrange("p a b -> p (a b)"), [("t1", 0), ("t1", 1)]),
                           (t1[:, 2:4, :].rearrange("p a b -> p (a b)"), [("t1", 2), ("t1", 3)]),
                           (qk_sb[:].rearrange("p a b -> p (a b)").bitcast(F32), [("qk", 0), ("qk", 1)]),
                           (ycf[:].rearrange("p a b -> p (a b)"), ["ycf"])]
                    wv = w_in[l].rearrange("(kt p) n -> p kt n", p=128)
                    ns = 0
                    for cc, (c0, c1) in enumerate(((0, 1024), (1024, 2048), (2048, NIN))):
                        for kt in range(8):
                            sbuf_, skeys = stg[ns % 4]
                            ns += 1
                            A("sp", lambda e, sbuf_=sbuf_, kt=kt, c0=c0, c1=c1: e.dma_start(out=sbuf_[:, 0:c1 - c0], in_=wv[:, kt, c0:c1]), writes=skeys, dma=True)
                            A("act" if ns % 2 else "dve", lambda e, sbuf_=sbuf_, kt=kt, c0=c0, c1=c1, ns=ns: (e.copy if ns % 2 else e.tensor_copy)(out=wsb[:, kt, c0:c1], in_=sbuf_[:, 0:c1 - c0]),
                              reads=skeys, writes=[("w", cc, kt)])

                stage_N(0)
                load_params()
                stage_T(0)
                if pre_w is None:
                    load_weights()
                diag_build()
                if NBLK > 1:
                    stage_N(1)
                late_setup()
                for b in range(NBLK):
                    stage_B1(b)
                    cfm_a(b)
                    ssm_u(b)
                    cfm_conv(b)
                    rope(b, (0, 1), qTd)
                    cfm_stats(b)
                    rope(b, (2, 3), kTd)
                    if b + 1 < NBLK:
                        stage_T(b + 1)
                    vproj(b)
                    if b + 2 < NBLK:
                        stage_N(b + 2)
                S.emit()

        def phase_att(l):
            with ExitStack() as es:
                sb, ps = mk(es)
                idi, identb, identf = make_ident(sb, "at")
                mb = sb("atmb", [128, 256], BF16)
                A("dve", lambda e: e.tensor_scalar(out=mb[:, 0:128], in0=idi[:], scalar1=0.0, scalar2=None, op0=ALU.is_ge),
                  reads=["idi"], writes=["mb"])
                A("dve", lambda e: e.tensor_scalar(out=mb[:, 128:256], in0=idi[:], scalar1=0.0, scalar2=None, op0=ALU.is_le),
                  reads=["idi"], writes=["mb"])
                sel = sb("atsel", [65, 64], F32)
                A("pool", lambda e: e.memset(sel[:], 0.0), writes=["sel"])
                A("pool", lambda e: e.memset(sel[64:65, :], 1.0), writes=["sel"])
                qT = sb("atq", [128, 2, SEQ], BF16)
                kT = sb("atk", [128, 2, SEQ], BF16)
                vT = sb("atv", [128, 2, SEQ], BF16)
                vext2 = [sb("atvext%d" % i, [128, 32, 65], BF16) for i in range(2)]
                for i in range(2):
                    A("pool", lambda e, i=i: e.memset(vext2[i][:, :, 64:65], 1.0), writes=[("vext", i)])
                accb2 = [sb("atacc%d" % i, [65, SEQ], F32) for i in range(2)]
                yo = sb("atyo", [64, SEQ], F32)
                rb = sb("atrb", [64, NB], F32)
                P = [sb("atP%d" % i, [128, 256], BF16) for i in range(5)]
                pB = ps("atpB", [128, NB], F32)
                pv0 = pB[:].bitcast(BF16).rearrange("p (a b) -> p a b", b=64)
                pS = [ps("atpS%d" % i, [128, 512], F32) for i in range(3)]
                pO = [ps("atpO%d" % i, [128, 512], F32) for i in range(4)]
                kb_ctr = [0]
                for s in range(n_seq):
                    t0 = s * SEQ
                    for dst, src, key in ((qT, qTd, "qT"), (kT, kTd, "kT"), (vT, vTd, "vT")):
                        A("sp", lambda e, dst=dst, src=src, t0=t0: e.dma_start(out=dst[:], in_=src[:, :, t0:t0 + SEQ].rearrange("t p n -> p t n")),
                          writes=[key], dma=True)
                    for ht in range(2):
                        for pi, (d, nb) in enumerate(PATTERNS):
                            qv = qT[:, ht, :].rearrange("p (n r) -> p r n", r=d)
                            kv = kT[:, ht, :].rearrange("p (n r) -> p r n", r=d)
                            vv = vT[:, ht, :].rearrange("p (n r) -> p r n", r=d)
                            avs = [accb2[hh][:].rearrange("p (n r) -> p r n", r=d) for hh in range(2)]
                            for hh in range(2):
                                po = 64 * hh
                                for g8 in range(4):
                                    for j8 in range(8):
                                        bi = g8 * 8 + j8
                                        r, kb = bi // nb, bi % nb
                                        A("pe", lambda e, j8=j8, r=r, kb=kb, vv=vv, po=po: e.transpose(
                                            out=pv0[:, j8, :], in_=vv[po:po + 64, r, kb * 128:(kb + 1) * 128], identity=identb[po:po + 64, po:po + 64]),
                                          reads=["vT", "identb"], writes=[("pv", 0)])
                                    A("act", lambda e, g8=g8, hh=hh: e.copy(out=vext2[hh][:, g8 * 8:(g8 + 1) * 8, 0:64], in_=pv0[:, 0:8, :]),
                                      reads=[("pv", 0)], writes=[("vext", hh)])
                            steps = [(hh, r, kb) for r in range(d) for kb in range(nb) for hh in range(2)]
                            cbase = kb_ctr[0]
                            kb_ctr[0] += len(steps)

                            def emit_qk(si, steps=steps, cbase=cbase, qv=qv, kv=kv, nb=nb):
                                hh, r, kb = steps[si]
                                po = 64 * hh
                                c = cbase + si
                                nq = 256 if kb + 1 < nb else 128
                                pSc = pS[c % 3]
                                A("pe", lambda e: e.matmul(pSc[:, 0:nq], lhsT=kv[po:po + 64, r, kb * 128:(kb + 1) * 128],
                                                           rhs=qv[po:po + 64, r, kb * 128:kb * 128 + nq], start=True, stop=True),
                                  reads=["kT", "qT"], writes=[("pS", c % 3)])

                            def emit_rest(si, steps=steps, cbase=cbase, avs=avs, nb=nb, pi=pi):
                                hh, r, kb = steps[si]
                                bi = r * nb + kb
                                c = cbase + si
                                nq = 256 if kb + 1 < nb else 128
                                pSc, Pc = pS[c % 3], P[c % 5]
                                vx = vext2[hh]
                                A("act", lambda e: e.activation(out=Pc[:, 0:nq], in_=pSc[:, 0:nq], func=AF.Exp, scale=0.125),
                                  reads=[("pS", c % 3)], writes=[("P", c % 5)])
                                A("pool", lambda e: e.tensor_tensor(out=Pc[:, 0:nq], in0=Pc[:, 0:nq], in1=mb[:, 0:nq], op=ALU.mult),
                                  reads=[("P", c % 5), "mb"], writes=[("P", c % 5)])
                                for half in range(nq // 128):
                                    qb = kb + half
                                    slot = 2 * hh + qb % 2
                                    first = (half == 1) or (kb == 0)
                                    A("pe", lambda e, half=half, slot=slot, first=first: e.matmul(
                                        pO[slot][0:65, 0:128], lhsT=vx[:, bi, :], rhs=Pc[:, half * 128:(half + 1) * 128],
                                        start=first, stop=(half == 0), skip_group_check=True),
                                      reads=[("vext", hh), ("P", c % 5)], writes=[("pO", slot)])
                                    if half == 0:
                                        dst = avs[hh][0:65, r, qb * 128:(qb + 1) * 128]
                                        if pi == 0:
                                            A("dve", lambda e, dst=dst, slot=slot: e.tensor_copy(out=dst, in_=pO[slot][0:65, 0:128]),
                                              reads=[("pO", slot)], writes=[("acc", hh)])
                                        else:
                                            A("dve", lambda e, dst=dst, slot=slot: e.tensor_tensor(out=dst, in0=pO[slot][0:65, 0:128], in1=dst, op=ALU.add),
                                              reads=[("pO", slot), ("acc", hh)], writes=[("acc", hh)])

                            for si in range(min(2, len(steps))):
                                emit_qk(si)
                            for si in range(len(steps)):
                                if si + 2 < len(steps):
                                    emit_qk(si + 2)
                                emit_rest(si)
                        for hh in range(2):
                            accb = accb2[hh]
                            po = 64 * hh
                            for c in range(SEQ // NB):
                                cs = slice(c * NB, (c + 1) * NB)
                                A("pe", lambda e, cs=cs, accb=accb: e.matmul(pB[0:64, :], lhsT=sel[:], rhs=accb[0:65, cs], start=True, stop=True),
                                  reads=["sel", ("acc", hh)], writes=[("pv", 0)])
                                A("dve", lambda e: e.reciprocal(out=rb[:], in_=pB[0:64, :]), reads=[("pv", 0)], writes=["rb"])
                                A("dve", lambda e, cs=cs, accb=accb: e.tensor_tensor(out=yo[:, cs], in0=accb[0:64, cs], in1=rb[:], op=ALU.mult),
                                  reads=[("acc", hh), "rb"], writes=["yo"])
                            A("pool", lambda e, ht=ht, po=po, t0=t0: e.dma_start(out=ymix[6 + ht, po:po + 64, t0:t0 + SEQ], in_=yo[:]),
                              reads=["yo"], writes=["ymix"], dma=True)
                S.emit()

        def phase_s5(l):
            with ExitStack() as eso:
                sbo, pso_ = mk(eso)
                Wst = sbo("s5W", [128, 16, 2, 8, 128], BF16)
                Vst = sbo("s5V", [128, 16, 2, 8, 128], BF16)
                BD = sbo("s5BD", [128, 16, 2, 128], BF16)
                Msr = sbo("s5Msr", [128, 8, 8], F32)
                Msi = sbo("s5Msi", [128, 8, 8], F32)
                nMsi = sbo("s5nMsi", [128, 8, 8], F32)
                wglu = sbo("s5wglu", [128, 2, 512], BF16)
                with ExitStack() as es:
                    sb, ps = mk(es)
                    idi, identb, identf = make_ident(sb, "s5")
                    load_w_bf16(wglu, "wglu", s5_glu_w[l], 2, 512)
                    are = sb("s5are", [128, 8], F32)
                    aim = sb("s5aim", [128, 8], F32)
                    ldt = sb("s5ldt", [128, 8], F32)
                    Bre = sb("s5Bre", [128, 8, 16], F32)
                    Bim = sb("s5Bim", [128, 8, 16], F32)
                    Cld = [sb("s5Cld%d" % i, [128, 128], F32) for i in range(2)]
                    CT = [sb("s5CT%d" % i, [128, 8, 16], F32) for i in range(2)]
                    Dsk = sb("s5D", [128, 2], F32)
                    A("sp", lambda e: e.dma_start(out=are[:], in_=s5_a_re[l].rearrange("(gt gl) p -> (gl p) gt", gl=2)), writes=["are"], dma=True)
                    A("sp", lambda e: e.dma_start(out=aim[:], in_=s5_a_im[l].rearrange("(gt gl) p -> (gl p) gt", gl=2)), writes=["aim"], dma=True)
                    for gl in range(2):
                        A("sp", lambda e, gl=gl: e.dma_start(out=ldt[gl * 64:(gl + 1) * 64, :],
                                                            in_=s5_log_dt[l].rearrange("(gt gl) -> gl gt", gl=2)[gl].partition_broadcast(64)),
                          writes=["ldt"], dma=True)
                    A("sp", lambda e: e.dma_start(out=Bre[:], in_=s5_b_re[l].rearrange("(gt gl) p h -> (gl p) gt h", gl=2)), writes=["Bre"], dma=True)
                    A("sp", lambda e: e.dma_start(out=Bim[:], in_=s5_b_im[l].rearrange("(gt gl) p h -> (gl p) gt h", gl=2)), writes=["Bim"], dma=True)
                    for ri, src in enumerate((s5_c_re, s5_c_im)):
                        for gt in range(8):
                            A("sp", lambda e, ri=ri, src=src, gt=gt: e.dma_start(
                                out=Cld[ri][gt * 16:(gt + 1) * 16, :].rearrange("h (gl p) -> h gl p", gl=2),
                                in_=src[l, 2 * gt:2 * gt + 2].rearrange("gl h p -> h gl p")),
                              writes=[("Cld", ri)], dma=True)
                    A("sp", lambda e: e.dma_start(out=Dsk[:], in_=s5_d[l].rearrange("(t p) -> p t", p=128)), writes=["Dsk"], dma=True)
                    pct = ps("s5pct", [128, 4, 128], F32)
                    for ri in range(2):
                        A("pe", lambda e, ri=ri: e.transpose(out=pct[:, ri, :], in_=Cld[ri][:], identity=identf[:]),
                          reads=[("Cld", ri), "identf"], writes=["pct"])
                        A("act", lambda e, ri=ri: e.copy(out=CT[ri][:].rearrange("p a b -> p (a b)"), in_=pct[:, ri, :]), reads=["pct"], writes=[("CT", ri)])

                    def sm(name):
                        return sb("s5_" + name, [128, 8], F32)
                    dt_, adr, mag, th, th2, sn, cs_, lr, li, den, t8, rden, nr, fr, fi = [sm(n) for n in
                        ("dt", "adr", "mag", "th", "th2", "sn", "cs", "lr", "li", "den", "t8", "rden", "nr", "fr", "fi")]
                    tf8 = sm("tf8")
                    ti8 = sb("s5_ti8", [128, 8], I32)

                    def TT(out, a, b, op, eng="dve", r=(), w=()):
                        A(eng, lambda e: e.tensor_tensor(out=out, in0=a, in1=b, op=op), reads=r, writes=w)

                    A("act", lambda e: e.activation(out=dt_[:], in_=ldt[:], func=AF.Exp), reads=["ldt"], writes=["dt"])
                    TT(adr[:], are[:], dt_[:], ALU.mult, r=["are", "dt"], w=["adr"])
                    A("act", lambda e: e.activation(out=mag[:], in_=adr[:], func=AF.Exp), reads=["adr"], writes=["mag"])
                    TT(th[:], aim[:], dt_[:], ALU.mult, r=["aim", "dt"], w=["th"])
                    A("dve", lambda e: e.tensor_scalar(out=th2[:], in0=th[:], scalar1=math.pi / 2, scalar2=None, op0=ALU.add), reads=["th"], writes=["th2"])
                    range_reduce(th[:], tf8[:], ti8[:], "th", "tf8")
                    range_reduce(th2[:], tf8[:], ti8[:], "th2", "tf8")
                    A("act", lambda e: e.activation(out=sn[:], in_=th[:], func=AF.Sin), reads=["th"], writes=["sn"])
                    A("act", lambda e: e.activation(out=cs_[:], in_=th2[:], func=AF.Sin), reads=["th2"], writes=["cs"])
                    TT(lr[:], mag[:], cs_[:], ALU.mult, r=["mag", "cs"], w=["lr"])
                    TT(li[:], mag[:], sn[:], ALU.mult, r=["mag", "sn"], w=["li"])
                    TT(den[:], are[:], are[:], ALU.mult, r=["are"], w=["den"])
                    TT(t8[:], aim[:], aim[:], ALU.mult, r=["aim"], w=["t8"])
                    TT(den[:], den[:], t8[:], ALU.add, r=["den", "t8"], w=["den"])
                    A("dve", lambda e: e.reciprocal(out=rden[:], in_=den[:]), reads=["den"], writes=["rden"])
                    A("dve", lambda e: e.tensor_scalar(out=nr[:], in0=lr[:], scalar1=-1.0, scalar2=None, op0=ALU.add), reads=["lr"], writes=["nr"])
                    TT(fr[:], nr[:], are[:], ALU.mult, r=["nr", "are"], w=["fr"])
                    TT(t8[:], li[:], aim[:], ALU.mult, r=["li", "aim", "den"], w=["t8"])
                    TT(fr[:], fr[:], t8[:], ALU.add, r=["fr", "t8"], w=["fr"])
                    TT(fr[:], fr[:], rden[:], ALU.mult, r=["fr", "rden"], w=["fr"])
                    TT(fi[:], li[:], are[:], ALU.mult, r=["li", "are"], w=["fi"])
                    TT(t8[:], nr[:], aim[:], ALU.mult, r=["nr", "aim", "fr"], w=["t8"])
                    TT(fi[:], fi[:], t8[:], ALU.subtract, r=["fi", "t8"], w=["fi"])
                    TT(fi[:], fi[:], rden[:], ALU.mult, r=["fi", "rden"], w=["fi"])
                    Bbr = sb("s5Bbr", [128, 8, 16], F32)
                    Bbi = sb("s5Bbi", [128, 8, 16], F32)
                    tb = sb("s5tb", [128, 8, 16], F32)
                    frb = fr[:].unsqueeze(2).to_broadcast([128, 8, 16])
                    fib = fi[:].unsqueeze(2).to_broadcast([128, 8, 16])
                    TT(Bbr[:], Bre[:], frb, ALU.mult, r=["Bre", "fr"], w=["Bbr"])
                    TT(tb[:], Bim[:], fib, ALU.mult, r=["Bim", "fi"], w=["tb"])
                    TT(Bbr[:], Bbr[:], tb[:], ALU.subtract, r=["Bbr", "tb"], w=["Bbr"])
                    TT(Bbi[:], Bim[:], frb, ALU.mult, r=["Bim", "fr"], w=["Bbi"])
                    TT(tb[:], Bre[:], fib, ALU.mult, r=["Bre", "fi", "Bbr"], w=["tb"])
                    TT(Bbi[:], Bbi[:], tb[:], ALU.add, r=["Bbi", "tb"], w=["Bbi"])
                    Lr = sb("s5Lr", [128, 17, 8], F32)
                    Li = sb("s5Li", [128, 17, 8], F32)
                    tl = sb("s5tl", [128, 8, 8], F32)
                    A("pool", lambda e: e.memset(Lr[:, 0, :], 1.0), writes=["L"])
                    A("pool", lambda e: e.memset(Li[:, 0, :], 0.0), writes=["L"])
                    A("dve", lambda e: e.tensor_copy(out=Lr[:, 1, :], in_=lr[:]), reads=["lr", "L"], writes=["L"])
                    A("dve", lambda e: e.tensor_copy(out=Li[:, 1, :], in_=li[:]), reads=["li", "L"], writes=["L"])
                    n = 1
                    while n < 16:
                        src_r, src_i = Lr[:, 1:n + 1, :], Li[:, 1:n + 1, :]
                        mr = Lr[:, n:n + 1, :].to_broadcast([128, n, 8])
                        mi = Li[:, n:n + 1, :].to_broadcast([128, n, 8])
                        dr, di = Lr[:, n + 1:2 * n + 1, :], Li[:, n + 1:2 * n + 1, :]
                        tln = tl[:, 0:n, :]
                        TT(dr, src_r, mr, ALU.mult, r=["L"], w=["L"])
                        TT(tln, src_i, mi, ALU.mult, r=["L"], w=["tl"])
                        TT(dr, dr, tln, ALU.subtract, r=["L", "tl"], w=["L"])
                        TT(di, src_r, mi, ALU.mult, r=["L"], w=["L"])
                        TT(tln, src_i, mr, ALU.mult, r=["L"], w=["tl"])
                        TT(di, di, tln, ALU.add, r=["L", "tl"], w=["L"])
                        n *= 2
                    A("dve", lambda e: e.tensor_copy(out=Msr[:, 0, :], in_=Lr[:, 16, :]), reads=["L"], writes=["Ms"])
                    A("dve", lambda e: e.tensor_copy(out=Msi[:, 0, :], in_=Li[:, 16, :]), reads=["L"], writes=["Ms"])
                    for s_ in range(7):
                        TT(Msr[:, s_ + 1, :], Msr[:, s_, :], Msr[:, s_, :], ALU.mult, r=["Ms"], w=["Ms"])
                        TT(t8[:], Msi[:, s_, :], Msi[:, s_, :], ALU.mult, r=["Ms"], w=["t8"])
                        TT(Msr[:, s_ + 1, :], Msr[:, s_ + 1, :], t8[:], ALU.subtract, r=["Ms", "t8"], w=["Ms"])
                        TT(t8[:], Msr[:, s_, :], Msi[:, s_, :], ALU.mult, r=["Ms"], w=["t8"])
                        A("dve", lambda e, s_=s_: e.tensor_scalar(out=Msi[:, s_ + 1, :], in0=t8[:], scalar1=2.0, scalar2=None, op0=ALU.mult),
                          reads=["t8"], writes=["Ms"])
                    A("dve", lambda e: e.tensor_scalar(out=nMsi[:], in0=Msi[:], scalar1=-1.0, scalar2=None, op0=ALU.mult), reads=["Ms"], writes=["nMs"])
                    LB = [sb("s5LB%d" % i, [128, 16, 8, 16], F32) for i in range(2)]
                    CL = [sb("s5CL%d" % i, [128, 16, 8, 16], F32) for i in range(2)]
                    scr = sb("s5scr", [128, 2048], F32)
                    tq = scr[:].rearrange("p (a b c) -> p a b c", a=16, b=8, c=16)
                    shp = [128, 16, 8, 16]
                    Lr0 = Lr[:, 0:16, :].unsqueeze(3).to_broadcast(shp)
                    Li0 = Li[:, 0:16, :].unsqueeze(3).to_broadcast(shp)
                    Lr1 = Lr[:, 1:17, :].unsqueeze(3).to_broadcast(shp)
                    Li1 = Li[:, 1:17, :].unsqueeze(3).to_broadcast(shp)
                    Bbrb = Bbr[:].unsqueeze(1).to_broadcast(shp)
                    Bbib = Bbi[:].unsqueeze(1).to_broadcast(shp)
                    CTrb = CT[0][:].unsqueeze(1).to_broadcast(shp)
                    CTib = CT[1][:].unsqueeze(1).to_broadcast(shp)
                    TT(LB[0][:], Lr0, Bbrb, ALU.mult, r=["L", "Bbr"], w=["LB0"])
                    TT(tq, Li0, Bbib, ALU.mult, r=["L", "Bbi"], w=["tq"])
                    TT(LB[0][:], LB[0][:], tq, ALU.subtract, r=["LB0", "tq"], w=["LB0"])
                    TT(LB[1][:], Lr0, Bbib, ALU.mult, r=["L", "Bbi"], w=["LB1"])
                    TT(tq, Li0, Bbrb, ALU.mult, r=["L", "Bbr", "LB0"], w=["tq"])
                    TT(LB[1][:], LB[1][:], tq, ALU.add, r=["LB1", "tq"], w=["LB1"])
                    TT(CL[0][:], Lr1, CTrb, ALU.mult, r=["L", ("CT", 0), "LB1"], w=["CL0"])
                    TT(tq, Li1, CTib, ALU.mult, r=["L", ("CT", 1), "LB1"], w=["tq"])
                    TT(CL[0][:], CL[0][:], tq, ALU.subtract, r=["CL0", "tq"], w=["CL0"])
                    TT(CL[1][:], Li1, CTrb, ALU.mult, r=["L", ("CT", 0)], w=["CL1"])
                    TT(tq, Lr1, CTib, ALU.mult, r=["L", ("CT", 1), "CL0"], w=["tq"])
                    TT(CL[1][:], CL[1][:], tq, ALU.add, r=["CL1", "tq"], w=["CL1"])
                    A("dve", lambda e: e.tensor_scalar(out=CL[1][:], in0=CL[1][:], scalar1=-1.0, scalar2=None, op0=ALU.mult), reads=["CL1"], writes=["CL1"])
                    Mi = sb("s5Mi", [128, 8, 8], I32)
                    Mk = sb("s5Mk", [128, 8, 8], F32)
                    for ct in range(2):
                        for gl in range(2):
                            A("pool", lambda e, ct=ct, gl=gl: e.iota(Mi[gl * 64:(gl + 1) * 64, 4 * ct:4 * ct + 4, :], pattern=[[-2, 4], [1, 8]],
                                                                    base=-gl, channel_multiplier=0), writes=["Mi"])
                    A("dve", lambda e: e.tensor_scalar(out=Mk[:], in0=Mi[:], scalar1=0.0, scalar2=None, op0=ALU.is_equal), reads=["Mi"], writes=["Mk"])
                    shp4 = [128, 8, 8, 16]
                    Mkb = Mk[:].unsqueeze(3).to_broadcast(shp4)
                    Cexp = sb("s5Cexp", [128, 2, 8, 128], F32)
                    for ri in range(2):
                        TT(Cexp[:, ri].rearrange("p g (q h) -> p g q h", h=16), CT[ri][:].unsqueeze(2).to_broadcast(shp4), Mkb, ALU.mult,
                           r=[("CT", ri), "Mk"], w=["Cexp"])
                    A("dve", lambda e: e.tensor_scalar(out=Cexp[:, 1], in0=Cexp[:, 1], scalar1=-1.0, scalar2=None, op0=ALU.mult), reads=["Cexp"], writes=["Cexp"])
                    for i in range(16):
                        for ri in range(2):
                            TT(Vst[:, i, ri].rearrange("p g (q h) -> p g q h", h=16), CL[ri][:, i].unsqueeze(2).to_broadcast(shp4), Mkb, ALU.mult,
                               eng=("pool" if (2 * i + ri) % 3 else "dve"), r=["CL%d" % ri, "Mk"], w=[("Vst", i, ri)])
                    Aexp0 = sb("s5Aexp0", [128, 2, 8, 128], F32)
                    Aexp = [Aexp0[:], scr[:].rearrange("p (r g n) -> p r g n", r=2, g=8, n=128)]
                    pbd = [ps("s5pbd%d" % i, [128, 512], F32) for i in range(2)]
                    pw = [ps("s5pw%d" % i, [128, 4, 128], F32) for i in range(2)]
                    nw = 0
                    for k in range(16):
                        Ak = Aexp[k % 2]
                        akey = ("Aexp", k % 2)
                        for ri in range(2):
                            TT(Ak[:, ri].rearrange("p g (q h) -> p g q h", h=16), LB[ri][:, k].unsqueeze(2).to_broadcast(shp4), Mkb, ALU.mult,
                               r=["LB%d" % ri, "Mk"], w=[akey, "tq"])
                        for ct in range(2):
                            pb = pbd[(2 * k + ct) % 2]
                            pkey = ("pbd", (2 * k + ct) % 2)
                            n_ = 0
                            for q in range(4):
                                for ri in range(2):
                                    gt = 4 * ct + q
                                    A("pe", lambda e, pb=pb, Ak=Ak, ri=ri, gt=gt, n_=n_: e.matmul(pb[:, 0:128], lhsT=Ak[:, ri, gt, :], rhs=Cexp[:, ri, gt, :],
                                                                                             start=(n_ == 0), stop=(n_ == 7)),
                                      reads=[akey, "Cexp"], writes=[pkey])
                                    n_ += 1
                            if k == 0:
                                A("dve", lambda e, pb=pb, ct=ct: e.scalar_tensor_tensor(out=BD[:, 0, ct, :], in0=identf[:], scalar=Dsk[:, ct:ct + 1],
                                                                                         in1=pb[:, 0:128], op0=ALU.mult, op1=ALU.add),
                                  reads=[pkey, "identf", "Dsk"], writes=["BD"])
                            else:
                                A("act", lambda e, pb=pb, ct=ct, k=k: e.copy(out=BD[:, k, ct, :], in_=pb[:, 0:128]), reads=[pkey], writes=["BD"])
                        for ri in range(2):
                            for g4 in range(2):
                                pwc = pw[nw % 2]
                                wkey = ("pw", nw % 2)
                                nw += 1
                                for q in range(4):
                                    gt = 4 * g4 + q
                                    A("pe", lambda e, pwc=pwc, Ak=Ak, ri=ri, gt=gt, q=q: e.transpose(out=pwc[:, q, :], in_=Ak[:, ri, gt, :], identity=identf[:]),
                                      reads=[akey, "identf"], writes=[wkey])
                                A("act", lambda e, pwc=pwc, k=k, ri=ri, g4=g4: e.copy(out=Wst[:, k, ri, 4 * g4:4 * g4 + 4, :], in_=pwc[:]),
                                  reads=[wkey], writes=["Wst"])
                    S.emit()
                with ExitStack() as es:
                    sb, ps = mk(es)
                    ub = sb("s5ub", [128, 2, SEQ], BF16)
                    ubv = ub[:].rearrange("p t (j c) -> p t j c", j=16)
                    St = [sb("s5St%d" % i, [128, 8, 256], F32) for i in range(2)]
                    Alt = [sb("s5Alt%d" % i, [128, 256], F32) for i in range(4)]
                    Tmp = [sb("s5Tmp%d" % i, [128, 256], F32) for i in range(4)]
                    z = sb("s5z", [128, 2, SEQ], BF16)
                    zv = z[:].rearrange("p t (c j) -> p t j c", j=16)
                    Stb = [sb("s5Stb%d" % i, [128, 8, 256], BF16) for i in range(2)]
                    yss = sb("s5yss", [128, 2, NB], F32)
                    gtmp = [sb("s5gtmp%d" % i, [128, 256], F32) for i in range(1)] * 2
                    pst = [ps("s5pst%d" % i, [128, 512], F32) for i in range(2)]
                    pso = [ps("s5pso%d" % i, [128, 512], F32) for i in range(2)]
                    pg = [ps("s5pg%d" % i, [128, NB], F32) for i in range(4)]
                    n1 = 0
                    n2 = 0
                    for s in range(n_seq):
                        t0 = s * SEQ
                        A("sp", lambda e, t0=t0: e.dma_start(out=z[:], in_=ubd[:, :, t0:t0 + SEQ].rearrange("t p n -> p t n")), writes=["z"], dma=True)
                        A("act", lambda e: e.copy(out=ubv[:, 0], in_=z[:, 0, :].rearrange("p (c j) -> p j c", j=16)), reads=["z"], writes=["ub"])
                        A("dve", lambda e: e.tensor_copy(out=ubv[:, 1], in_=z[:, 1, :].rearrange("p (c j) -> p j c", j=16)), reads=["z"], writes=["ub"])
                        for gt in range(8):
                            ct = gt // 4
                            for ri in range(2):
                                pc_ = pst[n1 % 2]
                                pk = ("pst", n1 % 2)
                                n1 += 1
                                for j in range(16):
                                    A("pe", lambda e, pc_=pc_, j=j, ri=ri, gt=gt, ct=ct: e.matmul(pc_[:, 0:256], lhsT=Wst[:, 15 - j, ri, gt, :], rhs=ubv[:, ct, j, :],
                                                                                             start=(j == 0), stop=(j == 15)),
                                      reads=["ub"], writes=[pk])
                                A("act", lambda e, pc_=pc_, ri=ri, gt=gt: e.copy(out=St[ri][:, gt, :], in_=pc_[:, 0:256]), reads=[pk], writes=[("St", gt)])
                        def hs_step(gt, s_, ch):
                            d = 1 << s_
                            A0, A1, T0, T1 = Alt[2 * ch], Alt[2 * ch + 1], Tmp[2 * ch], Tmp[2 * ch + 1]
                            ak, t0k, t1k = ("Alt", ch), ("Tmp0", ch), ("Tmp1", ch)
                            if s_ % 2 == 0:
                                sr, si, dr, di = St[0][:, gt, :], St[1][:, gt, :], A0[:], A1[:]
                                skey, dkey = ("St", gt), ak
                            else:
                                sr, si, dr, di = A0[:], A1[:], St[0][:, gt, :], St[1][:, gt, :]
                                skey, dkey = ak, ("St", gt)
                            mr, mi, nmi = Msr[:, s_, gt:gt + 1], Msi[:, s_, gt:gt + 1], nMsi[:, s_, gt:gt + 1]

                            def STT(out, a, sc, b, r, w):
                                A("dve", lambda e: e.scalar_tensor_tensor(out=out, in0=a, scalar=sc, in1=b, op0=ALU.mult, op1=ALU.add), reads=r, writes=w)
                            STT(T0[:, d:256], sr[:, 0:256 - d], mr, sr[:, d:256], [skey], [t0k])
                            STT(T1[:, d:256], si[:, 0:256 - d], mr, si[:, d:256], [skey], [t1k])
                            STT(dr[:, d:256], si[:, 0:256 - d], nmi, T0[:, d:256], [skey, t0k], [dkey])
                            STT(di[:, d:256], sr[:, 0:256 - d], mi, T1[:, d:256], [skey, t1k], [dkey])
                            A("pool", lambda e: e.tensor_copy(out=dr[:, 0:d], in_=sr[:, 0:d]), reads=[skey], writes=[dkey])
                            A("pool", lambda e: e.tensor_copy(out=di[:, 0:d], in_=si[:, 0:d]), reads=[skey], writes=[dkey])

                        def hs_pair(gta, gtb):
                            for s_ in range(8):
                                hs_step(gta, s_, 0)
                                hs_step(gtb, s_, 1)
                            for gt in (gta, gtb):
                                for ri in range(2):
                                    A("act" if ri == 0 else "pool", lambda e, ri=ri, gt=gt: (e.copy if ri == 0 else e.tensor_copy)(out=Stb[ri][:, gt, :], in_=St[ri][:, gt, :]),
                                      reads=[("St", gt)], writes=[("Stb", gt)])

                        def y_tile(i, ct):
                            nonlocal n2
                            po_ = pso[n2 % 2]
                            ok = ("pso", n2 % 2)
                            n2 += 1
                            for j in range(i + 1):
                                A("pe", lambda e, po_=po_, i=i, j=j, ct=ct: e.matmul(po_[:, 0:256], lhsT=BD[:, i - j, ct, :], rhs=ubv[:, ct, j, :],
                                                                                start=(j == 0), stop=False, skip_group_check=True),
                                  reads=["ub"], writes=[ok])
                            n_ = 0
                            for q in range(4):
                                for ri in range(2):
                                    gt = 4 * ct + q
                                    A("pe", lambda e, po_=po_, i=i, ri=ri, gt=gt, n_=n_: e.matmul(po_[:, 1:256], lhsT=Vst[:, i, ri, gt, :], rhs=Stb[ri][:, gt, 0:255],
                                                                                             start=False, stop=(n_ == 7), skip_group_check=True),
                                      reads=[("Stb", gt)], writes=[ok])
                                    n_ += 1
                            gt_ = gtmp[n2 % 2]
                            gk = ("gtmp", 0)
                            A("act", lambda e, po_=po_, gt_=gt_: e.activation(out=gt_[:], in_=po_[:, 0:256], func=AF.Square), reads=[ok], writes=[gk])
                            A("dve", lambda e, gt_=gt_: e.tensor_scalar(out=gt_[:], in0=gt_[:], scalar1=0.044715, scalar2=1.0, op0=ALU.mult, op1=ALU.add),
                              reads=[gk], writes=[gk])
                            A("dve", lambda e, po_=po_, gt_=gt_: e.tensor_tensor(out=gt_[:], in0=po_[:, 0:256], in1=gt_[:], op=ALU.mult), reads=[gk, ok], writes=[gk])
                            A("act", lambda e, gt_=gt_: e.activation(out=gt_[:], in_=gt_[:], func=AF.Sigmoid, scale=1.5957691216057308), reads=[gk], writes=[gk])
                            A("dve", lambda e, po_=po_, gt_=gt_, i=i, ct=ct: e.tensor_tensor(out=zv[:, ct, i, :], in0=po_[:, 0:256], in1=gt_[:], op=ALU.mult),
                              reads=[gk, ok], writes=["z"])

                        hs_pair(0, 1)
                        hs_pair(2, 3)
                        for i in range(16):
                            y_tile(i, 0)
                            if i == 1:
                                hs_pair(4, 5)
                            if i == 8:
                                hs_pair(6, 7)
                        for i in range(16):
                            y_tile(i, 1)
                        for c in range(SEQ // NB):
                            cs = slice(c * NB, (c + 1) * NB)
                            for mt in range(4):
                                for kt in range(2):
                                    A("pe", lambda e, mt=mt, kt=kt, cs=cs: e.matmul(pg[mt][:], lhsT=wglu[:, kt, mt * 128:(mt + 1) * 128], rhs=z[:, kt, cs],
                                                                               start=(kt == 0), stop=(kt == 1)),
                                      reads=["z"], writes=[("pg", mt)])
                            for t in range(2):
                                A("act", lambda e, t=t: e.activation(out=yss[:, t, :], in_=pg[2 + t][:], func=AF.Sigmoid), reads=[("pg", 2 + t)], writes=["yss"])
                                A("dve", lambda e, t=t: e.tensor_tensor(out=yss[:, t, :], in0=pg[t][:], in1=yss[:, t, :], op=ALU.mult),
                                  reads=[("pg", t), "yss"], writes=["yss"])
                            A("pool", lambda e, t0=t0, c=c: e.dma_start(out=ymix[4:6, :, t0 + c * NB:t0 + (c + 1) * NB].rearrange("t p n -> p t n"), in_=yss[:]),
                              reads=["yss"], writes=["ymix"], dma=True)
                    S.emit()

        def phase_m3a(l):
            xsrc = x_in if l == 0 else xs1
            with ExitStack() as es:
                sb, ps = mk(es)
                idi, identb, identf = make_ident(sb, "m3")
                wo = sb("m3wo", [128, 8, D], BF16)
                load_w_bf16(wo, "wo", w_out[l], 8, D)
                gg = sb("m3gg", [128, 8], F32)
                A("sp", lambda e: e.dma_start(out=gg[:], in_=grp_norm_g[l].rearrange("(t p) -> p t", p=128)), writes=["gg"], dma=True)
                gb2 = sb("m3gb2", [128, D], F32)
                A("sp", lambda e: e.dma_start(out=gb2[:], in_=norm_ffn_g[l].partition_broadcast(128)), writes=["gb2"], dma=True)
                ones256 = sb("m3ones", [128, 128], BF16)
                A("pool", lambda e: e.memset(ones256[:], 1.0 / 256), writes=["ones256"])
                ym = [sb("m3ym%d" % i, [128, 8, NB], F32) for i in range(2)]
                sqb = sb("m3sqb", [128, 8, NB], BF16)
                rs = [sb("m3rs%d" % i, [128, NB], F32) for i in range(2)]
                mixT = [sb("m3mixT%d" % i, [128, 8, NB], BF16) for i in range(2)]
                xt = [sb("m3xt%d" % i, [128, D], F32) for i in range(8)]
                sqj = sb("m3sqj", [128, D], BF16)
                ss = sb("m3ss", [128, 4], F32)
                rstd = sb("m3rstd", [128, 4], F32)
                hb = [sb("m3h%d" % i, [128, D], BF16) for i in range(4)]
                h2T = [sb("m3h2T%d" % i, [128, 8, NB], BF16) for i in range(2)]
                pstat = [ps("m3pstat%d" % i, [128, NB], F32) for i in range(2)]
                po = [ps("m3po%d" % i, [128, NB], F32) for i in range(4)]
                pt = [ps("m3pt%d" % i, [128, 8, 128], BF16) for i in range(2)]

                def L1(b):
                    tok0 = b * NB
                    ymb = ym[b % 2]
                    yk = ("ym", b % 2)
                    A("sp", lambda e: e.dma_start(out=ymb[:], in_=ymix[:, :, tok0:tok0 + NB].rearrange("t p n -> p t n")),
                      writes=[yk], dma=True)
                    for tile in range(8):
                        A("act", lambda e, tile=tile: e.activation(out=sqb[:, tile, :], in_=ymb[:, tile, :], func=AF.Square),
                          reads=[yk], writes=[("sqb", tile)])

                def L2(b):
                    ymb = ym[b % 2]
                    yk = ("ym", b % 2)
                    mx = mixT[b % 2]
                    for grp in range(4):
                        for t in range(2):
                            tile = 2 * grp + t
                            A("pe", lambda e, grp=grp, tile=tile, t=t: e.matmul(pstat[grp % 2][:], lhsT=ones256[:], rhs=sqb[:, tile, :], start=(t == 0), stop=(t == 1)),
                              reads=[("sqb", tile), "ones256"], writes=[("pstat", grp % 2)])
                        rsb = rs[grp % 2]
                        rk = ("rs", grp % 2)
                        A("act", lambda e, grp=grp, rsb=rsb: e.activation(out=rsb[:], in_=pstat[grp % 2][:], func=AF.Ln, bias=EPS),
                          reads=[("pstat", grp % 2)], writes=[rk])
                        A("act", lambda e, rsb=rsb: e.activation(out=rsb[:], in_=rsb[:], func=AF.Exp, scale=-0.5), reads=[rk], writes=[rk])
                        for t in range(2):
                            tile = 2 * grp + t
                            A("dve", lambda e, tile=tile, rsb=rsb: e.scalar_tensor_tensor(out=mx[:, tile, :], in0=ymb[:, tile, :], scalar=gg[:, tile:tile + 1],
                                                                                      in1=rsb[:], op0=ALU.mult, op1=ALU.mult),
                              reads=[yk, rk, "gg"], writes=[("mixT", b % 2, tile)])

                def OPm(b, tt):
                    mx = mixT[b % 2]
                    xb = xt[(b % 2) * 4 + tt]
                    xk = ("xt", b % 2, tt)
                    r0 = b * NB + tt * 128
                    for nh in range(2):
                        pi_ = (2 * tt + nh) % 4
                        for kt in range(8):
                            A("pe", lambda e, pi_=pi_, kt=kt, nh=nh: e.matmul(po[pi_][:], lhsT=mx[:, kt, tt * 128:(tt + 1) * 128],
                                                                              rhs=wo[:, kt, nh * NB:(nh + 1) * NB], start=(kt == 0), stop=(kt == 7)),
                              reads=[("mixT", b % 2, kt)] + wkeys("wo", 8, 0), writes=[("po", pi_)])
                        A("dve", lambda e, pi_=pi_, nh=nh: e.tensor_tensor(out=xb[:, nh * NB:(nh + 1) * NB], in0=po[pi_][:], in1=xb[:, nh * NB:(nh + 1) * NB], op=ALU.add),
                          reads=[("po", pi_), xk], writes=[xk])
                    A("pool", lambda e: e.dma_start(out=xmid[r0:r0 + 128, :], in_=xb[:]), reads=[xk], writes=["xmid"], dma=True)
                    A("act", lambda e: e.activation(out=sqj[:], in_=xb[:], func=AF.Square, accum_out=ss[:, tt:tt + 1]),
                      reads=[xk], writes=["sqj", ("ss", tt)])
                    A("act", lambda e: e.activation(out=rstd[:, tt:tt + 1], in_=ss[:, tt:tt + 1], func=AF.Ln, scale=1.0 / D, bias=EPS),
                      reads=[("ss", tt)], writes=[("rstd", tt)])
                    A("act", lambda e: e.activation(out=rstd[:, tt:tt + 1], in_=rstd[:, tt:tt + 1], func=AF.Exp, scale=-0.5), reads=[("rstd", tt)], writes=[("rstd", tt)])
                    A("dve", lambda e: e.scalar_tensor_tensor(out=hb[tt][:], in0=xb[:], scalar=rstd[:, tt:tt + 1], in1=gb2[:],
                                                              op0=ALU.mult, op1=ALU.mult),
                      reads=[xk, ("rstd", tt), "gb2"], writes=[("hb", tt)])

                def LX(b):
                    for tt in range(4):
                        xb = xt[(b % 2) * 4 + tt]
                        r0 = b * NB + tt * 128
                        A("sp", lambda e, xb=xb, r0=r0: e.dma_start(out=xb[:], in_=xsrc[r0:r0 + 128, :]), writes=[("xt", b % 2, tt)], dma=True)

                def TR(b, tt):
                    for kt in range(8):
                        A("pe", lambda e, kt=kt: e.transpose(out=pt[tt % 2][:, kt, :], in_=hb[tt][:, kt * 128:(kt + 1) * 128], identity=identb[:]),
                          reads=[("hb", tt), "identb"], writes=[("pt", tt % 2)])
                    A("act", lambda e: e.copy(out=h2T[b % 2][:, :, tt * 128:(tt + 1) * 128], in_=pt[tt % 2][:]),
                      reads=[("pt", tt % 2)], writes=[("h2T", b % 2)])
                    if tt == 3:
                        tok0 = b * NB
                        A("pool", lambda e: e.dma_start(out=h2Td[:, :, tok0:tok0 + NB].rearrange("t p n -> p t n"), in_=h2T[b % 2][:]),
                          reads=[("h2T", b % 2)], writes=["h2Td"], dma=True)

                L1(0)
                LX(0)
                L2(0)
                for b in range(NBLK):
                    if b + 1 < NBLK:
                        L1(b + 1)
                        LX(b + 1)
                    OPm(b, 0)
                    OPm(b, 1)
                    TR(b, 0)
                    OPm(b, 2)
                    TR(b, 1)
                    if b + 1 < NBLK:
                        L2(b + 1)
                    OPm(b, 3)
                    TR(b, 2)
                    TR(b, 3)
                S.emit()

        def phase_m3b(l, last):
            with ExitStack() as es:
                sb, ps = mk(es)
                wg = sb("m4wg", [128, 8, DFF], BF16)
                wu = sb("m4wu", [128, 8, DFF], BF16)
                wd = sb("m4wd", [128, 22, D], BF16)
                load_w_bf16(wg, "wg", w_gate[l], 8, DFF)
                load_w_bf16(wu, "wu", w_up[l], 8, DFF)
                load_w_bf16(wd, "wd", w_down[l], 22, D)
                gbf = sb("m4gbf", [128, D], F32)
                if last:
                    A("sp", lambda e: e.dma_start(out=gbf[:], in_=final_norm_g.partition_broadcast(128)), writes=["gbf"], dma=True)
                h2T = [sb("m4h2T%d" % i, [128, 8, NB], BF16) for i in range(2)]
                ffT = sb("m4ffT", [128, 22, NB], BF16)
                sl = [sb("m4sl%d" % i, [128, NB], F32) for i in range(2)]
                xt = [sb("m4xt%d" % i, [128, D], F32) for i in range(4)]
                sqj = sb("m4sqj", [128, D], BF16)
                ss = sb("m4ss", [128, 4], F32)
                rstd = sb("m4rstd", [128, 4], F32)
                pgt = [ps("m4pg%d" % i, [128, NB], F32) for i in range(2)]
                put = [ps("m4pu%d" % i, [128, NB], F32) for i in range(2)]
                pd = [ps("m4pd%d" % i, [128, NB], F32) for i in range(4)]
                nx = 0
                xdst = out if last else xs1
                for b in range(NBLK):
                    tok0 = b * NB
                    hb = h2T[b % 2]
                    hk = ("h2T", b % 2)
                    A("sp", lambda e, hb=hb, tok0=tok0: e.dma_start(out=hb[:], in_=h2Td[:, :, tok0:tok0 + NB].rearrange("t p n -> p t n")), writes=[hk], dma=True)
                    for tt in range(4):
                        A("sp", lambda e, tt=tt, tok0=tok0: e.dma_start(out=xt[tt][:], in_=xmid[tok0 + tt * 128:tok0 + (tt + 1) * 128, :]), writes=[("xt", tt)], dma=True)
                    for ft in range(22):
                        gi = ft % 2
                        for kt in range(8):
                            A("pe", lambda e, gi=gi, kt=kt, ft=ft, hb=hb: e.matmul(pgt[gi][:], lhsT=wg[:, kt, ft * 128:(ft + 1) * 128], rhs=hb[:, kt, :],
                                                                              start=(kt == 0), stop=(kt == 7)),
                              reads=wkeys("wg", 8, ft * 128) + [hk], writes=[("pgt", gi)])
                        for kt in range(8):
                            A("pe", lambda e, gi=gi, kt=kt, ft=ft, hb=hb: e.matmul(put[gi][:], lhsT=wu[:, kt, ft * 128:(ft + 1) * 128], rhs=hb[:, kt, :],
                                                                              start=(kt == 0), stop=(kt == 7)),
                              reads=wkeys("wu", 8, ft * 128) + [hk], writes=[("put", gi)])
                        A("act", lambda e, gi=gi: e.activation(out=sl[gi][:], in_=pgt[gi][:], func=AF.Silu), reads=[("pgt", gi)], writes=[("sl", gi)])
                        A("dve", lambda e, gi=gi, ft=ft: e.tensor_tensor(out=ffT[:, ft, :], in0=put[gi][:], in1=sl[gi][:], op=ALU.mult),
                          reads=[("put", gi), ("sl", gi)], writes=[("ffT", ft)])
                    for tt in range(4):
                        xb = xt[tt]
                        xk = ("xt", tt)
                        r0 = tok0 + tt * 128
                        for nh in range(2):
                            pi_ = (2 * tt + nh) % 4
                            for ft in range(22):
                                A("pe", lambda e, pi_=pi_, ft=ft, tt=tt, nh=nh: e.matmul(pd[pi_][:], lhsT=ffT[:, ft, tt * 128:(tt + 1) * 128],
                                                                                       rhs=wd[:, ft, nh * NB:(nh + 1) * NB], start=(ft == 0), stop=(ft == 21)),
                                  reads=[("ffT", ft)] + wkeys("wd", 22, 0), writes=[("pd", pi_)])
                            A("dve", lambda e, pi_=pi_, xb=xb, nh=nh: e.tensor_tensor(out=xb[:, nh * NB:(nh + 1) * NB], in0=pd[pi_][:], in1=xb[:, nh * NB:(nh + 1) * NB], op=ALU.add),
                              reads=[("pd", pi_), xk], writes=[xk])
                        if last:
                            A("act", lambda e, xb=xb, tt=tt: e.activation(out=sqj[:], in_=xb[:], func=AF.Square, accum_out=ss[:, tt:tt + 1]),
                              reads=[xk], writes=["sqj", "ss"])
                            A("act", lambda e, tt=tt: e.activation(out=rstd[:, tt:tt + 1], in_=ss[:, tt:tt + 1], func=AF.Ln, scale=1.0 / D, bias=EPS),
                              reads=["ss"], writes=["rstd"])
                            A("act", lambda e, tt=tt: e.activation(out=rstd[:, tt:tt + 1], in_=rstd[:, tt:tt + 1], func=AF.Exp, scale=-0.5), reads=["rstd"], writes=["rstd"])
                            A("dve", lambda e, xb=xb, tt=tt: e.scalar_tensor_tensor(out=xb[:], in0=xb[:], scalar=rstd[:, tt:tt + 1], in1=gbf[:],
                                                                                      op0=ALU.mult, op1=ALU.mult),
                              reads=[xk, "rstd", "gbf"], writes=[xk])
                        A("pool", lambda e, xb=xb, r0=r0: e.dma_start(out=xdst[r0:r0 + 128, :], in_=xb[:]), reads=[xk], writes=["xdst"], dma=True)
                S.emit()

        HF = DFF // 2

        def phase_mf1(l):
            xsrc = x_in if l == 0 else xs1
            with ExitStack() as es:
                sb, ps = mk(es)
                idi, identb, identf = make_ident(sb, "f1")
                wo = sb("f1wo", [128, 8, D], BF16)
                load_w_bf16(wo, "wo", w_out[l], 8, D)
                gg = sb("f1gg", [128, 8], F32)
                ggrow = sb("f1ggrow", [8, 128], F32)
                A("act", lambda e: e.dma_start(out=ggrow[:], in_=grp_norm_g[l].rearrange("(t p) -> t p", p=128)), writes=["ggrow"], dma=True)
                gb2 = sb("f1gb2", [128, D], F32)
                A("act", lambda e: e.dma_start(out=gb2[:], in_=norm_ffn_g[l].partition_broadcast(128)), writes=["gb2"], dma=True)
                ones256 = sb("f1ones", [128, 128], BF16)
                A("pool", lambda e: e.memset(ones256[:], 1.0 / 256), writes=["ones256"])
                wg = sb("f1wg", [128, 8, HF], BF16)
                wu = sb("f1wu", [128, 8, HF], BF16)
                wd = sb("f1wd", [128, 11, D], BF16)
                load_w_bf16(wg, "wg", w_gate[l][:, 0:HF], 8, HF)
                load_w_bf16(wu, "wu", w_up[l][:, 0:HF], 8, HF)
                load_w_bf16(wd, "wd", w_down[l][0:HF, :], 11, D)
                ym = sb("f1ym", [128, 8, NB], F32)
                sqb = sb("f1sqb", [128, 8, NB], BF16)
                rs = [sb("f1rs%d" % i, [128, NB], F32) for i in range(2)]
                mixT = sb("f1mixT", [128, 8, NB], BF16)
                xt = [sb("f1xt%d" % i, [128, D], F32) for i in range(8)]
                sqj = sb("f1sqj", [128, D], BF16)
                ss = sb("f1ss", [128, 4], F32)
                rstd = sb("f1rstd", [128, 4], F32)
                hb = [sb("f1h%d" % i, [128, D], BF16) for i in range(4)]
                h2T = [sb("f1h2T%d" % i, [128, 8, NB], BF16) for i in range(2)]
                ffT = sb("f1ffT", [128, 11, NB], BF16)
                sl = sb("f1sl", [128, NB], F32)
                pstat = ps("f1pstat", [128, NB], F32)
                po = [ps("f1po%d" % i, [128, NB], F32) for i in range(2)]
                pt = ps("f1pt", [128, 8, 128], BF16)
                pgt2 = [ps("f1pg%d" % i, [128, NB], F32) for i in range(2)]
                put2 = [ps("f1pu%d" % i, [128, NB], F32) for i in range(2)]
                sl2 = [sl, sb("f1sl1", [128, NB], F32)]

                A("pe", lambda e: e.transpose(out=pstat[:, 0:8], in_=ggrow[:], identity=identf[0:8, 0:8]), reads=["ggrow", "identf"], writes=["pstat"])
                A("dve", lambda e: e.tensor_copy(out=gg[:], in_=pstat[:, 0:8]), reads=["pstat"], writes=["gg"])

                def L1(b):
                    tok0 = b * NB
                    A("sp", lambda e: e.dma_start(out=ym[:], in_=ymix[:, :, tok0:tok0 + NB].rearrange("t p n -> p t n")),
                      writes=["ym"], dma=True)
                    for tile in range(8):
                        A("act", lambda e, tile=tile: e.activation(out=sqb[:, tile, :], in_=ym[:, tile, :], func=AF.Square),
                          reads=["ym"], writes=[("sqb", tile)])

                def LX(b):
                    for tt in range(4):
                        xb = xt[(b % 2) * 4 + tt]
                        r0 = b * NB + tt * 128
                        A("sp", lambda e, xb=xb, r0=r0: e.dma_start(out=xb[:], in_=xsrc[r0:r0 + 128, :]), writes=[("xt", b % 2, tt)], dma=True)

                def L2(b):
                    for grp in range(4):
                        for t in range(2):
                            tile = 2 * grp + t
                            A("pe", lambda e, tile=tile, t=t: e.matmul(pstat[:], lhsT=ones256[:], rhs=sqb[:, tile, :], start=(t == 0), stop=(t == 1)),
                              reads=[("sqb", tile), "ones256"], writes=["pstat"])
                        rsb = rs[grp % 2]
                        rk = ("rs", grp % 2)
                        A("act", lambda e, rsb=rsb: e.activation(out=rsb[:], in_=pstat[:], func=AF.Ln, bias=EPS), reads=["pstat"], writes=[rk])
                        A("act", lambda e, rsb=rsb: e.activation(out=rsb[:], in_=rsb[:], func=AF.Exp, scale=-0.5), reads=[rk], writes=[rk])
                        for t in range(2):
                            tile = 2 * grp + t
                            A("dve", lambda e, tile=tile, rsb=rsb: e.scalar_tensor_tensor(out=mixT[:, tile, :], in0=ym[:, tile, :], scalar=gg[:, tile:tile + 1],
                                                                                      in1=rsb[:], op0=ALU.mult, op1=ALU.mult),
                              reads=["ym", rk, "gg"], writes=[("mixT", tile)])

                def OPm(b, tt):
                    xb = xt[(b % 2) * 4 + tt]
                    xk = ("xt", b % 2, tt)
                    for nh in range(2):
                        for kt in range(8):
                            A("pe", lambda e, kt=kt, nh=nh: e.matmul(po[nh][:], lhsT=mixT[:, kt, tt * 128:(tt + 1) * 128],
                                                                     rhs=wo[:, kt, nh * NB:(nh + 1) * NB], start=(kt == 0), stop=(kt == 7)),
                              reads=[("mixT", kt)] + wkeys("wo", 8, 0), writes=[("po", nh)])
                        A("dve", lambda e, nh=nh: e.tensor_tensor(out=xb[:, nh * NB:(nh + 1) * NB], in0=po[nh][:], in1=xb[:, nh * NB:(nh + 1) * NB], op=ALU.add),
                          reads=[("po", nh), xk], writes=[xk])
                    A("act", lambda e: e.activation(out=sqj[:], in_=xb[:], func=AF.Square, accum_out=ss[:, tt:tt + 1]),
                      reads=[xk], writes=["sqj", ("ss", tt)])
                    A("act", lambda e: e.activation(out=rstd[:, tt:tt + 1], in_=ss[:, tt:tt + 1], func=AF.Ln, scale=1.0 / D, bias=EPS),
                      reads=[("ss", tt)], writes=[("rstd", tt)])
                    A("act", lambda e: e.activation(out=rstd[:, tt:tt + 1], in_=rstd[:, tt:tt + 1], func=AF.Exp, scale=-0.5), reads=[("rstd", tt)], writes=[("rstd", tt)])
                    A("dve", lambda e: e.scalar_tensor_tensor(out=hb[tt][:], in0=xb[:], scalar=rstd[:, tt:tt + 1], in1=gb2[:],
                                                              op0=ALU.mult, op1=ALU.mult),
                      reads=[xk, ("rstd", tt), "gb2"], writes=[("hb", tt)])

                def TR(b, tt):
                    for kt in range(8):
                        A("pe", lambda e, kt=kt: e.transpose(out=pt[:, kt, :], in_=hb[tt][:, kt * 128:(kt + 1) * 128], identity=identb[:]),
                          reads=[("hb", tt), "identb"], writes=["pt"])
                    A("act", lambda e: e.copy(out=h2T[b % 2][:, :, tt * 128:(tt + 1) * 128], in_=pt[:]),
                      reads=["pt"], writes=[("h2T", b % 2)])
                    if tt == 3:
                        tok0 = b * NB
                        A("pool", lambda e: e.dma_start(out=h2Td[:, :, tok0:tok0 + NB].rearrange("t p n -> p t n"), in_=h2T[b % 2][:]),
                          reads=[("h2T", b % 2)], writes=["h2Td"], dma=True)

                def GU(b, ft):
                    hk = ("h2T", b % 2)
                    hsrc = h2T[b % 2]
                    gi = ft % 2
                    pgt, put, slb = pgt2[gi], put2[gi], sl2[gi]
                    for kt in range(8):
                        A("pe", lambda e, kt=kt: e.matmul(pgt[:], lhsT=wg[:, kt, ft * 128:(ft + 1) * 128], rhs=hsrc[:, kt, :], start=(kt == 0), stop=(kt == 7)),
                          reads=wkeys("wg", 8, ft * 128) + [hk], writes=[("pgt", gi)])
                    for kt in range(8):
                        A("pe", lambda e, kt=kt: e.matmul(put[:], lhsT=wu[:, kt, ft * 128:(ft + 1) * 128], rhs=hsrc[:, kt, :], start=(kt == 0), stop=(kt == 7)),
                          reads=wkeys("wu", 8, ft * 128) + [hk], writes=[("put", gi)])
                    A("act", lambda e: e.activation(out=slb[:], in_=pgt[:], func=AF.Silu), reads=[("pgt", gi)], writes=[("sl", gi)])
                    A("dve", lambda e: e.tensor_tensor(out=ffT[:, ft, :], in0=put[:], in1=slb[:], op=ALU.mult),
                      reads=[("put", gi), ("sl", gi)], writes=[("ffT", ft)])

                def DN(b, tt):
                    xb = xt[(b % 2) * 4 + tt]
                    xk = ("xt", b % 2, tt)
                    r0 = b * NB + tt * 128
                    for nh in range(2):
                        for ft in range(11):
                            A("pe", lambda e, ft=ft, nh=nh: e.matmul(po[nh][:], lhsT=ffT[:, ft, tt * 128:(tt + 1) * 128],
                                                                     rhs=wd[:, ft, nh * NB:(nh + 1) * NB], start=(ft == 0), stop=(ft == 10)),
                              reads=[("ffT", ft)] + wkeys("wd", 11, 0), writes=[("po", nh)])
                        A("dve", lambda e, nh=nh: e.tensor_tensor(out=xb[:, nh * NB:(nh + 1) * NB], in0=po[nh][:], in1=xb[:, nh * NB:(nh + 1) * NB], op=ALU.add),
                          reads=[("po", nh), xk], writes=[xk])
                    A("pool", lambda e: e.dma_start(out=xmid[r0:r0 + 128, :], in_=xb[:]), reads=[xk], writes=["xmid"], dma=True)

                def m3a_items(b):
                    return [lambda: L2(b), lambda: OPm(b, 0), lambda: OPm(b, 1), lambda: TR(b, 0), lambda: OPm(b, 2),
                            lambda: TR(b, 1), lambda: OPm(b, 3), lambda: TR(b, 2), lambda: TR(b, 3)]

                L1(0)
                LX(0)
                for it in m3a_items(0):
                    it()
                for b in range(NBLK):
                    nxt = []
                    if b + 1 < NBLK:
                        L1(b + 1)
                        LX(b + 1)
                        nxt = m3a_items(b + 1)
                    for ft in range(11):
                        GU(b, ft)
                        if ft < len(nxt):
                            nxt[ft]()
                    for tt in range(4):
                        DN(b, tt)
                S.emit()

        def phase_mf2(l, last, prefetch=None):
            with ExitStack() as es:
                sb, ps = mk(es)
                wg = sb("f2wg", [128, 8, HF], BF16)
                wu = sb("f2wu", [128, 8, HF], BF16)
                wd = sb("f2wd", [128, 11, D], BF16)
                load_w_bf16(wg, "wg", w_gate[l][:, HF:DFF], 8, HF)
                load_w_bf16(wu, "wu", w_up[l][:, HF:DFF], 8, HF)
                load_w_bf16(wd, "wd", w_down[l][HF:DFF, :], 11, D)
                gbf = sb("f2gbf", [128, D], F32)
                if last:
                    A("act", lambda e: e.dma_start(out=gbf[:], in_=final_norm_g.partition_broadcast(128)), writes=["gbf"], dma=True)
                h2T = [sb("f2h2T%d" % i, [128, 8, NB], BF16) for i in range(2)]
                ffT = sb("f2ffT", [128, 11, NB], BF16)
                sl = [sb("f2sl%d" % i, [128, NB], F32) for i in range(2)]
                xt = [sb("f2xt%d" % i, [128, D], F32) for i in range(8)]
                sqj = sb("f2sqj", [128, D], BF16)
                ss = sb("f2ss", [128, 4], F32)
                rstd = sb("f2rstd", [128, 4], F32)
                pgt = [ps("f2pg%d" % i, [128, NB], F32) for i in range(2)]
                put = [ps("f2pu%d" % i, [128, NB], F32) for i in range(2)]
                pd = [ps("f2pd%d" % i, [128, NB], F32) for i in range(4)]
                xdst = out if last else xs1

                def LD(b):
                    tok0 = b * NB
                    A("sp", lambda e: e.dma_start(out=h2T[b % 2][:], in_=h2Td[:, :, tok0:tok0 + NB].rearrange("t p n -> p t n")), writes=[("h2T", b % 2)], dma=True)
                    for tt in range(4):
                        A("sp", lambda e, tt=tt: e.dma_start(out=xt[(b % 2) * 4 + tt][:], in_=xmid[tok0 + tt * 128:tok0 + (tt + 1) * 128, :]),
                          writes=[("xt", b % 2, tt)], dma=True)

                pf_chunks = []
                if prefetch is not None:
                    wnext, wsrc = prefetch
                    stgp = [sb("f2stg%d" % i, [128, 1024], F32) for i in range(2)]
                    wvn = wsrc.rearrange("(kt p) n -> p kt n", p=128)
                    for cc, (c0, c1) in enumerate(((0, 1024), (1024, 2048), (2048, NIN))):
                        for kt in range(8):
                            pf_chunks.append((kt, c0, c1))

                def PF(n):
                    for _ in range(n):
                        if not pf_chunks:
                            return
                        kt, c0, c1 = pf_chunks.pop(0)
                        i = len(pf_chunks) % 2
                        A("sp", lambda e, i=i, kt=kt, c0=c0, c1=c1: e.dma_start(out=stgp[i][:, 0:c1 - c0], in_=wvn[:, kt, c0:c1]), writes=[("stgp", i)], dma=True)
                        A("pool", lambda e, i=i, kt=kt, c0=c0, c1=c1: e.tensor_copy(out=wnext[:, kt, c0:c1], in_=stgp[i][:, 0:c1 - c0]),
                          reads=[("stgp", i)], writes=["wnext"])

                LD(0)
                for b in range(NBLK):
                    if b + 1 < NBLK:
                        LD(b + 1)
                    PF(2)
                    tok0 = b * NB
                    hb = h2T[b % 2]
                    hk = ("h2T", b % 2)
                    for ft in range(11):
                        gi = ft % 2
                        for kt in range(8):
                            A("pe", lambda e, gi=gi, kt=kt, ft=ft, hb=hb: e.matmul(pgt[gi][:], lhsT=wg[:, kt, ft * 128:(ft + 1) * 128], rhs=hb[:, kt, :],
                                                                              start=(kt == 0), stop=(kt == 7)),
                              reads=wkeys("wg", 8, ft * 128) + [hk], writes=[("pgt", gi)])
                        for kt in range(8):
                            A("pe", lambda e, gi=gi, kt=kt, ft=ft, hb=hb: e.matmul(put[gi][:], lhsT=wu[:, kt, ft * 128:(ft + 1) * 128], rhs=hb[:, kt, :],
                                                                              start=(kt == 0), stop=(kt == 7)),
                              reads=wkeys("wu", 8, ft * 128) + [hk], writes=[("put", gi)])
                        A("act", lambda e, gi=gi: e.activation(out=sl[gi][:], in_=pgt[gi][:], func=AF.Silu), reads=[("pgt", gi)], writes=[("sl", gi)])
                        A("dve", lambda e, gi=gi, ft=ft: e.tensor_tensor(out=ffT[:, ft, :], in0=put[gi][:], in1=sl[gi][:], op=ALU.mult),
                          reads=[("put", gi), ("sl", gi)], writes=[("ffT", ft)])
                    for tt in range(4):
                        xb = xt[(b % 2) * 4 + tt]
                        xk = ("xt", b % 2, tt)
                        r0 = tok0 + tt * 128
                        for nh in range(2):
                            pi_ = (2 * tt + nh) % 4
                            for ft in range(11):
                                A("pe", lambda e, pi_=pi_, ft=ft, tt=tt, nh=nh: e.matmul(pd[pi_][:], lhsT=ffT[:, ft, tt * 128:(tt + 1) * 128],
                                                                                       rhs=wd[:, ft, nh * NB:(nh + 1) * NB], start=(ft == 0), stop=(ft == 10)),
                                  reads=[("ffT", ft)] + wkeys("wd", 11, 0), writes=[("pd", pi_)])
                            A("dve", lambda e, pi_=pi_, xb=xb, nh=nh: e.tensor_tensor(out=xb[:, nh * NB:(nh + 1) * NB], in0=pd[pi_][:], in1=xb[:, nh * NB:(nh + 1) * NB], op=ALU.add),
                              reads=[("pd", pi_), xk], writes=[xk])
                        if last:
                            A("act", lambda e, xb=xb, tt=tt: e.activation(out=sqj[:], in_=xb[:], func=AF.Square, accum_out=ss[:, tt:tt + 1]),
                              reads=[xk], writes=["sqj", ("ss", tt)])
                            A("act", lambda e, tt=tt: e.activation(out=rstd[:, tt:tt + 1], in_=ss[:, tt:tt + 1], func=AF.Ln, scale=1.0 / D, bias=EPS),
                              reads=[("ss", tt)], writes=[("rstd", tt)])
                            A("act", lambda e, tt=tt: e.activation(out=rstd[:, tt:tt + 1], in_=rstd[:, tt:tt + 1], func=AF.Exp, scale=-0.5), reads=[("rstd", tt)], writes=[("rstd", tt)])
                            A("dve", lambda e, xb=xb, tt=tt: e.scalar_tensor_tensor(out=xb[:], in0=xb[:], scalar=rstd[:, tt:tt + 1], in1=gbf[:],
                                                                                      op0=ALU.mult, op1=ALU.mult),
                              reads=[xk, ("rstd", tt), "gbf"], writes=[xk])
                        A("pool", lambda e, xb=xb, r0=r0: e.dma_start(out=xdst[r0:r0 + 128, :], in_=xb[:]), reads=[xk], writes=["xdst"], dma=True)
                S.emit()

        pre_es = None
        pre_w = None
        for l in layers:
            if "m1" in phases:
                phase_m1(l, pre_w=pre_w)
            if pre_es is not None:
                pre_es.close()
                pre_es, pre_w = None, None
            if "att" in phases:
                phase_att(l)
            if "s5" in phases:
                phase_s5(l)
            if "m3a" in phases:
                phase_mf1(l) if USE_MF else phase_m3a(l)
            if "m3b" in phases:
                if USE_MF:
                    pf = None
                    if l == 0 and 1 in layers and "m1" in phases:
                        pre_es = ExitStack()
                        uniq[0] += 1
                        pre_w = pre_es.enter_context(nc.sbuf_tensor("wpre_%d" % uniq[0], [128, 8, NIN], BF16))
                        pf = (pre_w, w_in[1])
                    phase_mf2(l, last=(l == 1), prefetch=pf)
                else:
                    phase_m3b(l, last=(l == 1))
    return nc


_PERM = np.concatenate([np.arange(32, 64), np.arange(0, 32)])


def _prep_inputs(inputs):
    w_in = np.asarray(inputs["w_in"], dtype=np.float32)
    q = w_in[:, :, 1536:1792].reshape(2, D, 4, 64)[:, :, :, _PERM].reshape(2, D, 256)
    k = w_in[:, :, 1792:2048].reshape(2, D, 4, 64)[:, :, :, _PERM].reshape(2, D, 256)
    w_in_p = np.ascontiguousarray(np.concatenate([w_in, q, k], axis=2))
    shared = {n: np.ascontiguousarray(np.asarray(v, dtype=np.float32)) for n, v in inputs.items() if n not in ("x", "w_in")}
    shared["w_in"] = w_in_p
    return shared


_NC_CACHE = {}


def kernel(**inputs):
    x = np.ascontiguousarray(np.asarray(inputs["x"], dtype=np.float32))
    shared = _prep_inputs(inputs)
    if "full" not in _NC_CACHE:
        _NC_CACHE["full"] = build_program()
    nc = _NC_CACHE["full"]
    xs = x.reshape(N_CORES, 2 * SEQ, D)
    in_maps = [dict(shared, x=np.ascontiguousarray(xs[i])) for i in range(N_CORES)]
    res = run_bass_kernel_spmd(nc, in_maps, core_ids=list(range(N_CORES)))
    outs = [np.asarray(res.results[i]["out"], dtype=np.float32).reshape(2, SEQ, D) for i in range(N_CORES)]
    return np.concatenate(outs, axis=0)
```

```python
import math
import numpy as np
import concourse.bass as bass
import concourse.mybir as mybir
from concourse.bass_utils import run_bass_kernel_spmd
from contextlib import ExitStack

F32 = mybir.dt.float32
BF16 = mybir.dt.bfloat16
I32 = mybir.dt.int32
ALU = mybir.AluOpType
AF = mybir.ActivationFunctionType

N_CORES = 8
D = 1024
SEQ = 4096
NB = 512
DFF = 2816
NIN = 2816
EPS = 1e-6
TWO_PI = 2.0 * math.pi
N_DMA_SEMS = 24
USE_MF = True
PATTERNS = ((1, 32), (4, 8), (16, 2))


class Sched:
    ENGS = ("pe", "act", "dve", "pool", "sp")

    def __init__(self, nc, es):
        self.nc = nc
        self.ops = []
        self.last_writer = {}
        self.readers = {}
        self.n_dma = {"sp": 0, "pool": 0, "act": 0}
        self.cnt = {e: 0 for e in self.ENGS}
        self.esem = {e: es.enter_context(nc.semaphore("s_" + e)) for e in ("pe", "act", "dve", "pool")}
        self.dsem = {"sp": [es.enter_context(nc.semaphore("d%d" % j)) for j in range(N_DMA_SEMS)],
                     "pool": [es.enter_context(nc.semaphore("dp%d" % j)) for j in range(8)],
                     "act": [es.enter_context(nc.semaphore("da%d" % j)) for j in range(8)]}
        self.seen = {e: {} for e in self.ENGS}

    def add(self, eng, fn, reads=(), writes=(), dma=False):
        i = len(self.ops)
        deps = set()
        for r in reads:
            w = self.last_writer.get(r)
            if w is not None:
                deps.add(w)
        for w_ in writes:
            w = self.last_writer.get(w_)
            if w is not None:
                deps.add(w)
            for rd in self.readers.get(w_, ()):
                deps.add(rd)
        deps.discard(i)
        op = dict(eng=eng, fn=fn, deps=deps, dma=dma, has_dep=False)
        if dma:
            op["dma_idx"] = self.n_dma[eng]
            self.n_dma[eng] += 1
        self.ops.append(op)
        for r in reads:
            self.readers.setdefault(r, []).append(i)
        for w_ in writes:
            self.last_writer[w_] = i
            self.readers[w_] = []
        return i

    def emit(self):
        nc = self.nc
        ops = self.ops
        for op in ops:
            if op["eng"] == "pe" and not op["dma"]:
                op["deps"] = {d for d in op["deps"] if not (ops[d]["eng"] == "pe" and not ops[d]["dma"])}
            for d in op["deps"]:
                ops[d]["has_dep"] = True
        cnt = self.cnt
        for op in ops:
            if not op["dma"] and op["has_dep"]:
                cnt[op["eng"]] += 1
                op["ms"] = cnt[op["eng"]]
        esem, dsem = self.esem, self.dsem
        per_eng = {e: [op for op in ops if op["eng"] == e] for e in self.ENGS}
        final_dma = {}

        def dsem_of(p):
            pool_ = dsem[p["eng"]]
            K = len(pool_)
            j = p["dma_idx"]
            return (p["eng"], j % K), pool_[j % K], 16 * (j // K + 1), K

        for op in ops:
            if op["dma"]:
                key, sem, val, K = dsem_of(op)
                final_dma[key] = (sem, val)

        def run(engname, eng):
            seen = self.seen[engname]

            def wait(sem, key, val):
                if seen.get(key, 0) >= val:
                    return
                seen[key] = val
                eng.wait_ge(sem, val)

            for op in per_eng[engname]:
                need = {}
                for d in op["deps"]:
                    p = ops[d]
                    if p["dma"]:
                        key, sem, val, K = dsem_of(p)
                    else:
                        key, sem, val = p["eng"], esem[p["eng"]], p["ms"]
                    if key not in need or need[key][1] < val:
                        need[key] = (sem, val)
                for key in sorted(need, key=str):
                    wait(need[key][0], key, need[key][1])
                if op["dma"]:
                    key, sem, val, K = dsem_of(op)
                    if val > 16:
                        wait(sem, key, val - 16)
                    ins = op["fn"](eng)
                    ins.then_inc(sem, 16)
                else:
                    ins = op["fn"](eng)
                    if op["has_dep"]:
                        ins.then_inc(esem[engname], 1)
            if engname == "sp":
                for key in sorted(final_dma, key=str):
                    wait(final_dma[key][0], key, final_dma[key][1])

        with nc.Block() as block:
            @block.sync
            def _(e):
                run("sp", e)

            @block.tensor
            def _(e):
                run("pe", e)

            @block.scalar
            def _(e):
                run("act", e)

            @block.vector
            def _(e):
                run("dve", e)

            @block.gpsimd
            def _(e):
                run("pool", e)
        self.ops = []
        self.last_writer = {}
        self.readers = {}
        nc.all_engine_barrier()


def build_program(n_seq=2, layers=(0, 1), phases=("m1", "att", "s5", "m3a", "m3b"), debug=False):
    NT = n_seq * SEQ
    NBLK = NT // NB
    nc = bass.Bass("TRN2", target_bir_lowering=False)

    def din(name, shape):
        return nc.dram_tensor(name, shape, F32, kind="ExternalInput").ap()

    x_in = din("x", [NT, D])
    norm_mix_g = din("norm_mix_g", [2, D])
    w_in = din("w_in", [2, D, NIN])
    conv3_w = din("conv3_w", [2, 3, 256])
    cfm_dw_w = din("cfm_dw_w", [2, 31, 256])
    cfm_dw_b = din("cfm_dw_b", [2, 256])
    cfm_ln_g = din("cfm_ln_g", [2, 256])
    cfm_ln_b = din("cfm_ln_b", [2, 256])
    s5_a_re = din("s5_a_re", [2, 16, 64])
    s5_a_im = din("s5_a_im", [2, 16, 64])
    s5_log_dt = din("s5_log_dt", [2, 16])
    s5_b_re = din("s5_b_re", [2, 16, 64, 16])
    s5_b_im = din("s5_b_im", [2, 16, 64, 16])
    s5_c_re = din("s5_c_re", [2, 16, 16, 64])
    s5_c_im = din("s5_c_im", [2, 16, 16, 64])
    s5_d = din("s5_d", [2, 256])
    s5_glu_w = din("s5_glu_w", [2, 256, 512])
    grp_norm_g = din("grp_norm_g", [2, D])
    w_out = din("w_out", [2, D, D])
    norm_ffn_g = din("norm_ffn_g", [2, D])
    w_gate = din("w_gate", [2, D, DFF])
    w_up = din("w_up", [2, D, DFF])
    w_down = din("w_down", [2, DFF, D])
    final_norm_g = din("final_norm_g", [D])
    out = nc.dram_tensor("out", [NT, D], F32, kind="ExternalOutput").ap()

    skind = "ExternalOutput" if debug else "Internal"
    xs1 = nc.dram_tensor("xs1", [NT, D], F32, kind=skind).ap()
    xmid = nc.dram_tensor("xmid", [NT, D], F32, kind=skind).ap()
    ymix = nc.dram_tensor("ymix", [8, 128, NT], F32, kind=skind).ap()
    ubd = nc.dram_tensor("ubd", [2, 128, NT], BF16, kind=skind).ap()
    qTd = nc.dram_tensor("qTd", [2, 128, NT], BF16, kind=skind).ap()
    kTd = nc.dram_tensor("kTd", [2, 128, NT], BF16, kind=skind).ap()
    vTd = nc.dram_tensor("vTd", [2, 128, NT], BF16, kind=skind).ap()
    h2Td = nc.dram_tensor("h2Td", [8, 128, NT], BF16, kind=skind).ap()

    with ExitStack() as es0:
        es0.enter_context(nc.allow_non_contiguous_dma(reason="small parameter layouts"))
        S = Sched(nc, es0)
        A = S.add

        uniq = [0]

        def mk(es):
            def sb(name, shape, dt):
                uniq[0] += 1
                return es.enter_context(nc.sbuf_tensor("%s_%d" % (name, uniq[0]), shape, dt))

            def ps(name, shape, dt):
                uniq[0] += 1
                return es.enter_context(nc.psum_tensor("%s_%d" % (name, uniq[0]), shape, dt))
            return sb, ps

        def wkeys(dst_key, kt_n, col):
            return [(dst_key, col // 2048, k0) for k0 in range(0, kt_n, 4)]

        def load_w_bf16(dst, dst_key, src_ap, kt_n, ncols):
            v = src_ap.rearrange("(kt p) n -> p kt n", p=128)
            step = 2048
            for c0 in range(0, ncols, step):
                c1 = min(ncols, c0 + step)
                for k0 in range(0, kt_n, 4):
                    k1 = min(kt_n, k0 + 4)
                    A("pool", lambda e, c0=c0, c1=c1, k0=k0, k1=k1: e.dma_start(out=dst[:, k0:k1, c0:c1], in_=v[:, k0:k1, c0:c1]),
                      writes=[(dst_key, c0 // step, k0)], dma=True)

        def make_ident(sb, pfx):
            idi = sb(pfx + "idi", [128, 128], I32)
            identb = sb(pfx + "identb", [128, 128], BF16)
            identf = sb(pfx + "identf", [128, 128], F32)
            A("pool", lambda e: e.iota(idi[:], pattern=[[1, 128]], base=0, channel_multiplier=-1), writes=["idi"])
            A("dve", lambda e: e.tensor_scalar(out=identb[:], in0=idi[:], scalar1=0.0, scalar2=None, op0=ALU.is_equal),
              reads=["idi"], writes=["identb"])
            A("dve", lambda e: e.tensor_scalar(out=identf[:], in0=idi[:], scalar1=0.0, scalar2=None, op0=ALU.is_equal),
              reads=["idi"], writes=["identf"])
            return idi, identb, identf

        def range_reduce(ang, tmpf, tmpi, key, tkey):
            A("dve", lambda e: e.tensor_scalar(out=tmpf, in0=ang, scalar1=1.0 / TWO_PI, scalar2=None, op0=ALU.mult),
              reads=[key], writes=[tkey])
            A("dve", lambda e: e.tensor_copy(out=tmpi, in_=tmpf), reads=[tkey], writes=[tkey + "i"])
            A("dve", lambda e: e.tensor_copy(out=tmpf, in_=tmpi), reads=[tkey + "i"], writes=[tkey])
            A("dve", lambda e: e.scalar_tensor_tensor(out=ang, in0=tmpf, scalar=-TWO_PI, in1=ang, op0=ALU.mult, op1=ALU.add),
              reads=[tkey, key], writes=[key])
            A("dve", lambda e: e.tensor_scalar(out=ang, in0=ang, scalar1=math.pi, scalar2=-math.pi, op0=ALU.min, op1=ALU.max),
              reads=[key], writes=[key])

        def phase_m1(l, pre_w=None):
            xsrc = x_in if l == 0 else xs1
            with ExitStack() as es:
                sb, ps = mk(es)
                idi, identb, identf = make_ident(sb, "m1")
                wsb = pre_w if pre_w is not None else sb("m1w", [128, 8, NIN], BF16)
                gb = sb("m1gb", [128, D], F32)
                A("act", lambda e: e.dma_start(out=gb[:], in_=norm_mix_g[l].partition_broadcast(128)), writes=["gb"], dma=True)
                w3 = sb("m1w3", [128, 2, 3], F32)
                wdw = sb("m1wdw", [128, 2, 31], F32)
                dwb = sb("m1dwb", [128, 2], F32)
                lng = sb("m1lng", [128, 2], F32)
                lnb = sb("m1lnb", [128, 2], F32)


                def load_params():
                    prow = tf[:, 0:256]
                    A("act", lambda e: e.dma_start(out=prow[0:31, :], in_=cfm_dw_w[l]), writes=["tf"], dma=True)
                    A("act", lambda e: e.dma_start(out=prow[31:34, :], in_=conv3_w[l]), writes=["tf"], dma=True)
                    for j, src in enumerate((cfm_dw_b, cfm_ln_g, cfm_ln_b)):
                        A("act", lambda e, j=j, src=src: e.dma_start(out=prow[34 + j:35 + j, :], in_=src[l:l + 1, :]), writes=["tf"], dma=True)
                    for t in range(2):
                        A("pe", lambda e, t=t: e.transpose(out=pin[3][:, t * 64:t * 64 + 37], in_=prow[0:37, t * 128:(t + 1) * 128], identity=identf[0:37, 0:37]),
                          reads=["tf", "identf"], writes=[("pin", 3)])
                    for t in range(2):
                        A("dve", lambda e, t=t: e.tensor_copy(out=wdw[:, t, :], in_=pin[3][:, t * 64:t * 64 + 31]), reads=[("pin", 3)], writes=["wdw"])
                        A("dve", lambda e, t=t: e.tensor_copy(out=w3[:, t, :], in_=pin[3][:, t * 64 + 31:t * 64 + 34]), reads=[("pin", 3)], writes=["w3"])
                        A("dve", lambda e, t=t: e.tensor_copy(out=dwb[:, t:t + 1], in_=pin[3][:, t * 64 + 34:t * 64 + 35]), reads=[("pin", 3)], writes=["dwb"])
                        A("dve", lambda e, t=t: e.tensor_copy(out=lng[:, t:t + 1], in_=pin[3][:, t * 64 + 35:t * 64 + 36]), reads=[("pin", 3)], writes=["lng"])
                        A("dve", lambda e, t=t: e.tensor_copy(out=lnb[:, t:t + 1], in_=pin[3][:, t * 64 + 36:t * 64 + 37]), reads=[("pin", 3)], writes=["lnb"])

                diag = sb("m1diag", [128, 2, 31, 128], BF16)
                ones256 = sb("m1ones", [128, 128], BF16)
                COS = sb("m1cos", [128, SEQ], F32)
                SIN = sb("m1sin", [128, SEQ], F32)
                tf = sb("m1tf", [128, NB], F32)
                ti = sb("m1ti", [128, NB], I32)
                pidx = sb("m1pidx", [128, 1], I32)
                pj = sb("m1pj", [128, 1], I32)
                pf = sb("m1pf", [128, 1], F32)
                inv = sb("m1inv", [128, 1], F32)
                sgn = sb("m1sgn", [128, 1], F32)
                xt = [sb("m1xt%d" % i, [128, D], F32) for i in range(2)]
                sqj = sb("m1sqj", [128, D], BF16)
                ss = sb("m1ss", [128, 4], F32)
                rstd = sb("m1rstd", [128, 4], F32)
                hb = [sb("m1h%d" % i, [128, D], BF16) for i in range(8)]
                hT = [sb("m1hT%d" % i, [128, 8, NB], BF16) for i in range(2)]
                ch_sb = sb("m1ch", [128, 2, NB], F32)
                zbuf = sb("m1z", [128, 2, NB + 2], F32)
                acc = sb("m1acc", [128, 2, NB], F32)
                ycv = sb("m1ycv", [128, 2, NB], F32)
                sg_sb = sb("m1sg", [128, 2, NB], F32)
                zcb = sb("m1zcb", [128, 2, NB + 30], BF16)
                cf = sb("m1cf", [128, 2, NB], F32)
                cfb = sb("m1cfb", [128, 2, 2, NB], BF16)
                mean_sb = sb("m1mean", [128, NB], F32)
                var_sb = sb("m1var", [128, NB], F32)
                rs_sb = sb("m1rs", [128, NB], F32)
                tmpc = sb("m1tmpc", [128, 2, NB], F32)
                ycf = sb("m1ycf", [128, 2, NB], F32)
                ub_sb = sb("m1ub", [128, 2, NB], BF16)
                v_sb = sb("m1v", [128, 2, NB], BF16)
                t1 = sb("m1t1", [128, 4, NB], F32)
                t2 = sb("m1t2", [128, 2, NB], F32)
                qk_sb = sb("m1qk", [128, 4, NB], BF16)
                pt = [ps("m1pt%d" % i, [128, 8, 128], BF16) for i in range(2)]
                pin = [ps("m1pin%d" % i, [128, NB], F32) for i in range(4)]
                pc = [ps("m1pc%d" % i, [128, NB], F32) for i in range(2)]

                pin_ctr = [0]

                def inproj(m, b):
                    i = pin_ctr[0] % 4
                    pin_ctr[0] += 1
                    for kt in range(8):
                        A("pe", lambda e, kt=kt, i=i: e.matmul(pin[i][:], lhsT=wsb[:, kt, m * 128:(m + 1) * 128], rhs=hT[b % 2][:, kt, :],
                                                                start=(kt == 0), stop=(kt == 7)),
                          reads=[("w", min(2, (m * 128) // 1024), kt_) for kt_ in range(8)] + [("hT", b % 2)], writes=[("pin", i)])
                    return i

                def stage_N(b):
                    tok0 = b * NB
                    for tt in range(4):
                        xb = xt[tt % 2]
                        xk = ("xt", tt % 2)
                        r0 = tok0 + tt * 128
                        A("sp", lambda e, xb=xb, r0=r0: e.dma_start(out=xb[:], in_=xsrc[r0:r0 + 128, :]), writes=[xk], dma=True)
                        A("act", lambda e, xb=xb, tt=tt: e.activation(out=sqj[:], in_=xb[:], func=AF.Square, accum_out=ss[:, tt:tt + 1]),
                          reads=[xk], writes=["sqj", "ss"])
                        A("act", lambda e, tt=tt: e.activation(out=rstd[:, tt:tt + 1], in_=ss[:, tt:tt + 1], func=AF.Ln, scale=1.0 / D, bias=EPS),
                          reads=["ss"], writes=["rstd"])
                        A("act", lambda e, tt=tt: e.activation(out=rstd[:, tt:tt + 1], in_=rstd[:, tt:tt + 1], func=AF.Exp, scale=-0.5), reads=["rstd"], writes=["rstd"])
                        A("dve", lambda e, xb=xb, tt=tt: e.scalar_tensor_tensor(out=hb[(b % 2) * 4 + tt][:], in0=xb[:], scalar=rstd[:, tt:tt + 1], in1=gb[:],
                                                                                  op0=ALU.mult, op1=ALU.mult),
                          reads=[xk, "rstd", "gb"], writes=[("hb", b % 2, tt)])

                def stage_T(b):
                    for tt in range(4):
                        for kt in range(8):
                            A("pe", lambda e, kt=kt, tt=tt: e.transpose(out=pt[tt % 2][:, kt, :], in_=hb[(b % 2) * 4 + tt][:, kt * 128:(kt + 1) * 128],
                                                                          identity=identb[:]),
                              reads=[("hb", b % 2, tt), "identb"], writes=[("pt", tt % 2)])
                        A("act", lambda e, tt=tt, b=b: e.copy(out=hT[b % 2][:, :, tt * 128:(tt + 1) * 128], in_=pt[tt % 2][:]),
                          reads=[("pt", tt % 2)], writes=[("hT", b % 2)])

                def stage_B1(b):
                    tok0 = b * NB
                    bs = b % (SEQ // NB)
                    if bs == 0:
                        A("pool", lambda e: e.memset(zbuf[:, :, 0:2], 0.0), writes=["zbuf"])
                        A("pool", lambda e: e.memset(zcb[:, :, 0:30], 0.0), writes=[("zcb", 0), ("zcb", 1)])
                    for t in range(2):
                        i = inproj(0 + t, b)
                        A("act", lambda e, i=i, t=t: e.copy(out=ch_sb[:, t, :], in_=pin[i][:]), reads=[("pin", i)], writes=["ch"])
                    for t in range(2):
                        i = inproj(4 + t, b)
                        A("dve", lambda e, i=i, t=t: e.tensor_tensor(out=zbuf[:, t, 2:NB + 2], in0=pin[i][:], in1=ch_sb[:, t, :], op=ALU.mult),
                          reads=[("pin", i), "ch"], writes=["zbuf"])
                        A("dve", lambda e, t=t: e.tensor_scalar(out=acc[:, t, :], in0=zbuf[:, t, 0:NB], scalar1=w3[:, t, 0:1], scalar2=None, op0=ALU.mult),
                          reads=["zbuf", "w3"], writes=["acc"])
                        for k in (1, 2):
                            A("dve", lambda e, t=t, k=k: e.scalar_tensor_tensor(out=acc[:, t, :], in0=zbuf[:, t, k:NB + k], scalar=w3[:, t, k:k + 1],
                                                                                 in1=acc[:, t, :], op0=ALU.mult, op1=ALU.add),
                              reads=["zbuf", "w3", "acc"], writes=["acc"])
                        A("pool", lambda e, t=t: e.tensor_copy(out=zbuf[:, t, 0:2], in_=zbuf[:, t, NB:NB + 2]), reads=["zbuf"], writes=["zbuf"])
                    for t in range(2):
                        i = inproj(2 + t, b)
                        A("dve", lambda e, i=i, t=t: e.tensor_tensor(out=ycv[:, t, :], in0=pin[i][:], in1=acc[:, t, :], op=ALU.mult),
                          reads=[("pin", i), "acc"], writes=["ycv"])
                    A("pool", lambda e, tok0=tok0: e.dma_start(out=ymix[0:2, :, tok0:tok0 + NB].rearrange("t p n -> p t n"), in_=ycv[:]),
                      reads=["ycv"], writes=["ymix"], dma=True)

                def cfm_a(b):
                    for t in range(2):
                        i = inproj(8 + t, b)
                        A("act", lambda e, i=i, t=t: e.activation(out=sg_sb[:, t, :], in_=pin[i][:], func=AF.Sigmoid),
                          reads=[("pin", i)], writes=[("sg", t)])
                    for t in range(2):
                        i = inproj(6 + t, b)
                        A("dve", lambda e, i=i, t=t: e.tensor_tensor(out=zcb[:, t, 30:NB + 30], in0=pin[i][:], in1=sg_sb[:, t, :], op=ALU.mult),
                          reads=[("pin", i), ("sg", t)], writes=[("zcb", t)])

                def cfm_conv(b):
                    for t in range(2):
                        for k in range(31):
                            A("pe", lambda e, t=t, k=k: e.matmul(pc[t][:], lhsT=diag[:, t, k, :], rhs=zcb[:, t, k:k + NB],
                                                                  start=(k == 0), stop=(k == 30)),
                              reads=["diag", ("zcb", t)], writes=[("pc", t)])
                        A("pool", lambda e, t=t: e.tensor_copy(out=zcb[:, t, 0:30], in_=zcb[:, t, NB:NB + 30]), reads=[("zcb", t)], writes=[("zcb", t)])
                        A("act", lambda e, t=t: e.activation(out=cf[:, t, :], in_=pc[t][:], func=AF.Identity, bias=dwb[:, t:t + 1]),
                          reads=[("pc", t), "dwb"], writes=[("cf", t)])
                        A("act", lambda e, t=t: e.copy(out=cfb[:, 0, t, :], in_=cf[:, t, :]), reads=[("cf", t)], writes=[("cfb", 0, t)])
                        A("act", lambda e, t=t: e.activation(out=cfb[:, 1, t, :], in_=cf[:, t, :], func=AF.Square), reads=[("cf", t)], writes=[("cfb", 1, t)])

                def cfm_stats(b):
                    tok0 = b * NB
                    for j in range(2):
                        for t in range(2):
                            A("pe", lambda e, j=j, t=t: e.matmul(pc[j][:], lhsT=ones256[:], rhs=cfb[:, j, t, :], start=(t == 0), stop=(t == 1)),
                              reads=["ones256", ("cfb", j, t)], writes=[("pc", j)])
                    A("act", lambda e: e.copy(out=mean_sb[:], in_=pc[0][:]), reads=[("pc", 0)], writes=["mean"])
                    A("dve", lambda e: e.tensor_tensor(out=var_sb[:], in0=mean_sb[:], in1=mean_sb[:], op=ALU.mult), reads=["mean"], writes=["var"])
                    A("dve", lambda e: e.tensor_tensor(out=var_sb[:], in0=pc[1][:], in1=var_sb[:], op=ALU.subtract),
                      reads=[("pc", 1), "var"], writes=["var"])
                    A("act", lambda e: e.activation(out=rs_sb[:], in_=var_sb[:], func=AF.Ln, bias=EPS), reads=["var"], writes=["rs"])
                    A("act", lambda e: e.activation(out=rs_sb[:], in_=rs_sb[:], func=AF.Exp, scale=-0.5), reads=["rs"], writes=["rs"])
                    for t in range(2):
                        A("dve", lambda e, t=t: e.tensor_tensor(out=tmpc[:, t, :], in0=cf[:, t, :], in1=mean_sb[:], op=ALU.subtract),
                          reads=[("cf", t), "mean"], writes=[("tmpc", t)])
                        A("dve", lambda e, t=t: e.tensor_tensor(out=tmpc[:, t, :], in0=tmpc[:, t, :], in1=rs_sb[:], op=ALU.mult), reads=[("tmpc", t), "rs"], writes=[("tmpc", t)])
                        A("act", lambda e, t=t: e.activation(out=ycf[:, t, :], in_=tmpc[:, t, :], func=AF.Silu, scale=lng[:, t:t + 1], bias=lnb[:, t:t + 1]),
                          reads=[("tmpc", t), "lng", "lnb"], writes=["ycf"])
                    A("pool", lambda e, tok0=tok0: e.dma_start(out=ymix[2:4, :, tok0:tok0 + NB].rearrange("t p n -> p t n"), in_=ycf[:]),
                      reads=["ycf"], writes=["ymix"], dma=True)

                def ssm_u(b):
                    tok0 = b * NB
                    for t in range(2):
                        i = inproj(10 + t, b)
                        A("act", lambda e, i=i, t=t: e.copy(out=ub_sb[:, t, :], in_=pin[i][:]), reads=[("pin", i)], writes=["ub"])
                    A("pool", lambda e, tok0=tok0: e.dma_start(out=ubd[:, :, tok0:tok0 + NB].rearrange("t p n -> p t n"), in_=ub_sb[:]),
                      reads=["ub"], writes=["ubd"], dma=True)

                def rope(b, t4s, dstd):
                    tok0 = b * NB
                    bs = b % (SEQ // NB)
                    pos = slice(bs * NB, (bs + 1) * NB)
                    for t4 in t4s:
                        i = inproj(12 + t4, b)
                        A("dve", lambda e, i=i, t4=t4: e.tensor_tensor(out=t1[:, t4, :], in0=pin[i][:], in1=COS[:, pos], op=ALU.mult),
                          reads=[("pin", i), "COS"], writes=[("t1", t4)])
                        i = inproj(18 + t4, b)
                        A("dve", lambda e, i=i, t4=t4: e.tensor_tensor(out=t2[:, t4 % 2, :], in0=pin[i][:], in1=SIN[:, pos], op=ALU.mult),
                          reads=[("pin", i), "SIN"], writes=[("t2", t4 % 2)])
                        A("pool", lambda e, t4=t4: e.tensor_tensor(out=qk_sb[:, t4, :], in0=t1[:, t4, :], in1=t2[:, t4 % 2, :], op=ALU.add),
                          reads=[("t1", t4), ("t2", t4 % 2)], writes=[("qk", t4 // 2)])
                    g2 = t4s[0] // 2
                    A("pool", lambda e: e.dma_start(out=dstd[:, :, tok0:tok0 + NB].rearrange("t p n -> p t n"), in_=qk_sb[:, 2 * g2:2 * g2 + 2, :]),
                      reads=[("qk", g2)], writes=["qkTd"], dma=True)

                def vproj(b):
                    tok0 = b * NB
                    for t in range(2):
                        i = inproj(16 + t, b)
                        A("act", lambda e, i=i, t=t: e.copy(out=v_sb[:, t, :], in_=pin[i][:]), reads=[("pin", i)], writes=["v"])
                    A("pool", lambda e, tok0=tok0: e.dma_start(out=vTd[:, :, tok0:tok0 + NB].rearrange("t p n -> p t n"), in_=v_sb[:]),
                      reads=["v"], writes=["vTd"], dma=True)

                def diag_build():
                    for t in range(2):
                        for k in range(31):
                            A("act", lambda e, t=t, k=k: e.activation(out=diag[:, t, k, :], in_=identf[:], func=AF.Copy, scale=wdw[:, t, k:k + 1]),
                              reads=["identf", "wdw"], writes=["diag"])
                    A("pool", lambda e: e.memset(ones256[:], 1.0 / 256), writes=["ones256"])

                def late_setup():
                    A("pool", lambda e: e.iota(pidx[:], pattern=[[0, 1]], base=0, channel_multiplier=1), writes=["pidx"])
                    A("dve", lambda e: e.tensor_scalar(out=pj[:], in0=pidx[:], scalar1=31, scalar2=None, op0=ALU.bitwise_and),
                      reads=["pidx"], writes=["pj"])
                    A("dve", lambda e: e.tensor_copy(out=pf[:], in_=pj[:]), reads=["pj"], writes=["pf"])
                    A("act", lambda e: e.activation(out=inv[:], in_=pf[:], func=AF.Exp, scale=-math.log(10000.0) / 32.0),
                      reads=["pf"], writes=["inv"])
                    A("dve", lambda e: e.tensor_scalar(out=pj[:], in0=pidx[:], scalar1=32, scalar2=None, op0=ALU.bitwise_and),
                      reads=["pidx", "pf"], writes=["pj"])
                    A("dve", lambda e: e.tensor_copy(out=pf[:], in_=pj[:]), reads=["pj", "inv"], writes=["pf"])
                    A("dve", lambda e: e.tensor_scalar(out=sgn[:], in0=pf[:], scalar1=1.0 / 16, scalar2=-1.0, op0=ALU.mult, op1=ALU.add),
                      reads=["pf"], writes=["sgn"])
                    for c in range(SEQ // NB):
                        cs = slice(c * NB, (c + 1) * NB)
                        A("pool", lambda e, c=c: e.iota(ti[:], pattern=[[1, NB]], base=c * NB, channel_multiplier=0),
                          reads=["tf"], writes=["tfi"])
                        A("dve", lambda e, cs=cs: e.tensor_copy(out=SIN[:, cs], in_=ti[:]), reads=["tfi"], writes=["SIN"])
                        A("dve", lambda e, cs=cs: e.tensor_scalar(out=SIN[:, cs], in0=SIN[:, cs], scalar1=inv[:, 0:1], scalar2=None, op0=ALU.mult),
                          reads=["SIN", "inv"], writes=["SIN"])
                        A("dve", lambda e, cs=cs: e.tensor_scalar(out=COS[:, cs], in0=SIN[:, cs], scalar1=math.pi / 2, scalar2=None, op0=ALU.add),
                          reads=["SIN"], writes=["COS"])
                        range_reduce(SIN[:, cs], tf[:], ti[:], "SIN", "tf")
                        range_reduce(COS[:, cs], tf[:], ti[:], "COS", "tf")
                        A("act", lambda e, cs=cs: e.activation(out=SIN[:, cs], in_=SIN[:, cs], func=AF.Sin), reads=["SIN"], writes=["SIN"])
                        A("act", lambda e, cs=cs: e.activation(out=COS[:, cs], in_=COS[:, cs], func=AF.Sin), reads=["COS"], writes=["COS"])
                        A("dve", lambda e, cs=cs: e.tensor_scalar(out=SIN[:, cs], in0=SIN[:, cs], scalar1=sgn[:, 0:1], scalar2=None, op0=ALU.mult),
                          reads=["SIN", "sgn"], writes=["SIN"])


                def load_weights():
                    stg = [(t1[:, 0:2, :].rearrange("p a b -> p (a b)"), [("t1", 0), ("t1", 1)]),
                           (t1[:, 2:4, :].rearrange("p a b -> p (a b)"), [("t1", 2), ("t1", 3)]),
                           (qk_sb[:].rearrange("p a b -> p (a b)").bitcast(F32), [("qk", 0), ("qk", 1)]),
                           (ycf[:].rearrange("p a b -> p (a b)"), ["ycf"])]
                    wv = w_in[l].rearrange("(kt p) n -> p kt n", p=128)
                    ns = 0
                    for cc, (c0, c1) in enumerate(((0, 1024), (1024, 2048), (2048, NIN))):
                        for kt in range(8):
                            sbuf_, skeys = stg[ns % 4]
                            ns += 1
                            A("sp", lambda e, sbuf_=sbuf_, kt=kt, c0=c0, c1=c1: e.dma_start(out=sbuf_[:, 0:c1 - c0], in_=wv[:, kt, c0:c1]), writes=skeys, dma=True)
                            A("act" if ns % 2 else "dve", lambda e, sbuf_=sbuf_, kt=kt, c0=c0, c1=c1, ns=ns: (e.copy if ns % 2 else e.tensor_copy)(out=wsb[:, kt, c0:c1], in_=sbuf_[:, 0:c1 - c0]),
                              reads=skeys, writes=[("w", cc, kt)])

                stage_N(0)
                load_params()
                stage_T(0)
                if pre_w is None:
                    load_weights()
                diag_build()
                if NBLK > 1:
                    stage_N(1)
                late_setup()
                for b in range(NBLK):
                    stage_B1(b)
                    cfm_a(b)
                    ssm_u(b)
                    cfm_conv(b)
                    rope(b, (0, 1), qTd)
                    cfm_stats(b)
                    rope(b, (2, 3), kTd)
                    if b + 1 < NBLK:
                        stage_T(b + 1)
                    vproj(b)
                    if b + 2 < NBLK:
                        stage_N(b + 2)
                S.emit()

        def phase_att(l):
            with ExitStack() as es:
                sb, ps = mk(es)
                idi, identb, identf = make_ident(sb, "at")
                mb = sb("atmb", [128, 256], BF16)
                A("dve", lambda e: e.tensor_scalar(out=mb[:, 0:128], in0=idi[:], scalar1=0.0, scalar2=None, op0=ALU.is_ge),
                  reads=["idi"], writes=["mb"])
                A("dve", lambda e: e.tensor_scalar(out=mb[:, 128:256], in0=idi[:], scalar1=0.0, scalar2=None, op0=ALU.is_le),
                  reads=["idi"], writes=["mb"])
                sel = sb("atsel", [65, 64], F32)
                A("pool", lambda e: e.memset(sel[:], 0.0), writes=["sel"])
                A("pool", lambda e: e.memset(sel[64:65, :], 1.0), writes=["sel"])
                qT2 = [sb("atq%d" % i, [128, 2, SEQ], BF16) for i in range(2)]
                kT2 = [sb("atk%d" % i, [128, 2, SEQ], BF16) for i in range(2)]
                vT2 = [sb("atv%d" % i, [128, 2, SEQ], BF16) for i in range(2)]

                def load_qkv(s_):
                    t0_ = s_ * SEQ
                    for dst, src, key in ((qT2[s_ % 2], qTd, "qT"), (kT2[s_ % 2], kTd, "kT"), (vT2[s_ % 2], vTd, "vT")):
                        A("sp", lambda e, dst=dst, src=src, t0_=t0_: e.dma_start(out=dst[:], in_=src[:, :, t0_:t0_ + SEQ].rearrange("t p n -> p t n")),
                          writes=[(key, s_ % 2)], dma=True)
                vext2 = [sb("atvext%d" % i, [128, 32, 65], BF16) for i in range(2)]
                for i in range(2):
                    A("pool", lambda e, i=i: e.memset(vext2[i][:, :, 64:65], 1.0), writes=[("vext", i)])
                accb2 = [sb("atacc%d" % i, [65, SEQ], F32) for i in range(2)]
                yo = sb("atyo", [64, SEQ], F32)
                rb = sb("atrb", [64, NB], F32)
                P = [sb("atP%d" % i, [128, 256], BF16) for i in range(5)]
                pB = ps("atpB", [128, NB], F32)
                pv0 = pB[:].bitcast(BF16).rearrange("p (a b) -> p a b", b=64)
                pS = [ps("atpS%d" % i, [128, 512], F32) for i in range(3)]
                pO = [ps("atpO%d" % i, [128, 512], F32) for i in range(4)]
                kb_ctr = [0]
                for s in range(n_seq):
                    t0 = s * SEQ
                    if s == 0:
                        load_qkv(0)
                    if s + 1 < n_seq:
                        load_qkv(s + 1)
                    qT, kT, vT = qT2[s % 2], kT2[s % 2], vT2[s % 2]
                    for ht in range(2):
                        for pi, (d, nb) in enumerate(PATTERNS):
                            qv = qT[:, ht, :].rearrange("p (n r) -> p r n", r=d)
                            kv = kT[:, ht, :].rearrange("p (n r) -> p r n", r=d)
                            vv = vT[:, ht, :].rearrange("p (n r) -> p r n", r=d)
                            avs = [accb2[hh][:].rearrange("p (n r) -> p r n", r=d) for hh in range(2)]
                            for hh in range(2):
                                po = 64 * hh
                                for g8 in range(4):
                                    for j8 in range(8):
                                        bi = g8 * 8 + j8
                                        r, kb = bi // nb, bi % nb
                                        A("pe", lambda e, j8=j8, r=r, kb=kb, vv=vv, po=po: e.transpose(
                                            out=pv0[:, j8, :], in_=vv[po:po + 64, r, kb * 128:(kb + 1) * 128], identity=identb[po:po + 64, po:po + 64]),
                                          reads=[("vT", s % 2), "identb"], writes=[("pv", 0)])
                                    A("act", lambda e, g8=g8, hh=hh: e.copy(out=vext2[hh][:, g8 * 8:(g8 + 1) * 8, 0:64], in_=pv0[:, 0:8, :]),
                                      reads=[("pv", 0)], writes=[("vext", hh)])
                            steps = [(hh, r, kb) for r in range(d) for kb in range(nb) for hh in range(2)]
                            cbase = kb_ctr[0]
                            kb_ctr[0] += len(steps)

                            def emit_qk(si, steps=steps, cbase=cbase, qv=qv, kv=kv, nb=nb):
                                hh, r, kb = steps[si]
                                po = 64 * hh
                                c = cbase + si
                                nq = 256 if kb + 1 < nb else 128
                                pSc = pS[c % 3]
                                A("pe", lambda e: e.matmul(pSc[:, 0:nq], lhsT=kv[po:po + 64, r, kb * 128:(kb + 1) * 128],
                                                           rhs=qv[po:po + 64, r, kb * 128:kb * 128 + nq], start=True, stop=True),
                                  reads=[("kT", s % 2), ("qT", s % 2)], writes=[("pS", c % 3)])

                            def emit_rest(si, steps=steps, cbase=cbase, avs=avs, nb=nb, pi=pi):
                                hh, r, kb = steps[si]
                                bi = r * nb + kb
                                c = cbase + si
                                nq = 256 if kb + 1 < nb else 128
                                pSc, Pc = pS[c % 3], P[c % 5]
                                vx = vext2[hh]
                                A("act", lambda e: e.activation(out=Pc[:, 0:nq], in_=pSc[:, 0:nq], func=AF.Exp, scale=0.125),
                                  reads=[("pS", c % 3)], writes=[("P", c % 5)])
                                A("pool", lambda e: e.tensor_tensor(out=Pc[:, 0:nq], in0=Pc[:, 0:nq], in1=mb[:, 0:nq], op=ALU.mult),
                                  reads=[("P", c % 5), "mb"], writes=[("P", c % 5)])
                                for half in range(nq // 128):
                                    qb = kb + half
                                    slot = 2 * hh + qb % 2
                                    first = (half == 1) or (kb == 0)
                                    A("pe", lambda e, half=half, slot=slot, first=first: e.matmul(
                                        pO[slot][0:65, 0:128], lhsT=vx[:, bi, :], rhs=Pc[:, half * 128:(half + 1) * 128],
                                        start=first, stop=(half == 0), skip_group_check=True),
                                      reads=[("vext", hh), ("P", c % 5)], writes=[("pO", slot)])
                                    if half == 0:
                                        dst = avs[hh][0:65, r, qb * 128:(qb + 1) * 128]
                                        if pi == 0:
                                            A("dve", lambda e, dst=dst, slot=slot: e.tensor_copy(out=dst, in_=pO[slot][0:65, 0:128]),
                                              reads=[("pO", slot)], writes=[("acc", hh)])
                                        else:
                                            A("dve", lambda e, dst=dst, slot=slot: e.tensor_tensor(out=dst, in0=pO[slot][0:65, 0:128], in1=dst, op=ALU.add),
                                              reads=[("pO", slot), ("acc", hh)], writes=[("acc", hh)])

                            for si in range(min(2, len(steps))):
                                emit_qk(si)
                            for si in range(len(steps)):
                                if si + 2 < len(steps):
                                    emit_qk(si + 2)
                                emit_rest(si)
                        for hh in range(2):
                            accb = accb2[hh]
                            po = 64 * hh
                            for c in range(SEQ // NB):
                                cs = slice(c * NB, (c + 1) * NB)
                                A("pe", lambda e, cs=cs, accb=accb: e.matmul(pB[0:64, :], lhsT=sel[:], rhs=accb[0:65, cs], start=True, stop=True),
                                  reads=["sel", ("acc", hh)], writes=[("pv", 0)])
                                A("dve", lambda e: e.reciprocal(out=rb[:], in_=pB[0:64, :]), reads=[("pv", 0)], writes=["rb"])
                                A("dve", lambda e, cs=cs, accb=accb: e.tensor_tensor(out=yo[:, cs], in0=accb[0:64, cs], in1=rb[:], op=ALU.mult),
                                  reads=[("acc", hh), "rb"], writes=["yo"])
                            A("pool", lambda e, ht=ht, po=po, t0=t0: e.dma_start(out=ymix[6 + ht, po:po + 64, t0:t0 + SEQ], in_=yo[:]),
                              reads=["yo"], writes=["ymix"], dma=True)
                S.emit()

        def phase_s5(l):
            with ExitStack() as eso:
                sbo, pso_ = mk(eso)
                Wst = sbo("s5W", [128, 16, 2, 8, 128], BF16)
                Vst = sbo("s5V", [128, 16, 2, 8, 128], BF16)
                BD = sbo("s5BD", [128, 16, 2, 128], BF16)
                Msr = sbo("s5Msr", [128, 8, 8], F32)
                Msi = sbo("s5Msi", [128, 8, 8], F32)
                nMsi = sbo("s5nMsi", [128, 8, 8], F32)
                wglu = sbo("s5wglu", [128, 2, 512], BF16)
                with ExitStack() as es:
                    sb, ps = mk(es)
                    idi, identb, identf = make_ident(sb, "s5")
                    load_w_bf16(wglu, "wglu", s5_glu_w[l], 2, 512)
                    are = sb("s5are", [128, 8], F32)
                    aim = sb("s5aim", [128, 8], F32)
                    ldt = sb("s5ldt", [128, 8], F32)
                    Bre = sb("s5Bre", [128, 8, 16], F32)
                    Bim = sb("s5Bim", [128, 8, 16], F32)
                    Cld = [sb("s5Cld%d" % i, [128, 128], F32) for i in range(2)]
                    CT = [sb("s5CT%d" % i, [128, 8, 16], F32) for i in range(2)]
                    Dsk = sb("s5D", [128, 2], F32)
                    A("sp", lambda e: e.dma_start(out=are[:], in_=s5_a_re[l].rearrange("(gt gl) p -> (gl p) gt", gl=2)), writes=["are"], dma=True)
                    A("sp", lambda e: e.dma_start(out=aim[:], in_=s5_a_im[l].rearrange("(gt gl) p -> (gl p) gt", gl=2)), writes=["aim"], dma=True)
                    for gl in range(2):
                        A("sp", lambda e, gl=gl: e.dma_start(out=ldt[gl * 64:(gl + 1) * 64, :],
                                                            in_=s5_log_dt[l].rearrange("(gt gl) -> gl gt", gl=2)[gl].partition_broadcast(64)),
                          writes=["ldt"], dma=True)
                    A("sp", lambda e: e.dma_start(out=Bre[:], in_=s5_b_re[l].rearrange("(gt gl) p h -> (gl p) gt h", gl=2)), writes=["Bre"], dma=True)
                    A("sp", lambda e: e.dma_start(out=Bim[:], in_=s5_b_im[l].rearrange("(gt gl) p h -> (gl p) gt h", gl=2)), writes=["Bim"], dma=True)
                    for ri, src in enumerate((s5_c_re, s5_c_im)):
                        for gt in range(8):
                            A("sp", lambda e, ri=ri, src=src, gt=gt: e.dma_start(
                                out=Cld[ri][gt * 16:(gt + 1) * 16, :].rearrange("h (gl p) -> h gl p", gl=2),
                                in_=src[l, 2 * gt:2 * gt + 2].rearrange("gl h p -> h gl p")),
                              writes=[("Cld", ri)], dma=True)
                    A("sp", lambda e: e.dma_start(out=Dsk[:], in_=s5_d[l].rearrange("(t p) -> p t", p=128)), writes=["Dsk"], dma=True)
                    pct = ps("s5pct", [128, 4, 128], F32)
                    for ri in range(2):
                        A("pe", lambda e, ri=ri: e.transpose(out=pct[:, ri, :], in_=Cld[ri][:], identity=identf[:]),
                          reads=[("Cld", ri), "identf"], writes=["pct"])
                        A("act", lambda e, ri=ri: e.copy(out=CT[ri][:].rearrange("p a b -> p (a b)"), in_=pct[:, ri, :]), reads=["pct"], writes=[("CT", ri)])

                    def sm(name):
                        return sb("s5_" + name, [128, 8], F32)
                    dt_, adr, mag, th, th2, sn, cs_, lr, li, den, t8, rden, nr, fr, fi = [sm(n) for n in
                        ("dt", "adr", "mag", "th", "th2", "sn", "cs", "lr", "li", "den", "t8", "rden", "nr", "fr", "fi")]
                    tf8 = sm("tf8")
                    ti8 = sb("s5_ti8", [128, 8], I32)

                    def TT(out, a, b, op, eng="dve", r=(), w=()):
                        A(eng, lambda e: e.tensor_tensor(out=out, in0=a, in1=b, op=op), reads=r, writes=w)

                    A("act", lambda e: e.activation(out=dt_[:], in_=ldt[:], func=AF.Exp), reads=["ldt"], writes=["dt"])
                    TT(adr[:], are[:], dt_[:], ALU.mult, r=["are", "dt"], w=["adr"])
                    A("act", lambda e: e.activation(out=mag[:], in_=adr[:], func=AF.Exp), reads=["adr"], writes=["mag"])
                    TT(th[:], aim[:], dt_[:], ALU.mult, r=["aim", "dt"], w=["th"])
                    A("dve", lambda e: e.tensor_scalar(out=th2[:], in0=th[:], scalar1=math.pi / 2, scalar2=None, op0=ALU.add), reads=["th"], writes=["th2"])
                    range_reduce(th[:], tf8[:], ti8[:], "th", "tf8")
                    range_reduce(th2[:], tf8[:], ti8[:], "th2", "tf8")
                    A("act", lambda e: e.activation(out=sn[:], in_=th[:], func=AF.Sin), reads=["th"], writes=["sn"])
                    A("act", lambda e: e.activation(out=cs_[:], in_=th2[:], func=AF.Sin), reads=["th2"], writes=["cs"])
                    TT(lr[:], mag[:], cs_[:], ALU.mult, r=["mag", "cs"], w=["lr"])
                    TT(li[:], mag[:], sn[:], ALU.mult, r=["mag", "sn"], w=["li"])
                    TT(den[:], are[:], are[:], ALU.mult, r=["are"], w=["den"])
                    TT(t8[:], aim[:], aim[:], ALU.mult, r=["aim"], w=["t8"])
                    TT(den[:], den[:], t8[:], ALU.add, r=["den", "t8"], w=["den"])
                    A("dve", lambda e: e.reciprocal(out=rden[:], in_=den[:]), reads=["den"], writes=["rden"])
                    A("dve", lambda e: e.tensor_scalar(out=nr[:], in0=lr[:], scalar1=-1.0, scalar2=None, op0=ALU.add), reads=["lr"], writes=["nr"])
                    TT(fr[:], nr[:], are[:], ALU.mult, r=["nr", "are"], w=["fr"])
                    TT(t8[:], li[:], aim[:], ALU.mult, r=["li", "aim", "den"], w=["t8"])
                    TT(fr[:], fr[:], t8[:], ALU.add, r=["fr", "t8"], w=["fr"])
                    TT(fr[:], fr[:], rden[:], ALU.mult, r=["fr", "rden"], w=["fr"])
                    TT(fi[:], li[:], are[:], ALU.mult, r=["li", "are"], w=["fi"])
                    TT(t8[:], nr[:], aim[:], ALU.mult, r=["nr", "aim", "fr"], w=["t8"])
                    TT(fi[:], fi[:], t8[:], ALU.subtract, r=["fi", "t8"], w=["fi"])
                    TT(fi[:], fi[:], rden[:], ALU.mult, r=["fi", "rden"], w=["fi"])
                    Bbr = sb("s5Bbr", [128, 8, 16], F32)
                    Bbi = sb("s5Bbi", [128, 8, 16], F32)
                    tb = sb("s5tb", [128, 8, 16], F32)
                    frb = fr[:].unsqueeze(2).to_broadcast([128, 8, 16])
                    fib = fi[:].unsqueeze(2).to_broadcast([128, 8, 16])
                    TT(Bbr[:], Bre[:], frb, ALU.mult, r=["Bre", "fr"], w=["Bbr"])
                    TT(tb[:], Bim[:], fib, ALU.mult, r=["Bim", "fi"], w=["tb"])
                    TT(Bbr[:], Bbr[:], tb[:], ALU.subtract, r=["Bbr", "tb"], w=["Bbr"])
                    TT(Bbi[:], Bim[:], frb, ALU.mult, r=["Bim", "fr"], w=["Bbi"])
                    TT(tb[:], Bre[:], fib, ALU.mult, r=["Bre", "fi", "Bbr"], w=["tb"])
                    TT(Bbi[:], Bbi[:], tb[:], ALU.add, r=["Bbi", "tb"], w=["Bbi"])
                    Lr = sb("s5Lr", [128, 17, 8], F32)
                    Li = sb("s5Li", [128, 17, 8], F32)
                    tl = sb("s5tl", [128, 8, 8], F32)
                    A("pool", lambda e: e.memset(Lr[:, 0, :], 1.0), writes=["L"])
                    A("pool", lambda e: e.memset(Li[:, 0, :], 0.0), writes=["L"])
                    A("dve", lambda e: e.tensor_copy(out=Lr[:, 1, :], in_=lr[:]), reads=["lr", "L"], writes=["L"])
                    A("dve", lambda e: e.tensor_copy(out=Li[:, 1, :], in_=li[:]), reads=["li", "L"], writes=["L"])
                    n = 1
                    while n < 16:
                        src_r, src_i = Lr[:, 1:n + 1, :], Li[:, 1:n + 1, :]
                        mr = Lr[:, n:n + 1, :].to_broadcast([128, n, 8])
                        mi = Li[:, n:n + 1, :].to_broadcast([128, n, 8])
                        dr, di = Lr[:, n + 1:2 * n + 1, :], Li[:, n + 1:2 * n + 1, :]
                        tln = tl[:, 0:n, :]
                        TT(dr, src_r, mr, ALU.mult, r=["L"], w=["L"])
                        TT(tln, src_i, mi, ALU.mult, r=["L"], w=["tl"])
                        TT(dr, dr, tln, ALU.subtract, r=["L", "tl"], w=["L"])
                        TT(di, src_r, mi, ALU.mult, r=["L"], w=["L"])
                        TT(tln, src_i, mr, ALU.mult, r=["L"], w=["tl"])
                        TT(di, di, tln, ALU.add, r=["L", "tl"], w=["L"])
                        n *= 2
                    A("dve", lambda e: e.tensor_copy(out=Msr[:, 0, :], in_=Lr[:, 16, :]), reads=["L"], writes=["Ms"])
                    A("dve", lambda e: e.tensor_copy(out=Msi[:, 0, :], in_=Li[:, 16, :]), reads=["L"], writes=["Ms"])
                    for s_ in range(7):
                        TT(Msr[:, s_ + 1, :], Msr[:, s_, :], Msr[:, s_, :], ALU.mult, r=["Ms"], w=["Ms"])
                        TT(t8[:], Msi[:, s_, :], Msi[:, s_, :], ALU.mult, r=["Ms"], w=["t8"])
                        TT(Msr[:, s_ + 1, :], Msr[:, s_ + 1, :], t8[:], ALU.subtract, r=["Ms", "t8"], w=["Ms"])
                        TT(t8[:], Msr[:, s_, :], Msi[:, s_, :], ALU.mult, r=["Ms"], w=["t8"])
                        A("dve", lambda e, s_=s_: e.tensor_scalar(out=Msi[:, s_ + 1, :], in0=t8[:], scalar1=2.0, scalar2=None, op0=ALU.mult),
                          reads=["t8"], writes=["Ms"])
                    A("dve", lambda e: e.tensor_scalar(out=nMsi[:], in0=Msi[:], scalar1=-1.0, scalar2=None, op0=ALU.mult), reads=["Ms"], writes=["nMs"])
                    LB = [sb("s5LB%d" % i, [128, 16, 8, 16], F32) for i in range(2)]
                    CL = [sb("s5CL%d" % i, [128, 16, 8, 16], F32) for i in range(2)]
                    scr = sb("s5scr", [128, 2048], F32)
                    tq = scr[:].rearrange("p (a b c) -> p a b c", a=16, b=8, c=16)
                    shp = [128, 16, 8, 16]
                    Lr0 = Lr[:, 0:16, :].unsqueeze(3).to_broadcast(shp)
                    Li0 = Li[:, 0:16, :].unsqueeze(3).to_broadcast(shp)
                    Lr1 = Lr[:, 1:17, :].unsqueeze(3).to_broadcast(shp)
                    Li1 = Li[:, 1:17, :].unsqueeze(3).to_broadcast(shp)
                    Bbrb = Bbr[:].unsqueeze(1).to_broadcast(shp)
                    Bbib = Bbi[:].unsqueeze(1).to_broadcast(shp)
                    CTrb = CT[0][:].unsqueeze(1).to_broadcast(shp)
                    CTib = CT[1][:].unsqueeze(1).to_broadcast(shp)
                    TT(LB[0][:], Lr0, Bbrb, ALU.mult, r=["L", "Bbr"], w=["LB0"])
                    TT(tq, Li0, Bbib, ALU.mult, r=["L", "Bbi"], w=["tq"])
                    TT(LB[0][:], LB[0][:], tq, ALU.subtract, r=["LB0", "tq"], w=["LB0"])
                    TT(LB[1][:], Lr0, Bbib, ALU.mult, r=["L", "Bbi"], w=["LB1"])
                    TT(tq, Li0, Bbrb, ALU.mult, r=["L", "Bbr", "LB0"], w=["tq"])
                    TT(LB[1][:], LB[1][:], tq, ALU.add, r=["LB1", "tq"], w=["LB1"])
                    TT(CL[0][:], Lr1, CTrb, ALU.mult, r=["L", ("CT", 0), "LB1"], w=["CL0"])
                    TT(tq, Li1, CTib, ALU.mult, r=["L", ("CT", 1), "LB1"], w=["tq"])
                    TT(CL[0][:], CL[0][:], tq, ALU.subtract, r=["CL0", "tq"], w=["CL0"])
                    TT(CL[1][:], Li1, CTrb, ALU.mult, r=["L", ("CT", 0)], w=["CL1"])
                    TT(tq, Lr1, CTib, ALU.mult, r=["L", ("CT", 1), "CL0"], w=["tq"])
                    TT(CL[1][:], CL[1][:], tq, ALU.add, r=["CL1", "tq"], w=["CL1"])
                    A("dve", lambda e: e.tensor_scalar(out=CL[1][:], in0=CL[1][:], scalar1=-1.0, scalar2=None, op0=ALU.mult), reads=["CL1"], writes=["CL1"])
                    Mi = sb("s5Mi", [128, 8, 8], I32)
                    Mk = sb("s5Mk", [128, 8, 8], F32)
                    for ct in range(2):
                        for gl in range(2):
                            A("pool", lambda e, ct=ct, gl=gl: e.iota(Mi[gl * 64:(gl + 1) * 64, 4 * ct:4 * ct + 4, :], pattern=[[-2, 4], [1, 8]],
                                                                    base=-gl, channel_multiplier=0), writes=["Mi"])
                    A("dve", lambda e: e.tensor_scalar(out=Mk[:], in0=Mi[:], scalar1=0.0, scalar2=None, op0=ALU.is_equal), reads=["Mi"], writes=["Mk"])
                    shp4 = [128, 8, 8, 16]
                    Mkb = Mk[:].unsqueeze(3).to_broadcast(shp4)
                    Cexp = sb("s5Cexp", [128, 2, 8, 128], F32)
                    for ri in range(2):
                        TT(Cexp[:, ri].rearrange("p g (q h) -> p g q h", h=16), CT[ri][:].unsqueeze(2).to_broadcast(shp4), Mkb, ALU.mult,
                           r=[("CT", ri), "Mk"], w=["Cexp"])
                    A("dve", lambda e: e.tensor_scalar(out=Cexp[:, 1], in0=Cexp[:, 1], scalar1=-1.0, scalar2=None, op0=ALU.mult), reads=["Cexp"], writes=["Cexp"])
                    for i in range(16):
                        for ri in range(2):
                            TT(Vst[:, i, ri].rearrange("p g (q h) -> p g q h", h=16), CL[ri][:, i].unsqueeze(2).to_broadcast(shp4), Mkb, ALU.mult,
                               eng=("pool" if (2 * i + ri) % 3 else "dve"), r=["CL%d" % ri, "Mk"], w=[("Vst", i, ri)])
                    Aexp0 = sb("s5Aexp0", [128, 2, 8, 128], F32)
                    Aexp = [Aexp0[:], scr[:].rearrange("p (r g n) -> p r g n", r=2, g=8, n=128)]
                    pbd = [ps("s5pbd%d" % i, [128, 512], F32) for i in range(2)]
                    pw = [ps("s5pw%d" % i, [128, 4, 128], F32) for i in range(2)]
                    nw = 0
                    for k in range(16):
                        Ak = Aexp[k % 2]
                        akey = ("Aexp", k % 2)
                        for ri in range(2):
                            TT(Ak[:, ri].rearrange("p g (q h) -> p g q h", h=16), LB[ri][:, k].unsqueeze(2).to_broadcast(shp4), Mkb, ALU.mult,
                               r=["LB%d" % ri, "Mk"], w=[akey, "tq"])
                        for ct in range(2):
                            pb = pbd[(2 * k + ct) % 2]
                            pkey = ("pbd", (2 * k + ct) % 2)
                            n_ = 0
                            for q in range(4):
                                for ri in range(2):
                                    gt = 4 * ct + q
                                    A("pe", lambda e, pb=pb, Ak=Ak, ri=ri, gt=gt, n_=n_: e.matmul(pb[:, 0:128], lhsT=Ak[:, ri, gt, :], rhs=Cexp[:, ri, gt, :],
                                                                                             start=(n_ == 0), stop=(n_ == 7)),
                                      reads=[akey, "Cexp"], writes=[pkey])
                                    n_ += 1
                            if k == 0:
                                A("dve", lambda e, pb=pb, ct=ct: e.scalar_tensor_tensor(out=BD[:, 0, ct, :], in0=identf[:], scalar=Dsk[:, ct:ct + 1],
                                                                                         in1=pb[:, 0:128], op0=ALU.mult, op1=ALU.add),
                                  reads=[pkey, "identf", "Dsk"], writes=["BD"])
                            else:
                                A("act", lambda e, pb=pb, ct=ct, k=k: e.copy(out=BD[:, k, ct, :], in_=pb[:, 0:128]), reads=[pkey], writes=["BD"])
                        for ri in range(2):
                            for g4 in range(2):
                                pwc = pw[nw % 2]
                                wkey = ("pw", nw % 2)
                                nw += 1
                                for q in range(4):
                                    gt = 4 * g4 + q
                                    A("pe", lambda e, pwc=pwc, Ak=Ak, ri=ri, gt=gt, q=q: e.transpose(out=pwc[:, q, :], in_=Ak[:, ri, gt, :], identity=identf[:]),
                                      reads=[akey, "identf"], writes=[wkey])
                                A("act", lambda e, pwc=pwc, k=k, ri=ri, g4=g4: e.copy(out=Wst[:, k, ri, 4 * g4:4 * g4 + 4, :], in_=pwc[:]),
                                  reads=[wkey], writes=["Wst"])
                    S.emit()
                with ExitStack() as es:
                    sb, ps = mk(es)
                    ub = sb("s5ub", [128, 2, SEQ], BF16)
                    ubv = ub[:].rearrange("p t (j c) -> p t j c", j=16)
                    St = [sb("s5St%d" % i, [128, 8, 256], F32) for i in range(2)]
                    Alt = [sb("s5Alt%d" % i, [128, 256], F32) for i in range(4)]
                    Tmp = [sb("s5Tmp%d" % i, [128, 256], F32) for i in range(4)]
                    z = sb("s5z", [128, 2, SEQ], BF16)
                    zv = z[:].rearrange("p t (c j) -> p t j c", j=16)
                    Stb = [sb("s5Stb%d" % i, [128, 8, 256], BF16) for i in range(2)]
                    yss = sb("s5yss", [128, 2, NB], F32)
                    gtmp = [sb("s5gtmp%d" % i, [128, 256], F32) for i in range(1)] * 2
                    pst = [ps("s5pst%d" % i, [128, 512], F32) for i in range(2)]
                    pso = [ps("s5pso%d" % i, [128, 512], F32) for i in range(2)]
                    pg = [ps("s5pg%d" % i, [128, NB], F32) for i in range(4)]
                    n1 = 0
                    n2 = 0
                    for s in range(n_seq):
                        t0 = s * SEQ
                        A("sp", lambda e, t0=t0: e.dma_start(out=z[:], in_=ubd[:, :, t0:t0 + SEQ].rearrange("t p n -> p t n")), writes=["z"], dma=True)
                        A("act", lambda e: e.copy(out=ubv[:, 0], in_=z[:, 0, :].rearrange("p (c j) -> p j c", j=16)), reads=["z"], writes=["ub"])
                        A("dve", lambda e: e.tensor_copy(out=ubv[:, 1], in_=z[:, 1, :].rearrange("p (c j) -> p j c", j=16)), reads=["z"], writes=["ub"])
                        for gt in range(8):
                            ct = gt // 4
                            for ri in range(2):
                                pc_ = pst[n1 % 2]
                                pk = ("pst", n1 % 2)
                                n1 += 1
                                for j in range(16):
                                    A("pe", lambda e, pc_=pc_, j=j, ri=ri, gt=gt, ct=ct: e.matmul(pc_[:, 0:256], lhsT=Wst[:, 15 - j, ri, gt, :], rhs=ubv[:, ct, j, :],
                                                                                             start=(j == 0), stop=(j == 15)),
                                      reads=["ub"], writes=[pk])
                                A("act", lambda e, pc_=pc_, ri=ri, gt=gt: e.copy(out=St[ri][:, gt, :], in_=pc_[:, 0:256]), reads=[pk], writes=[("St", gt)])
                        def hs_step(gt, s_, ch):
                            d = 1 << s_
                            A0, A1, T0, T1 = Alt[2 * ch], Alt[2 * ch + 1], Tmp[2 * ch], Tmp[2 * ch + 1]
                            ak, t0k, t1k = ("Alt", ch), ("Tmp0", ch), ("Tmp1", ch)
                            if s_ % 2 == 0:
                                sr, si, dr, di = St[0][:, gt, :], St[1][:, gt, :], A0[:], A1[:]
                                skey, dkey = ("St", gt), ak
                            else:
                                sr, si, dr, di = A0[:], A1[:], St[0][:, gt, :], St[1][:, gt, :]
                                skey, dkey = ak, ("St", gt)
                            mr, mi, nmi = Msr[:, s_, gt:gt + 1], Msi[:, s_, gt:gt + 1], nMsi[:, s_, gt:gt + 1]

                            def STT(out, a, sc, b, r, w):
                                A("dve", lambda e: e.scalar_tensor_tensor(out=out, in0=a, scalar=sc, in1=b, op0=ALU.mult, op1=ALU.add), reads=r, writes=w)
                            STT(T0[:, d:256], sr[:, 0:256 - d], mr, sr[:, d:256], [skey], [t0k])
                            STT(T1[:, d:256], si[:, 0:256 - d], mr, si[:, d:256], [skey], [t1k])
                            STT(dr[:, d:256], si[:, 0:256 - d], nmi, T0[:, d:256], [skey, t0k], [dkey])
                            STT(di[:, d:256], sr[:, 0:256 - d], mi, T1[:, d:256], [skey, t1k], [dkey])
                            A("pool", lambda e: e.tensor_copy(out=dr[:, 0:d], in_=sr[:, 0:d]), reads=[skey], writes=[dkey])
                            A("pool", lambda e: e.tensor_copy(out=di[:, 0:d], in_=si[:, 0:d]), reads=[skey], writes=[dkey])

                        def hs_pair(gta, gtb):
                            for s_ in range(8):
                                hs_step(gta, s_, 0)
                                hs_step(gtb, s_, 1)
                            for gt in (gta, gtb):
                                for ri in range(2):
                                    A("act" if ri == 0 else "pool", lambda e, ri=ri, gt=gt: (e.copy if ri == 0 else e.tensor_copy)(out=Stb[ri][:, gt, :], in_=St[ri][:, gt, :]),
                                      reads=[("St", gt)], writes=[("Stb", gt)])

                        def y_tile(i, ct):
                            nonlocal n2
                            po_ = pso[n2 % 2]
                            ok = ("pso", n2 % 2)
                            n2 += 1
                            for j in range(i + 1):
                                A("pe", lambda e, po_=po_, i=i, j=j, ct=ct: e.matmul(po_[:, 0:256], lhsT=BD[:, i - j, ct, :], rhs=ubv[:, ct, j, :],
                                                                                start=(j == 0), stop=False, skip_group_check=True),
                                  reads=["ub"], writes=[ok])
                            n_ = 0
                            for q in range(4):
                                for ri in range(2):
                                    gt = 4 * ct + q
                                    A("pe", lambda e, po_=po_, i=i, ri=ri, gt=gt, n_=n_: e.matmul(po_[:, 1:256], lhsT=Vst[:, i, ri, gt, :], rhs=Stb[ri][:, gt, 0:255],
                                                                                             start=False, stop=(n_ == 7), skip_group_check=True),
                                      reads=[("Stb", gt)], writes=[ok])
                                    n_ += 1
                            gt_ = gtmp[n2 % 2]
                            gk = ("gtmp", 0)
                            A("act", lambda e, po_=po_, gt_=gt_: e.activation(out=gt_[:], in_=po_[:, 0:256], func=AF.Square), reads=[ok], writes=[gk])
                            A("dve", lambda e, gt_=gt_: e.tensor_scalar(out=gt_[:], in0=gt_[:], scalar1=0.044715, scalar2=1.0, op0=ALU.mult, op1=ALU.add),
                              reads=[gk], writes=[gk])
                            A("dve", lambda e, po_=po_, gt_=gt_: e.tensor_tensor(out=gt_[:], in0=po_[:, 0:256], in1=gt_[:], op=ALU.mult), reads=[gk, ok], writes=[gk])
                            A("act", lambda e, gt_=gt_: e.activation(out=gt_[:], in_=gt_[:], func=AF.Sigmoid, scale=1.5957691216057308), reads=[gk], writes=[gk])
                            A("dve", lambda e, po_=po_, gt_=gt_, i=i, ct=ct: e.tensor_tensor(out=zv[:, ct, i, :], in0=po_[:, 0:256], in1=gt_[:], op=ALU.mult),
                              reads=[gk, ok], writes=["z"])

                        hs_pair(0, 1)
                        hs_pair(2, 3)
                        for i in range(16):
                            y_tile(i, 0)
                            if i == 1:
                                hs_pair(4, 5)
                            if i == 8:
                                hs_pair(6, 7)
                        for i in range(16):
                            y_tile(i, 1)
                        for c in range(SEQ // NB):
                            cs = slice(c * NB, (c + 1) * NB)
                            for mt in range(4):
                                for kt in range(2):
                                    A("pe", lambda e, mt=mt, kt=kt, cs=cs: e.matmul(pg[mt][:], lhsT=wglu[:, kt, mt * 128:(mt + 1) * 128], rhs=z[:, kt, cs],
                                                                               start=(kt == 0), stop=(kt == 1)),
                                      reads=["z"], writes=[("pg", mt)])
                            for t in range(2):
                                A("act", lambda e, t=t: e.activation(out=yss[:, t, :], in_=pg[2 + t][:], func=AF.Sigmoid), reads=[("pg", 2 + t)], writes=["yss"])
                                A("dve", lambda e, t=t: e.tensor_tensor(out=yss[:, t, :], in0=pg[t][:], in1=yss[:, t, :], op=ALU.mult),
                                  reads=[("pg", t), "yss"], writes=["yss"])
                            A("pool", lambda e, t0=t0, c=c: e.dma_start(out=ymix[4:6, :, t0 + c * NB:t0 + (c + 1) * NB].rearrange("t p n -> p t n"), in_=yss[:]),
                              reads=["yss"], writes=["ymix"], dma=True)
                    S.emit()

        def phase_m3a(l):
            xsrc = x_in if l == 0 else xs1
            with ExitStack() as es:
                sb, ps = mk(es)
                idi, identb, identf = make_ident(sb, "m3")
                wo = sb("m3wo", [128, 8, D], BF16)
                load_w_bf16(wo, "wo", w_out[l], 8, D)
                gg = sb("m3gg", [128, 8], F32)
                A("sp", lambda e: e.dma_start(out=gg[:], in_=grp_norm_g[l].rearrange("(t p) -> p t", p=128)), writes=["gg"], dma=True)
                gb2 = sb("m3gb2", [128, D], F32)
                A("sp", lambda e: e.dma_start(out=gb2[:], in_=norm_ffn_g[l].partition_broadcast(128)), writes=["gb2"], dma=True)
                ones256 = sb("m3ones", [128, 128], BF16)
                A("pool", lambda e: e.memset(ones256[:], 1.0 / 256), writes=["ones256"])
                ym = [sb("m3ym%d" % i, [128, 8, NB], F32) for i in range(2)]
                sqb = sb("m3sqb", [128, 8, NB], BF16)
                rs = [sb("m3rs%d" % i, [128, NB], F32) for i in range(2)]
                mixT = [sb("m3mixT%d" % i, [128, 8, NB], BF16) for i in range(2)]
                xt = [sb("m3xt%d" % i, [128, D], F32) for i in range(8)]
                sqj = sb("m3sqj", [128, D], BF16)
                ss = sb("m3ss", [128, 4], F32)
                rstd = sb("m3rstd", [128, 4], F32)
                hb = [sb("m3h%d" % i, [128, D], BF16) for i in range(4)]
                h2T = [sb("m3h2T%d" % i, [128, 8, NB], BF16) for i in range(2)]
                pstat = [ps("m3pstat%d" % i, [128, NB], F32) for i in range(2)]
                po = [ps("m3po%d" % i, [128, NB], F32) for i in range(4)]
                pt = [ps("m3pt%d" % i, [128, 8, 128], BF16) for i in range(2)]

                def L1(b):
                    tok0 = b * NB
                    ymb = ym[b % 2]
                    yk = ("ym", b % 2)
                    A("sp", lambda e: e.dma_start(out=ymb[:], in_=ymix[:, :, tok0:tok0 + NB].rearrange("t p n -> p t n")),
                      writes=[yk], dma=True)
                    for tile in range(8):
                        A("act", lambda e, tile=tile: e.activation(out=sqb[:, tile, :], in_=ymb[:, tile, :], func=AF.Square),
                          reads=[yk], writes=[("sqb", tile)])

                def L2(b):
                    ymb = ym[b % 2]
                    yk = ("ym", b % 2)
                    mx = mixT[b % 2]
                    for grp in range(4):
                        for t in range(2):
                            tile = 2 * grp + t
                            A("pe", lambda e, grp=grp, tile=tile, t=t: e.matmul(pstat[grp % 2][:], lhsT=ones256[:], rhs=sqb[:, tile, :], start=(t == 0), stop=(t == 1)),
                              reads=[("sqb", tile), "ones256"], writes=[("pstat", grp % 2)])
                        rsb = rs[grp % 2]
                        rk = ("rs", grp % 2)
                        A("act", lambda e, grp=grp, rsb=rsb: e.activation(out=rsb[:], in_=pstat[grp % 2][:], func=AF.Ln, bias=EPS),
                          reads=[("pstat", grp % 2)], writes=[rk])
                        A("act", lambda e, rsb=rsb: e.activation(out=rsb[:], in_=rsb[:], func=AF.Exp, scale=-0.5), reads=[rk], writes=[rk])
                        for t in range(2):
                            tile = 2 * grp + t
                            A("dve", lambda e, tile=tile, rsb=rsb: e.scalar_tensor_tensor(out=mx[:, tile, :], in0=ymb[:, tile, :], scalar=gg[:, tile:tile + 1],
                                                                                      in1=rsb[:], op0=ALU.mult, op1=ALU.mult),
                              reads=[yk, rk, "gg"], writes=[("mixT", b % 2, tile)])

                def OPm(b, tt):
                    mx = mixT[b % 2]
                    xb = xt[(b % 2) * 4 + tt]
                    xk = ("xt", b % 2, tt)
                    r0 = b * NB + tt * 128
                    for nh in range(2):
                        pi_ = (2 * tt + nh) % 4
                        for kt in range(8):
                            A("pe", lambda e, pi_=pi_, kt=kt, nh=nh: e.matmul(po[pi_][:], lhsT=mx[:, kt, tt * 128:(tt + 1) * 128],
                                                                              rhs=wo[:, kt, nh * NB:(nh + 1) * NB], start=(kt == 0), stop=(kt == 7)),
                              reads=[("mixT", b % 2, kt)] + wkeys("wo", 8, 0), writes=[("po", pi_)])
                        A("dve", lambda e, pi_=pi_, nh=nh: e.tensor_tensor(out=xb[:, nh * NB:(nh + 1) * NB], in0=po[pi_][:], in1=xb[:, nh * NB:(nh + 1) * NB], op=ALU.add),
                          reads=[("po", pi_), xk], writes=[xk])
                    A("pool", lambda e: e.dma_start(out=xmid[r0:r0 + 128, :], in_=xb[:]), reads=[xk], writes=["xmid"], dma=True)
                    A("act", lambda e: e.activation(out=sqj[:], in_=xb[:], func=AF.Square, accum_out=ss[:, tt:tt + 1]),
                      reads=[xk], writes=["sqj", ("ss", tt)])
                    A("act", lambda e: e.activation(out=rstd[:, tt:tt + 1], in_=ss[:, tt:tt + 1], func=AF.Ln, scale=1.0 / D, bias=EPS),
                      reads=[("ss", tt)], writes=[("rstd", tt)])
                    A("act", lambda e: e.activation(out=rstd[:, tt:tt + 1], in_=rstd[:, tt:tt + 1], func=AF.Exp, scale=-0.5), reads=[("rstd", tt)], writes=[("rstd", tt)])
                    A("dve", lambda e: e.scalar_tensor_tensor(out=hb[tt][:], in0=xb[:], scalar=rstd[:, tt:tt + 1], in1=gb2[:],
                                                              op0=ALU.mult, op1=ALU.mult),
                      reads=[xk, ("rstd", tt), "gb2"], writes=[("hb", tt)])

                def LX(b):
                    for tt in range(4):
                        xb = xt[(b % 2) * 4 + tt]
                        r0 = b * NB + tt * 128
                        A("sp", lambda e, xb=xb, r0=r0: e.dma_start(out=xb[:], in_=xsrc[r0:r0 + 128, :]), writes=[("xt", b % 2, tt)], dma=True)

                def TR(b, tt):
                    for kt in range(8):
                        A("pe", lambda e, kt=kt: e.transpose(out=pt[tt % 2][:, kt, :], in_=hb[tt][:, kt * 128:(kt + 1) * 128], identity=identb[:]),
                          reads=[("hb", tt), "identb"], writes=[("pt", tt % 2)])
                    A("act", lambda e: e.copy(out=h2T[b % 2][:, :, tt * 128:(tt + 1) * 128], in_=pt[tt % 2][:]),
                      reads=[("pt", tt % 2)], writes=[("h2T", b % 2)])
                    if tt == 3:
                        tok0 = b * NB
                        A("pool", lambda e: e.dma_start(out=h2Td[:, :, tok0:tok0 + NB].rearrange("t p n -> p t n"), in_=h2T[b % 2][:]),
                          reads=[("h2T", b % 2)], writes=["h2Td"], dma=True)

                L1(0)
                LX(0)
                L2(0)
                for b in range(NBLK):
                    if b + 1 < NBLK:
                        L1(b + 1)
                        LX(b + 1)
                    OPm(b, 0)
                    OPm(b, 1)
                    TR(b, 0)
                    OPm(b, 2)
                    TR(b, 1)
                    if b + 1 < NBLK:
                        L2(b + 1)
                    OPm(b, 3)
                    TR(b, 2)
                    TR(b, 3)
                S.emit()

        def phase_m3b(l, last):
            with ExitStack() as es:
                sb, ps = mk(es)
                wg = sb("m4wg", [128, 8, DFF], BF16)
                wu = sb("m4wu", [128, 8, DFF], BF16)
                wd = sb("m4wd", [128, 22, D], BF16)
                load_w_bf16(wg, "wg", w_gate[l], 8, DFF)
                load_w_bf16(wu, "wu", w_up[l], 8, DFF)
                load_w_bf16(wd, "wd", w_down[l], 22, D)
                gbf = sb("m4gbf", [128, D], F32)
                if last:
                    A("sp", lambda e: e.dma_start(out=gbf[:], in_=final_norm_g.partition_broadcast(128)), writes=["gbf"], dma=True)
                h2T = [sb("m4h2T%d" % i, [128, 8, NB], BF16) for i in range(2)]
                ffT = sb("m4ffT", [128, 22, NB], BF16)
                sl = [sb("m4sl%d" % i, [128, NB], F32) for i in range(2)]
                xt = [sb("m4xt%d" % i, [128, D], F32) for i in range(4)]
                sqj = sb("m4sqj", [128, D], BF16)
                ss = sb("m4ss", [128, 4], F32)
                rstd = sb("m4rstd", [128, 4], F32)
                pgt = [ps("m4pg%d" % i, [128, NB], F32) for i in range(2)]
                put = [ps("m4pu%d" % i, [128, NB], F32) for i in range(2)]
                pd = [ps("m4pd%d" % i, [128, NB], F32) for i in range(4)]
                nx = 0
                xdst = out if last else xs1
                for b in range(NBLK):
                    tok0 = b * NB
                    hb = h2T[b % 2]
                    hk = ("h2T", b % 2)
                    A("sp", lambda e, hb=hb, tok0=tok0: e.dma_start(out=hb[:], in_=h2Td[:, :, tok0:tok0 + NB].rearrange("t p n -> p t n")), writes=[hk], dma=True)
                    for tt in range(4):
                        A("sp", lambda e, tt=tt, tok0=tok0: e.dma_start(out=xt[tt][:], in_=xmid[tok0 + tt * 128:tok0 + (tt + 1) * 128, :]), writes=[("xt", tt)], dma=True)
                    for ft in range(22):
                        gi = ft % 2
                        for kt in range(8):
                            A("pe", lambda e, gi=gi, kt=kt, ft=ft, hb=hb: e.matmul(pgt[gi][:], lhsT=wg[:, kt, ft * 128:(ft + 1) * 128], rhs=hb[:, kt, :],
                                                                              start=(kt == 0), stop=(kt == 7)),
                              reads=wkeys("wg", 8, ft * 128) + [hk], writes=[("pgt", gi)])
                        for kt in range(8):
                            A("pe", lambda e, gi=gi, kt=kt, ft=ft, hb=hb: e.matmul(put[gi][:], lhsT=wu[:, kt, ft * 128:(ft + 1) * 128], rhs=hb[:, kt, :],
                                                                              start=(kt == 0), stop=(kt == 7)),
                              reads=wkeys("wu", 8, ft * 128) + [hk], writes=[("put", gi)])
                        A("act", lambda e, gi=gi: e.activation(out=sl[gi][:], in_=pgt[gi][:], func=AF.Silu), reads=[("pgt", gi)], writes=[("sl", gi)])
                        A("dve", lambda e, gi=gi, ft=ft: e.tensor_tensor(out=ffT[:, ft, :], in0=put[gi][:], in1=sl[gi][:], op=ALU.mult),
                          reads=[("put", gi), ("sl", gi)], writes=[("ffT", ft)])
                    for tt in range(4):
                        xb = xt[tt]
                        xk = ("xt", tt)
                        r0 = tok0 + tt * 128
                        for nh in range(2):
                            pi_ = (2 * tt + nh) % 4
                            for ft in range(22):
                                A("pe", lambda e, pi_=pi_, ft=ft, tt=tt, nh=nh: e.matmul(pd[pi_][:], lhsT=ffT[:, ft, tt * 128:(tt + 1) * 128],
                                                                                       rhs=wd[:, ft, nh * NB:(nh + 1) * NB], start=(ft == 0), stop=(ft == 21)),
                                  reads=[("ffT", ft)] + wkeys("wd", 22, 0), writes=[("pd", pi_)])
                            A("dve", lambda e, pi_=pi_, xb=xb, nh=nh: e.tensor_tensor(out=xb[:, nh * NB:(nh + 1) * NB], in0=pd[pi_][:], in1=xb[:, nh * NB:(nh + 1) * NB], op=ALU.add),
                              reads=[("pd", pi_), xk], writes=[xk])
                        if last:
                            A("act", lambda e, xb=xb, tt=tt: e.activation(out=sqj[:], in_=xb[:], func=AF.Square, accum_out=ss[:, tt:tt + 1]),
                              reads=[xk], writes=["sqj", "ss"])
                            A("act", lambda e, tt=tt: e.activation(out=rstd[:, tt:tt + 1], in_=ss[:, tt:tt + 1], func=AF.Ln, scale=1.0 / D, bias=EPS),
                              reads=["ss"], writes=["rstd"])
                            A("act", lambda e, tt=tt: e.activation(out=rstd[:, tt:tt + 1], in_=rstd[:, tt:tt + 1], func=AF.Exp, scale=-0.5), reads=["rstd"], writes=["rstd"])
                            A("dve", lambda e, xb=xb, tt=tt: e.scalar_tensor_tensor(out=xb[:], in0=xb[:], scalar=rstd[:, tt:tt + 1], in1=gbf[:],
                                                                                      op0=ALU.mult, op1=ALU.mult),
                              reads=[xk, "rstd", "gbf"], writes=[xk])
                        A("pool", lambda e, xb=xb, r0=r0: e.dma_start(out=xdst[r0:r0 + 128, :], in_=xb[:]), reads=[xk], writes=["xdst"], dma=True)
                S.emit()

        HF = DFF // 2

        def phase_mf1(l):
            xsrc = x_in if l == 0 else xs1
            with ExitStack() as es:
                sb, ps = mk(es)
                idi, identb, identf = make_ident(sb, "f1")
                wo = sb("f1wo", [128, 8, D], BF16)
                load_w_bf16(wo, "wo", w_out[l], 8, D)
                gg = sb("f1gg", [128, 8], F32)
                ggrow = sb("f1ggrow", [8, 128], F32)
                A("act", lambda e: e.dma_start(out=ggrow[:], in_=grp_norm_g[l].rearrange("(t p) -> t p", p=128)), writes=["ggrow"], dma=True)
                gb2 = sb("f1gb2", [128, D], F32)
                A("act", lambda e: e.dma_start(out=gb2[:], in_=norm_ffn_g[l].partition_broadcast(128)), writes=["gb2"], dma=True)
                ones256 = sb("f1ones", [128, 128], BF16)
                A("pool", lambda e: e.memset(ones256[:], 1.0 / 256), writes=["ones256"])
                wg = sb("f1wg", [128, 8, HF], BF16)
                wu = sb("f1wu", [128, 8, HF], BF16)
                wd = sb("f1wd", [128, 11, D], BF16)
                load_w_bf16(wg, "wg", w_gate[l][:, 0:HF], 8, HF)
                load_w_bf16(wu, "wu", w_up[l][:, 0:HF], 8, HF)
                load_w_bf16(wd, "wd", w_down[l][0:HF, :], 11, D)
                ym = sb("f1ym", [128, 8, NB], F32)
                sqb = sb("f1sqb", [128, 8, NB], BF16)
                rs = [sb("f1rs%d" % i, [128, NB], F32) for i in range(2)]
                mixT = sb("f1mixT", [128, 8, NB], BF16)
                xt = [sb("f1xt%d" % i, [128, D], F32) for i in range(8)]
                sqj = sb("f1sqj", [128, D], BF16)
                ss = sb("f1ss", [128, 4], F32)
                rstd = sb("f1rstd", [128, 4], F32)
                hb = [sb("f1h%d" % i, [128, D], BF16) for i in range(4)]
                h2T = [sb("f1h2T%d" % i, [128, 8, NB], BF16) for i in range(2)]
                ffT = sb("f1ffT", [128, 11, NB], BF16)
                sl = sb("f1sl", [128, NB], F32)
                pstat = ps("f1pstat", [128, NB], F32)
                po = [ps("f1po%d" % i, [128, NB], F32) for i in range(2)]
                pt = ps("f1pt", [128, 8, 128], BF16)
                pgt2 = [ps("f1pg%d" % i, [128, NB], F32) for i in range(2)]
                put2 = [ps("f1pu%d" % i, [128, NB], F32) for i in range(2)]
                sl2 = [sl, sb("f1sl1", [128, NB], F32)]

                A("pe", lambda e: e.transpose(out=pstat[:, 0:8], in_=ggrow[:], identity=identf[0:8, 0:8]), reads=["ggrow", "identf"], writes=["pstat"])
                A("dve", lambda e: e.tensor_copy(out=gg[:], in_=pstat[:, 0:8]), reads=["pstat"], writes=["gg"])

                def L1(b):
                    tok0 = b * NB
                    A("sp", lambda e: e.dma_start(out=ym[:], in_=ymix[:, :, tok0:tok0 + NB].rearrange("t p n -> p t n")),
                      writes=["ym"], dma=True)
                    for tile in range(8):
                        A("act", lambda e, tile=tile: e.activation(out=sqb[:, tile, :], in_=ym[:, tile, :], func=AF.Square),
                          reads=["ym"], writes=[("sqb", tile)])

                def LX(b):
                    for tt in range(4):
                        xb = xt[(b % 2) * 4 + tt]
                        r0 = b * NB + tt * 128
                        A("sp", lambda e, xb=xb, r0=r0: e.dma_start(out=xb[:], in_=xsrc[r0:r0 + 128, :]), writes=[("xt", b % 2, tt)], dma=True)

                def L2(b):
                    for grp in range(4):
                        for t in range(2):
                            tile = 2 * grp + t
                            A("pe", lambda e, tile=tile, t=t: e.matmul(pstat[:], lhsT=ones256[:], rhs=sqb[:, tile, :], start=(t == 0), stop=(t == 1)),
                              reads=[("sqb", tile), "ones256"], writes=["pstat"])
                        rsb = rs[grp % 2]
                        rk = ("rs", grp % 2)
                        A("act", lambda e, rsb=rsb: e.activation(out=rsb[:], in_=pstat[:], func=AF.Ln, bias=EPS), reads=["pstat"], writes=[rk])
                        A("act", lambda e, rsb=rsb: e.activation(out=rsb[:], in_=rsb[:], func=AF.Exp, scale=-0.5), reads=[rk], writes=[rk])
                        for t in range(2):
                            tile = 2 * grp + t
                            A("dve", lambda e, tile=tile, rsb=rsb: e.scalar_tensor_tensor(out=mixT[:, tile, :], in0=ym[:, tile, :], scalar=gg[:, tile:tile + 1],
                                                                                      in1=rsb[:], op0=ALU.mult, op1=ALU.mult),
                              reads=["ym", rk, "gg"], writes=[("mixT", tile)])

                def OPm(b, tt):
                    xb = xt[(b % 2) * 4 + tt]
                    xk = ("xt", b % 2, tt)
                    for nh in range(2):
                        for kt in range(8):
                            A("pe", lambda e, kt=kt, nh=nh: e.matmul(po[nh][:], lhsT=mixT[:, kt, tt * 128:(tt + 1) * 128],
                                                                     rhs=wo[:, kt, nh * NB:(nh + 1) * NB], start=(kt == 0), stop=(kt == 7)),
                              reads=[("mixT", kt)] + wkeys("wo", 8, 0), writes=[("po", nh)])
                        A("dve", lambda e, nh=nh: e.tensor_tensor(out=xb[:, nh * NB:(nh + 1) * NB], in0=po[nh][:], in1=xb[:, nh * NB:(nh + 1) * NB], op=ALU.add),
                          reads=[("po", nh), xk], writes=[xk])
                    A("act", lambda e: e.activation(out=sqj[:], in_=xb[:], func=AF.Square, accum_out=ss[:, tt:tt + 1]),
                      reads=[xk], writes=["sqj", ("ss", tt)])
                    A("act", lambda e: e.activation(out=rstd[:, tt:tt + 1], in_=ss[:, tt:tt + 1], func=AF.Ln, scale=1.0 / D, bias=EPS),
                      reads=[("ss", tt)], writes=[("rstd", tt)])
                    A("act", lambda e: e.activation(out=rstd[:, tt:tt + 1], in_=rstd[:, tt:tt + 1], func=AF.Exp, scale=-0.5), reads=[("rstd", tt)], writes=[("rstd", tt)])
                    A("dve", lambda e: e.scalar_tensor_tensor(out=hb[tt][:], in0=xb[:], scalar=rstd[:, tt:tt + 1], in1=gb2[:],
                                                              op0=ALU.mult, op1=ALU.mult),
                      reads=[xk, ("rstd", tt), "gb2"], writes=[("hb", tt)])

                def TR(b, tt):
                    for kt in range(8):
                        A("pe", lambda e, kt=kt: e.transpose(out=pt[:, kt, :], in_=hb[tt][:, kt * 128:(kt + 1) * 128], identity=identb[:]),
                          reads=[("hb", tt), "identb"], writes=["pt"])
                    A("act", lambda e: e.copy(out=h2T[b % 2][:, :, tt * 128:(tt + 1) * 128], in_=pt[:]),
                      reads=["pt"], writes=[("h2T", b % 2)])
                    if tt == 3:
                        tok0 = b * NB
                        A("pool", lambda e: e.dma_start(out=h2Td[:, :, tok0:tok0 + NB].rearrange("t p n -> p t n"), in_=h2T[b % 2][:]),
                          reads=[("h2T", b % 2)], writes=["h2Td"], dma=True)

                def GU(b, ft):
                    hk = ("h2T", b % 2)
                    hsrc = h2T[b % 2]
                    gi = ft % 2
                    pgt, put, slb = pgt2[gi], put2[gi], sl2[gi]
                    for kt in range(8):
                        A("pe", lambda e, kt=kt: e.matmul(pgt[:], lhsT=wg[:, kt, ft * 128:(ft + 1) * 128], rhs=hsrc[:, kt, :], start=(kt == 0), stop=(kt == 7)),
                          reads=wkeys("wg", 8, ft * 128) + [hk], writes=[("pgt", gi)])
                    for kt in range(8):
                        A("pe", lambda e, kt=kt: e.matmul(put[:], lhsT=wu[:, kt, ft * 128:(ft + 1) * 128], rhs=hsrc[:, kt, :], start=(kt == 0), stop=(kt == 7)),
                          reads=wkeys("wu", 8, ft * 128) + [hk], writes=[("put", gi)])
                    A("act", lambda e: e.activation(out=slb[:], in_=pgt[:], func=AF.Silu), reads=[("pgt", gi)], writes=[("sl", gi)])
                    A("dve", lambda e: e.tensor_tensor(out=ffT[:, ft, :], in0=put[:], in1=slb[:], op=ALU.mult),
                      reads=[("put", gi), ("sl", gi)], writes=[("ffT", ft)])

                def DN(b, tt):
                    xb = xt[(b % 2) * 4 + tt]
                    xk = ("xt", b % 2, tt)
                    r0 = b * NB + tt * 128
                    for nh in range(2):
                        for ft in range(11):
                            A("pe", lambda e, ft=ft, nh=nh: e.matmul(po[nh][:], lhsT=ffT[:, ft, tt * 128:(tt + 1) * 128],
                                                                     rhs=wd[:, ft, nh * NB:(nh + 1) * NB], start=(ft == 0), stop=(ft == 10)),
                              reads=[("ffT", ft)] + wkeys("wd", 11, 0), writes=[("po", nh)])
                        A("dve", lambda e, nh=nh: e.tensor_tensor(out=xb[:, nh * NB:(nh + 1) * NB], in0=po[nh][:], in1=xb[:, nh * NB:(nh + 1) * NB], op=ALU.add),
                          reads=[("po", nh), xk], writes=[xk])
                    A("pool", lambda e: e.dma_start(out=xmid[r0:r0 + 128, :], in_=xb[:]), reads=[xk], writes=["xmid"], dma=True)

                def m3a_items(b):
                    return [lambda: L2(b), lambda: OPm(b, 0), lambda: OPm(b, 1), lambda: TR(b, 0), lambda: OPm(b, 2),
                            lambda: TR(b, 1), lambda: OPm(b, 3), lambda: TR(b, 2), lambda: TR(b, 3)]

                L1(0)
                LX(0)
                for it in m3a_items(0):
                    it()
                for b in range(NBLK):
                    nxt = []
                    if b + 1 < NBLK:
                        L1(b + 1)
                        LX(b + 1)
                        nxt = m3a_items(b + 1)
                    for ft in range(11):
                        GU(b, ft)
                        if ft < len(nxt):
                            nxt[ft]()
                    for tt in range(4):
                        DN(b, tt)
                S.emit()

        def phase_mf2(l, last, prefetch=None):
            with ExitStack() as es:
                sb, ps = mk(es)
                wg = sb("f2wg", [128, 8, HF], BF16)
                wu = sb("f2wu", [128, 8, HF], BF16)
                wd = sb("f2wd", [128, 11, D], BF16)
                load_w_bf16(wg, "wg", w_gate[l][:, HF:DFF], 8, HF)
                load_w_bf16(wu, "wu", w_up[l][:, HF:DFF], 8, HF)
                load_w_bf16(wd, "wd", w_down[l][HF:DFF, :], 11, D)
                gbf = sb("f2gbf", [128, D], F32)
                if last:
                    A("act", lambda e: e.dma_start(out=gbf[:], in_=final_norm_g.partition_broadcast(128)), writes=["gbf"], dma=True)
                h2T = [sb("f2h2T%d" % i, [128, 8, NB], BF16) for i in range(2)]
                ffT = sb("f2ffT", [128, 11, NB], BF16)
                sl = [sb("f2sl%d" % i, [128, NB], F32) for i in range(2)]
                xt = [sb("f2xt%d" % i, [128, D], F32) for i in range(8)]
                sqj = sb("f2sqj", [128, D], BF16)
                ss = sb("f2ss", [128, 4], F32)
                rstd = sb("f2rstd", [128, 4], F32)
                pgt = [ps("f2pg%d" % i, [128, NB], F32) for i in range(2)]
                put = [ps("f2pu%d" % i, [128, NB], F32) for i in range(2)]
                pd = [ps("f2pd%d" % i, [128, NB], F32) for i in range(4)]
                xdst = out if last else xs1

                def LD(b):
                    tok0 = b * NB
                    A("sp", lambda e: e.dma_start(out=h2T[b % 2][:], in_=h2Td[:, :, tok0:tok0 + NB].rearrange("t p n -> p t n")), writes=[("h2T", b % 2)], dma=True)
                    for tt in range(4):
                        A("sp", lambda e, tt=tt: e.dma_start(out=xt[(b % 2) * 4 + tt][:], in_=xmid[tok0 + tt * 128:tok0 + (tt + 1) * 128, :]),
                          writes=[("xt", b % 2, tt)], dma=True)

                pf_chunks = []
                if prefetch is not None:
                    wnext, wsrc = prefetch
                    stgp = [sb("f2stg%d" % i, [128, 1024], F32) for i in range(2)]
                    wvn = wsrc.rearrange("(kt p) n -> p kt n", p=128)
                    for cc, (c0, c1) in enumerate(((0, 1024), (1024, 2048), (2048, NIN))):
                        for kt in range(8):
                            pf_chunks.append((kt, c0, c1))

                def PF(n):
                    for _ in range(n):
                        if not pf_chunks:
                            return
                        kt, c0, c1 = pf_chunks.pop(0)
                        i = len(pf_chunks) % 2
                        A("sp", lambda e, i=i, kt=kt, c0=c0, c1=c1: e.dma_start(out=stgp[i][:, 0:c1 - c0], in_=wvn[:, kt, c0:c1]), writes=[("stgp", i)], dma=True)
                        A("pool", lambda e, i=i, kt=kt, c0=c0, c1=c1: e.tensor_copy(out=wnext[:, kt, c0:c1], in_=stgp[i][:, 0:c1 - c0]),
                          reads=[("stgp", i)], writes=["wnext"])

                LD(0)
                for b in range(NBLK):
                    if b + 1 < NBLK:
                        LD(b + 1)
                    PF(2)
                    tok0 = b * NB
                    hb = h2T[b % 2]
                    hk = ("h2T", b % 2)
                    for ft in range(11):
                        gi = ft % 2
                        for kt in range(8):
                            A("pe", lambda e, gi=gi, kt=kt, ft=ft, hb=hb: e.matmul(pgt[gi][:], lhsT=wg[:, kt, ft * 128:(ft + 1) * 128], rhs=hb[:, kt, :],
                                                                              start=(kt == 0), stop=(kt == 7)),
                              reads=wkeys("wg", 8, ft * 128) + [hk], writes=[("pgt", gi)])
                        for kt in range(8):
                            A("pe", lambda e, gi=gi, kt=kt, ft=ft, hb=hb: e.matmul(put[gi][:], lhsT=wu[:, kt, ft * 128:(ft + 1) * 128], rhs=hb[:, kt, :],
                                                                              start=(kt == 0), stop=(kt == 7)),
                              reads=wkeys("wu", 8, ft * 128) + [hk], writes=[("put", gi)])
                        A("act", lambda e, gi=gi: e.activation(out=sl[gi][:], in_=pgt[gi][:], func=AF.Silu), reads=[("pgt", gi)], writes=[("sl", gi)])
                        A("dve", lambda e, gi=gi, ft=ft: e.tensor_tensor(out=ffT[:, ft, :], in0=put[gi][:], in1=sl[gi][:], op=ALU.mult),
                          reads=[("put", gi), ("sl", gi)], writes=[("ffT", ft)])
                    for tt in range(4):
                        xb = xt[(b % 2) * 4 + tt]
                        xk = ("xt", b % 2, tt)
                        r0 = tok0 + tt * 128
                        for nh in range(2):
                            pi_ = (2 * tt + nh) % 4
                            for ft in range(11):
                                A("pe", lambda e, pi_=pi_, ft=ft, tt=tt, nh=nh: e.matmul(pd[pi_][:], lhsT=ffT[:, ft, tt * 128:(tt + 1) * 128],
                                                                                       rhs=wd[:, ft, nh * NB:(nh + 1) * NB], start=(ft == 0), stop=(ft == 10)),
                                  reads=[("ffT", ft)] + wkeys("wd", 11, 0), writes=[("pd", pi_)])
                            A("dve", lambda e, pi_=pi_, xb=xb, nh=nh: e.tensor_tensor(out=xb[:, nh * NB:(nh + 1) * NB], in0=pd[pi_][:], in1=xb[:, nh * NB:(nh + 1) * NB], op=ALU.add),
                              reads=[("pd", pi_), xk], writes=[xk])
                        if last:
                            A("act", lambda e, xb=xb, tt=tt: e.activation(out=sqj[:], in_=xb[:], func=AF.Square, accum_out=ss[:, tt:tt + 1]),
                              reads=[xk], writes=["sqj", ("ss", tt)])
                            A("act", lambda e, tt=tt: e.activation(out=rstd[:, tt:tt + 1], in_=ss[:, tt:tt + 1], func=AF.Ln, scale=1.0 / D, bias=EPS),
                              reads=[("ss", tt)], writes=[("rstd", tt)])
                            A("act", lambda e, tt=tt: e.activation(out=rstd[:, tt:tt + 1], in_=rstd[:, tt:tt + 1], func=AF.Exp, scale=-0.5), reads=[("rstd", tt)], writes=[("rstd", tt)])
                            A("dve", lambda e, xb=xb, tt=tt: e.scalar_tensor_tensor(out=xb[:], in0=xb[:], scalar=rstd[:, tt:tt + 1], in1=gbf[:],
                                                                                      op0=ALU.mult, op1=ALU.mult),
                              reads=[xk, ("rstd", tt), "gbf"], writes=[xk])
                        A("pool", lambda e, xb=xb, r0=r0: e.dma_start(out=xdst[r0:r0 + 128, :], in_=xb[:]), reads=[xk], writes=["xdst"], dma=True)
                S.emit()

        pre_es = None
        pre_w = None
        for l in layers:
            if "m1" in phases:
                phase_m1(l, pre_w=pre_w)
            if pre_es is not None:
                pre_es.close()
                pre_es, pre_w = None, None
            if "att" in phases:
                phase_att(l)
            if "s5" in phases:
                phase_s5(l)
            if "m3a" in phases:
                phase_mf1(l) if USE_MF else phase_m3a(l)
            if "m3b" in phases:
                if USE_MF:
                    pf = None
                    if l == 0 and 1 in layers and "m1" in phases:
                        pre_es = ExitStack()
                        uniq[0] += 1
                        pre_w = pre_es.enter_context(nc.sbuf_tensor("wpre_%d" % uniq[0], [128, 8, NIN], BF16))
                        pf = (pre_w, w_in[1])
                    phase_mf2(l, last=(l == 1), prefetch=pf)
                else:
                    phase_m3b(l, last=(l == 1))
    return nc


_PERM = np.concatenate([np.arange(32, 64), np.arange(0, 32)])


def _prep_inputs(inputs):
    w_in = np.asarray(inputs["w_in"], dtype=np.float32)
    q = w_in[:, :, 1536:1792].reshape(2, D, 4, 64)[:, :, :, _PERM].reshape(2, D, 256)
    k = w_in[:, :, 1792:2048].reshape(2, D, 4, 64)[:, :, :, _PERM].reshape(2, D, 256)
    w_in_p = np.ascontiguousarray(np.concatenate([w_in, q, k], axis=2))
    shared = {n: np.ascontiguousarray(np.asarray(v, dtype=np.float32)) for n, v in inputs.items() if n not in ("x", "w_in")}
    shared["w_in"] = w_in_p
    return shared


_NC_CACHE = {}


def kernel(**inputs):
    x = np.ascontiguousarray(np.asarray(inputs["x"], dtype=np.float32))
    shared = _prep_inputs(inputs)
    if "full" not in _NC_CACHE:
        _NC_CACHE["full"] = build_program()
    nc = _NC_CACHE["full"]
    xs = x.reshape(N_CORES, 2 * SEQ, D)
    in_maps = [dict(shared, x=np.ascontiguousarray(xs[i])) for i in range(N_CORES)]
    res = run_bass_kernel_spmd(nc, in_maps, core_ids=list(range(N_CORES)))
    outs = [np.asarray(res.results[i]["out"], dtype=np.float32).reshape(2, SEQ, D) for i in range(N_CORES)]
    return np.concatenate(outs, axis=0)
```
